# Optimizing a Trainium2 kernel written in Bass

```python
import math
import jax, jax.numpy as jnp
from jax import lax
import numpy as np

D_MODEL = 1024
BATCH = 4
SEQ = 8192
DEPTH = 4

CTX_LEN = 256
GRID_W = 64
N_MIXERS = 2
N_LAYERS_A = (DEPTH + 1) // 2
N_LAYERS_B = DEPTH // 2
BRANCH_WIDTH = D_MODEL
DA_HEAD_DIM = 64
DA_HEADS = BRANCH_WIDTH // (2 * DA_HEAD_DIM)
DA_QK_WIDTH = DA_HEADS * 2 * DA_HEAD_DIM
DA_IN_WIDTH = 2 * DA_QK_WIDTH + 2 * BRANCH_WIDTH
GQ_HEAD_DIM = 128
GQ_HEADS = BRANCH_WIDTH // GQ_HEAD_DIM
GQ_KV_HEADS = 2
GQ_Q_WIDTH = GQ_HEADS * GQ_HEAD_DIM
GQ_KV_WIDTH = GQ_KV_HEADS * GQ_HEAD_DIM
GQ_IN_WIDTH = GQ_Q_WIDTH + 2 * GQ_KV_WIDTH + BRANCH_WIDTH
ROPE_THETA = 10000.0
Q_BLOCK = 128
NORM_EPS = 1e-6

kernel_name = "hybrid_diffattn_gqa_prefix_backbone"


def rms_norm(x, g):
    xf = x.astype(jnp.float32)
    y = xf * lax.rsqrt(jnp.mean(xf * xf, axis=-1, keepdims=True) + NORM_EPS)
    return (y * g.astype(jnp.float32)).astype(x.dtype)


def adaln(cvec, w, b):
    m = jax.nn.silu(cvec) @ w + b
    return jnp.split(m, 3, axis=-1)


def axial_rope_tables(rows, cols, head_dim):
    axis_dim = head_dim // 2
    freqs = ROPE_THETA ** (-jnp.arange(0, axis_dim, 2, dtype=jnp.float32) / axis_dim)
    ang = jnp.concatenate([rows.astype(jnp.float32)[:, None] * freqs,
                           cols.astype(jnp.float32)[:, None] * freqs], axis=-1)
    return jnp.cos(ang), jnp.sin(ang)


def apply_rope(x, cos, sin):
    S, half = cos.shape
    shp = (1, S) + (1,) * (x.ndim - 3) + (half,)
    cs = cos.reshape(shp).astype(x.dtype)
    sn = sin.reshape(shp).astype(x.dtype)
    x1, x2 = jnp.split(x, 2, axis=-1)
    return jnp.concatenate([x1 * cs - x2 * sn, x2 * cs + x1 * sn], axis=-1)


def sweep_attention(q, k, v):
    B, Sq, Hq, Dh = q.shape
    Hkv = k.shape[2]
    G = Hq // Hkv
    Dv = v.shape[-1]
    scale = Dh ** -0.5
    k32 = k.astype(jnp.float32)
    qb = q.reshape(B, Sq // Q_BLOCK, Q_BLOCK, Hkv, G, Dh).transpose(1, 0, 2, 3, 4, 5)

    def one_block(qblk):
        s = jnp.einsum('bqhgd,bkhd->bhgqk', qblk.astype(jnp.float32), k32) * scale
        p = jax.nn.softmax(s, axis=-1)
        return jnp.einsum('bhgqk,bkhd->bqhgd', p.astype(v.dtype), v)

    out = lax.map(one_block, qb)
    return out.transpose(1, 0, 2, 3, 4, 5).reshape(B, Sq, Hq, Dv)


def lambda_init_fn(layer_idx):
    return 0.8 - 0.6 * math.exp(-0.3 * layer_idx)


def diff_attention_mixer(h, hc, w_in, w_out, lam_p, subln_g, lam_init, cos, sin, with_ctx_out):
    B, S, _ = h.shape

    def project(t):
        L = t.shape[1]
        p = t @ w_in
        q, k, v, z = jnp.split(p, [DA_QK_WIDTH, 2 * DA_QK_WIDTH, 2 * DA_QK_WIDTH + BRANCH_WIDTH], axis=-1)
        q = q.reshape(B, L, DA_HEADS, 2, DA_HEAD_DIM)
        k = k.reshape(B, L, DA_HEADS, 2, DA_HEAD_DIM)
        v = v.reshape(B, L, DA_HEADS, 2 * DA_HEAD_DIM)
        return q, k, v, z

    q, k, v, z = project(h)
    qc, kc, vc, zc = project(hc)
    q = apply_rope(q, cos, sin)
    k = apply_rope(k, cos, sin)
    k_all = jnp.concatenate([k, kc], axis=1)
    v_all = jnp.concatenate([v, vc], axis=1)

    lp = lam_p.astype(jnp.float32)
    lam = jnp.exp(jnp.sum(lp[0] * lp[1])) - jnp.exp(jnp.sum(lp[2] * lp[3])) + lam_init

    def diff(qq, kk, vv, gate):
        o1 = sweep_attention(qq[..., 0, :], kk[..., 0, :], vv)
        o2 = sweep_attention(qq[..., 1, :], kk[..., 1, :], vv)
        o = o1 - lam.astype(o1.dtype) * o2
        o = rms_norm(o, subln_g) * (1.0 - lam_init)
        o = o.reshape(o.shape[0], o.shape[1], BRANCH_WIDTH)
        return (o * jax.nn.silu(gate)) @ w_out

    y = diff(q, k_all, v_all, z)
    yc = diff(qc, kc, vc, zc) if with_ctx_out else None
    return y, yc


def gqa_mixer(h, hc, w_in, w_out, qk_g, cos, sin, with_ctx_out):
    B, S, _ = h.shape

    def project(t):
        L = t.shape[1]
        p = t @ w_in
        q, k, v, z = jnp.split(p, [GQ_Q_WIDTH, GQ_Q_WIDTH + GQ_KV_WIDTH, GQ_Q_WIDTH + 2 * GQ_KV_WIDTH], axis=-1)
        q = rms_norm(q.reshape(B, L, GQ_HEADS, GQ_HEAD_DIM), qk_g[0])
        k = rms_norm(k.reshape(B, L, GQ_KV_HEADS, GQ_HEAD_DIM), qk_g[1])
        v = v.reshape(B, L, GQ_KV_HEADS, GQ_HEAD_DIM)
        return q, k, v, z

    q, k, v, z = project(h)
    qc, kc, vc, zc = project(hc)
    q = apply_rope(q, cos, sin)
    k = apply_rope(k, cos, sin)
    k_all = jnp.concatenate([k, kc], axis=1)
    v_all = jnp.concatenate([v, vc], axis=1)

    def attend(qq, kk, vv, gate):
        o = sweep_attention(qq, kk, vv)
        o = o.reshape(o.shape[0], o.shape[1], BRANCH_WIDTH)
        return (o * jax.nn.silu(gate)) @ w_out

    y = attend(q, k_all, v_all, z)
    yc = attend(qc, kc, vc, zc) if with_ctx_out else None
    return y, yc


def setup_inputs(seed: int = 0) -> dict:
    key = jax.random.key(seed)
    ks = jax.random.split(key, 16)
    f32 = jnp.float32
    D = D_MODEL
    s_in = D ** -0.5
    return {
        "x": jax.random.normal(ks[0], (BATCH, SEQ, D), f32),
        "c": jax.random.normal(ks[1], (BATCH, D), f32),
        "ctx": jax.random.normal(ks[2], (BATCH, CTX_LEN, D), f32),
        "c_ctx": jax.random.normal(ks[3], (D,), f32),
        "ada_w": jax.random.normal(ks[4], (DEPTH, D, 3 * D), f32) * (0.5 * s_in),
        "ada_b": jax.random.normal(ks[5], (DEPTH, 3 * D), f32) * 0.01,
        "pre_g": 1.0 + 0.05 * jax.random.normal(ks[6], (DEPTH, D), f32),
        "post_g": 1.0 + 0.05 * jax.random.normal(ks[7], (DEPTH, D), f32),
        "w_out": jax.random.normal(ks[8], (DEPTH, BRANCH_WIDTH, D), f32) * (BRANCH_WIDTH ** -0.5),
        "a_w_in": jax.random.normal(ks[9], (N_LAYERS_A, D, DA_IN_WIDTH), f32) * s_in,
        "a_lambda": jax.random.normal(ks[10], (N_LAYERS_A, 4, DA_HEAD_DIM), f32) * 0.1,
        "a_subln_g": 1.0 + 0.05 * jax.random.normal(ks[11], (N_LAYERS_A, 2 * DA_HEAD_DIM), f32),
        "b_w_in": jax.random.normal(ks[12], (N_LAYERS_B, D, GQ_IN_WIDTH), f32) * s_in,
        "b_qk_g": 1.0 + 0.05 * jax.random.normal(ks[13], (N_LAYERS_B, 2, GQ_HEAD_DIM), f32),
    }


def reference(x, c, ctx, c_ctx, ada_w, ada_b, pre_g, post_g, w_out, a_w_in, a_lambda, a_subln_g, b_w_in, b_qk_g):
    n_tokens = x.shape[1]
    n_rows = n_tokens // GRID_W
    rows = jnp.repeat(jnp.arange(n_rows, dtype=jnp.int32), GRID_W)
    cols = jnp.tile(jnp.arange(GRID_W, dtype=jnp.int32), n_rows)
    cos_a, sin_a = axial_rope_tables(rows, cols, DA_HEAD_DIM)
    cos_b, sin_b = axial_rope_tables(rows, cols, GQ_HEAD_DIM)

    xc = ctx
    for i in range(DEPTH):
        last = i == DEPTH - 1
        shift, scale, gate = adaln(c, ada_w[i], ada_b[i])
        shift, scale, gate = shift[:, None, :], scale[:, None, :], gate[:, None, :]
        cshift, cscale, cgate = adaln(c_ctx, ada_w[i], ada_b[i])
        h = rms_norm(x, pre_g[i]) * (1.0 + scale) + shift
        hc = rms_norm(xc, pre_g[i]) * (1.0 + cscale) + cshift
        j = i // N_MIXERS
        if i % N_MIXERS == 0:
            y, yc = diff_attention_mixer(h, hc, a_w_in[j], w_out[i], a_lambda[j], a_subln_g[j],
                                         lambda_init_fn(i), cos_a, sin_a, not last)
        else:
            y, yc = gqa_mixer(h, hc, b_w_in[j], w_out[i], b_qk_g[j], cos_b, sin_b, not last)
        x = x + gate * rms_norm(y, post_g[i])
        if not last:
            xc = xc + cgate * rms_norm(yc, post_g[i])
    return x
```

```python
import math
import numpy as np
import ml_dtypes
import concourse.bass as bass
import concourse.mybir as mybir
from concourse.bass_utils import run_bass_kernel_spmd

F32 = mybir.dt.float32
BF16 = mybir.dt.bfloat16
AF = mybir.ActivationFunctionType
ALU = mybir.AluOpType
AX = mybir.AxisListType

D = 1024
NCORES = 8
CTXH = 128
EPS = 1e-6
ROPE_THETA = 10000.0
GRID_W = 64


def lambda_init_fn(i):
    return 0.8 - 0.6 * math.exp(-0.3 * i)


class Op:
    __slots__ = ("eng", "fn", "deps", "kind", "signal", "val", "sem", "prewait")

    def __init__(self, eng, fn, deps, kind):
        self.eng = eng
        self.fn = fn
        self.deps = deps
        self.kind = kind
        self.signal = kind != "c"
        self.val = None
        self.sem = None
        self.prewait = None


class Rec:
    ENGS = ("pe", "act", "dve", "pool", "sp")
    NPOOL = 8

    def __init__(self, nc, sems, dma_sems, cc_sem):
        self.nc = nc
        self.sems = sems
        self.dma_sems = dma_sems
        self.cc_sem = cc_sem
        self.cnt = {e: 0 for e in self.ENGS}
        self.dma_n = {e: 0 for e in self.ENGS}
        self.cc_n = 0
        self.ops = []
        self.last_w = {}
        self.readers = {}
        self.nops = 0

    def op(self, eng, fn, reads=(), writes=(), kind="c"):
        deps = set()
        for k in reads:
            w = self.last_w.get(k)
            if w is not None:
                deps.add(w)
        for k in writes:
            w = self.last_w.get(k)
            if w is not None:
                deps.add(w)
            for r in self.readers.get(k, ()):
                deps.add(r)
        o = Op(eng, fn, deps, kind)
        for k in reads:
            self.readers.setdefault(k, []).append(o)
        for k in writes:
            self.last_w[k] = o
            self.readers[k] = []
        self.ops.append(o)
        return o

    def pe(self, fn, reads=(), writes=()):
        return self.op("pe", fn, reads, writes)

    def act(self, fn, reads=(), writes=()):
        return self.op("act", fn, reads, writes)

    def dve(self, fn, reads=(), writes=()):
        return self.op("dve", fn, reads, writes)

    def pool(self, fn, reads=(), writes=()):
        return self.op("pool", fn, reads, writes)

    def dma(self, eng, fn, reads=(), writes=()):
        return self.op(eng, fn, reads, writes, kind="d")

    def flush(self):
        nc = self.nc
        ops = self.ops
        live = set(id(o) for o in ops)
        for o in ops:
            nd = set()
            for d in o.deps:
                if id(d) not in live:
                    continue
                if d.eng == "pe" and o.eng == "pe" and d.kind == "c":
                    continue
                nd.add(d)
                d.signal = True
            o.deps = nd
        per = {e: [] for e in self.ENGS}
        for o in ops:
            per[o.eng].append(o)
            if o.kind == "d":
                n = self.dma_n[o.eng]
                self.dma_n[o.eng] = n + 1
                o.sem = self.dma_sems[o.eng][n % self.NPOOL]
                o.val = 16 * (n // self.NPOOL + 1)
                if n >= self.NPOOL:
                    o.prewait = (o.sem, 16 * (n // self.NPOOL))
            elif o.kind == "cc":
                self.cc_n += 1
                o.sem = self.cc_sem
                o.val = self.cc_n
            elif o.signal:
                self.cnt[o.eng] += 1
                o.sem = self.sems[o.eng]
                o.val = self.cnt[o.eng]
        dma_final = {}
        for e in self.ENGS:
            n = self.dma_n[e]
            fin = []
            for i in range(min(n, self.NPOOL)):
                last = ((n - 1 - i) // self.NPOOL) * self.NPOOL + i
                fin.append((self.dma_sems[e][i], 16 * (last // self.NPOOL + 1)))
            dma_final[e] = fin
        self.nops += len(ops)

        def emit(e_ops, eng_name):
            def body(e):
                waited = {}
                for o in e_ops:
                    ws = []
                    if o.prewait is not None:
                        ws.append(o.prewait)
                    for d in o.deps:
                        ws.append((d.sem, d.val))
                    for (s, v) in ws:
                        key = id(s)
                        if waited.get(key, 0) >= v:
                            continue
                        waited[key] = v
                        e.wait_ge(s, v)
                    ins = o.fn(e)
                    if o.kind == "d":
                        ins.then_inc(o.sem, 16)
                    elif o.kind == "cc":
                        ins.then_inc(o.sem)
                    elif o.signal:
                        ins.then_inc(o.sem, 1)
                for (s, v) in dma_final[eng_name]:
                    if waited.get(id(s), 0) < v:
                        e.wait_ge(s, v)
            return body

        with nc.Block() as block:
            reg = {"pe": block.tensor, "act": block.scalar, "dve": block.vector,
                   "pool": block.gpsimd, "sp": block.sync}
            for e in self.ENGS:
                if per[e] or dma_final[e]:
                    reg[e](emit(per[e], e))
        self.ops = []
        self.last_w = {}
        self.readers = {}


class Cfg:
    def __init__(self, seq, depth, debug=False, stab=True, stop=None):
        self.stop = stop
        self.SEQ = seq
        self.DEPTH = depth
        self.LAT = seq // 2
        self.NTOK = self.LAT + CTXH
        self.NW = self.LAT // 512
        self.KT_R = self.NTOK // 128
        self.NKT = 2 * self.KT_R
        self.debug = debug
        self.stab = stab


def layer_dims(l):
    if l % 2 == 0:
        return True, 1024, 1024, 1024, 4096, 0, 1024, 2048, 3072
    return False, 1024, 256, 256, 2560, 0, 1024, 1280, 1536


def build_program(cfg):
    nc = bass.Bass("TRN2", target_bir_lowering=False)
    NTOK, LAT, NW, KT_R, NKT, DEPTH = cfg.NTOK, cfg.LAT, cfg.NW, cfg.KT_R, cfg.NKT, cfg.DEPTH
    dbg_kind = "ExternalOutput" if cfg.debug else "Internal"

    def din(name, shape, dt=F32):
        return nc.dram_tensor(name, list(shape), dt, kind="ExternalInput")

    xin = din("xin", [NTOK, D])
    cin = din("cin", [128, 2, 8])
    ada_w = din("ada_w", [DEPTH, D, 3 * D])
    ada_b = din("ada_b", [DEPTH, 128, 3 * D])
    pre_g = din("pre_g", [DEPTH, 128, D])
    post_g = din("post_g", [DEPTH, 128, D])
    w_out = din("w_out", [DEPTH, D, D])
    NA = (DEPTH + 1) // 2
    NB = max(DEPTH // 2, 1)
    a_w_in = din("a_w_in", [NA, D, 4096])
    a_lambda = din("a_lambda", [NA, 128, 256])
    a_subln = din("a_subln", [NA, 128, 1])
    b_w_in = din("b_w_in", [NB, D, 2560])
    b_qk_g = din("b_qk_g", [NB, 2, 128, 1])
    ropeC = [din("ropeA_C", [128, NTOK]), din("ropeB_C", [128, NTOK])]
    ropeS = [din("ropeA_S", [128, NTOK]), din("ropeB_S", [128, NTOK])]
    ident_in = din("ident", [128, 128], BF16)
    ind_in = din("indmat", [2, 128, 128], BF16)
    out = nc.dram_tensor("out", [LAT, D], F32, kind="ExternalOutput")

    xs = nc.dram_tensor("xs", [NTOK, D], F32, kind=dbg_kind)
    qT, zT, ogT, kTb, kTg, vb, vg = [], [], [], [], [], [], []
    for l in range(DEPTH):
        is_a, FQ, FK, FV, *_ = layer_dims(l)
        qT.append(nc.dram_tensor(f"qT{l}", [FQ, NTOK], BF16, kind=dbg_kind))
        zT.append(nc.dram_tensor(f"zT{l}", [D, NTOK], BF16, kind=dbg_kind))
        ogT.append(nc.dram_tensor(f"ogT{l}", [D, NTOK], BF16, kind=dbg_kind))
        kTb.append(nc.dram_tensor(f"kTb{l}", [FK, NTOK], BF16))
        kTg.append(nc.dram_tensor(f"kTg{l}", [2 * FK, NTOK], BF16))
        vb.append(nc.dram_tensor(f"vb{l}", [(FV // 128) * NTOK, 128], BF16))
        vg.append(nc.dram_tensor(f"vg{l}", [(FV // 128) * 2 * NTOK, 128], BF16))

    import contextlib
    es = contextlib.ExitStack()
    with es:
        def sem(name):
            return es.enter_context(nc.semaphore(name))

        sems = {e: sem(f"s_{e}") for e in Rec.ENGS}
        dma_sems = {e: [sem(f"d_{e}{i}") for i in range(Rec.NPOOL)] for e in ("sp", "pool", "act")}
        dma_sems["pe"] = dma_sems["dve"] = []
        cc_sem = sem("cc")
        R = Rec(nc, sems, dma_sems, cc_sem)

        l_tag = ["g"]

        def sb(st, name, shape, dt):
            return st.enter_context(nc.sbuf_tensor(f"sb{l_tag[0]}_{name}", list(shape), dt))

        def ps(st, name, shape, dt=F32):
            return st.enter_context(nc.psum_tensor(f"ps{l_tag[0]}_{name}", list(shape), dt))

        ident = sb(es, "ident", [128, 128], BF16)
        identf = sb(es, "identf", [128, 128], F32)
        ones_bf = sb(es, "ones_bf", [128, 128], BF16)
        onesf = sb(es, "onesf", [128, 128], F32)
        indm = sb(es, "indm", [128, 2, 128], BF16)
        epsb = sb(es, "epsb", [128, 1], F32)
        cint = sb(es, "cint", [128, 2, 8], F32)
        scs = sb(es, "scs", [128, 2, 8], F32)
        scb = sb(es, "scb", [128, 2, 8, 128], F32)
        Gb = [sb(es, f"Gb{r}", [128, D], F32) for r in range(2)]
        Acol = [sb(es, f"Acol{r}", [128, 8], F32) for r in range(2)]
        Scol = [sb(es, f"Scol{r}", [128, 8], F32) for r in range(2)]

        R.dma("sp", lambda e: e.dma_start(out=ident[:], in_=ident_in[:, :]), writes=["ident"])
        R.dma("sp", lambda e: e.dma_start(out=indm[:], in_=ind_in.ap().rearrange("s p m -> p s m")),
              writes=["indm"])
        R.dma("sp", lambda e: e.dma_start(out=cint[:], in_=cin[:, :, :]), writes=["cint"])
        R.dve(lambda e: e.tensor_copy(out=identf[:], in_=ident[:]), reads=["ident"], writes=["identf"])
        R.dve(lambda e: e.memset(ones_bf[:], 1.0), writes=["ones"])
        R.dve(lambda e: e.memset(onesf[:], 1.0), writes=["onesf"])
        R.dve(lambda e: e.memset(epsb[:], EPS), writes=["eps"])
        import os
        ZS = int(os.environ.get('KDEBUG_ZSTEP', '9'))
        if ZS >= 2:
            R.act(lambda e: e.activation(out=scs[:], in_=cint[:], func=AF.Silu), reads=["cint"], writes=["scs"])
        for r in range(2 if ZS >= 3 else 0):
            for k in range(8):
                R.dve(lambda e, r=r, k=k: e.tensor_scalar(
                    out=scb[:, r, k, :], in0=onesf[:], scalar1=scs[:, r, k:k + 1], scalar2=None,
                    op0=ALU.mult), reads=["onesf", "scs"], writes=[("scb", r, k)])
        R.flush()

        for l in range(DEPTH):
            if cfg.stop == 'Z':
                break
            is_a, FQ, FK, FV, NCOL, colq, colk, colv, colz = layer_dims(l)
            l_tag[0] = str(l)
            j = l // 2
            last = l == DEPTH - 1
            w_in = a_w_in if is_a else b_w_in
            rC, rS = (ropeC[0], ropeS[0]) if is_a else (ropeC[1], ropeS[1])
            src = xin if l == 0 else xs
            NQC = FQ // 128
            NKC = FK // 128
            NVH = FV // 128

            PH = ['M', 'P', 'X', 'A', 'O']
            run_ph = lambda t: cfg.stop is None or PH.index(t) <= PH.index(cfg.stop)
            with contextlib.ExitStack() as st:
                awt = [sb(st, f"awt{i}", [128, 8, 512], F32) for i in range(2)]
                modt = [sb(st, f"modt{r}", [128, 3 * D], F32) for r in range(2)]
                adab = sb(st, "adab", [128, 3 * D], F32)
                pgb = sb(st, "pgb", [128, D], F32)
                qgb = sb(st, "qgb", [128, D], F32)
                tmpA = sb(st, "tmpA", [128, D], F32)
                junk = sb(st, "junkM", [128, 128], F32)
                pm = [ps(st, f"pmM{i}", [128, 512]) for i in range(4)]
                R.dma("sp", lambda e: e.dma_start(out=adab[:], in_=ada_b[l, :, :]), writes=["adab"])
                R.dma("sp", lambda e: e.dma_start(out=pgb[:], in_=pre_g[l, :, :]), writes=["pgb"])
                R.dma("sp", lambda e: e.dma_start(out=qgb[:], in_=post_g[l, :, :]), writes=["qgb"])
                awv = ada_w[l].rearrange("(k p) n -> p k n", p=128)
                for c in range(6):
                    R.dma("sp", lambda e, c=c: e.dma_start(out=awt[c % 2][:], in_=awv[:, :, c * 512:(c + 1) * 512]),
                          writes=[("awt", c % 2)])
                    for r in range(2):
                        bank = pm[(2 * c + r) % 4]

                        def mm(e, c=c, r=r, bank=bank):
                            ins = None
                            for k in range(8):
                                ins = e.matmul(bank[:], scb[:, r, k, :], awt[c % 2][:, k, :],
                                               start=(k == 0), stop=(k == 7))
                            return ins
                        R.pe(mm, reads=[("awt", c % 2)], writes=[("pmM", (2 * c + r) % 4)])
                        R.dve(lambda e, c=c, r=r, bank=bank: e.tensor_tensor(
                            out=modt[r][:, c * 512:(c + 1) * 512], in0=bank[:],
                            in1=adab[:, c * 512:(c + 1) * 512], op=ALU.add),
                            reads=[("pmM", (2 * c + r) % 4), "adab"], writes=[("modt", r, c)])
                import os
                MS = int(os.environ.get('KDEBUG_MSTEP', '9'))
                for r in range(2 if MS >= 2 else 0):
                    allmod = [("modt", r, c) for c in range(6)]
                    R.dve(lambda e, r=r: e.scalar_tensor_tensor(
                        out=tmpA[:], in0=modt[r][:, D:2 * D], scalar=1.0, in1=pgb[:],
                        op0=ALU.add, op1=ALU.mult), reads=allmod + ["pgb"], writes=["tmpA"])
                    for k in range(8):
                        R.dve(lambda e, r=r, k=k: e.scalar_tensor_tensor(
                            out=junk[:], in0=tmpA[:, k * 128:(k + 1) * 128], scalar=1.0, in1=identf[:],
                            op0=ALU.mult, op1=ALU.mult, accum_out=Acol[r][:, k:k + 1]),
                            reads=["tmpA"], writes=["junkM", ("Acol", r)])
                    for k in range(8):
                        R.dve(lambda e, r=r, k=k: e.scalar_tensor_tensor(
                            out=junk[:], in0=modt[r][:, k * 128:(k + 1) * 128], scalar=1.0, in1=identf[:],
                            op0=ALU.mult, op1=ALU.mult, accum_out=Scol[r][:, k:k + 1]),
                            reads=allmod, writes=["junkM", ("Scol", r)])
                    R.dve(lambda e, r=r: e.tensor_tensor(out=Gb[r][:], in0=modt[r][:, 2 * D:3 * D], in1=qgb[:],
                                                         op=ALU.mult), reads=allmod + ["qgb"], writes=[("Gb", r)])
                R.flush()

            if not run_ph('P'):
                continue
            with contextlib.ExitStack() as st:
                wbf = sb(st, "wbf", [128, 8, NCOL], BF16)
                NXT = 8
                xt = [sb(st, f"xt{i}", [128, D], F32) for i in range(NXT)]
                xn = [sb(st, f"xn{i}", [128, D], BF16) for i in range(4)]
                sqj = sb(st, "sqj", [128, D], BF16)
                ssq = [sb(st, f"ssq{i}", [128, 4], F32) for i in range(2)]
                lnv = [sb(st, f"lnv{i}", [128, 4], F32) for i in range(2)]
                rst = [sb(st, f"rst{i}", [128, 4], F32) for i in range(2)]
                hT = [sb(st, f"hT{i}", [128, 8, 512], BF16) for i in range(2)]
                rCt = [sb(st, f"rCt{i}", [128, 512], F32) for i in range(2)]
                rSt = [sb(st, f"rSt{i}", [128, 512], F32) for i in range(2)]
                swt = [sb(st, f"swt{i}", [128, 512], F32) for i in range(2)]
                t1 = [sb(st, f"t1_{i}", [128, 512], F32) for i in range(2)]
                t2 = [sb(st, f"t2_{i}", [128, 512], F32) for i in range(2)]
                NOC = 4
                oc = [sb(st, f"oc{i}", [128, 512], BF16) for i in range(NOC)]
                vo = [sb(st, f"vo{i}", [128, FV], BF16) for i in range(2)]
                if not is_a:
                    sqb = [sb(st, f"sqb{i}", [128, 512], BF16) for i in range(2)]
                    lnt = [sb(st, f"lnt{i}", [128, 512], F32) for i in range(2)]
                    rsb = [sb(st, f"rsb{i}", [128, 512], F32) for i in range(2)]
                    qn = [sb(st, f"qn{i}", [128, 512], F32) for i in range(2)]
                    gqk = sb(st, "gqk", [128, 2], F32)
                tp = [ps(st, f"tp{i}", [128, 1024], BF16) for i in range(2)]
                pm = [ps(st, f"pmP{i}", [128, 512]) for i in range(4)]
                if not is_a:
                    pss = [ps(st, f"pss{i}", [128, 512]) for i in range(2)]

                wv = w_in[j].rearrange("(k p) n -> p k n", p=128)
                CW = 1024
                for k in range(8):
                    for c0 in range(0, NCOL, CW):
                        c1 = min(NCOL, c0 + CW)
                        R.dma("pool", lambda e, k=k, c0=c0, c1=c1: e.dma_start(out=wbf[:, k, c0:c1], in_=wv[:, k, c0:c1]),
                              writes=[("wbf", k, c0)])
                wkeys = [("wbf", k, c0) for k in range(8) for c0 in range(0, NCOL, CW)]
                if not is_a:
                    for i in range(2):
                        R.dma("sp", lambda e, i=i: e.dma_start(out=gqk[:, i:i + 1], in_=b_qk_g[j, i, :, :]),
                              writes=[("gqk", i)])

                cnt = {"xt": 0, "oc": 0, "pm": 0, "vo": 0, "tp": 0, "rp": 0, "b": 0}
                PS = int(os.environ.get('KDEBUG_PSTEP', '9'))
                for w in range(NW + 1 if PS >= 2 else 0):
                    T = 512 if w < NW else 128
                    nsub = T // 128
                    tok0 = w * 512 if w < NW else LAT
                    r = 0 if w < NW else 1
                    wi = w % 2
                    xts = []
                    for jj in range(nsub):
                        xi = cnt["xt"] % NXT
                        cnt["xt"] += 1
                        xts.append(xi)
                        R.dma("sp", lambda e, xi=xi, jj=jj, tok0=tok0: e.dma_start(
                            out=xt[xi][:], in_=src[tok0 + jj * 128: tok0 + (jj + 1) * 128, :]),
                            reads=[("xs", tok0 + jj * 128)], writes=[("xt", xi)])
                        R.act(lambda e, xi=xi, jj=jj, wi=wi: e.activation(
                            out=sqj[:], in_=xt[xi][:], func=AF.Square, accum_out=ssq[wi][:, jj:jj + 1]),
                            reads=[("xt", xi)], writes=["sqj", ("ssq", wi, jj)])
                    R.dma("sp", lambda e, wi=wi, tok0=tok0, T=T: e.dma_start(out=rCt[wi][:, 0:T], in_=rC[:, tok0:tok0 + T]),
                          writes=[("rCt", wi)])
                    R.dma("sp", lambda e, wi=wi, tok0=tok0, T=T: e.dma_start(out=rSt[wi][:, 0:T], in_=rS[:, tok0:tok0 + T]),
                          writes=[("rSt", wi)])
                    sskeys = [("ssq", wi, jj) for jj in range(nsub)]
                    R.act(lambda e, wi=wi, nsub=nsub: e.activation(
                        out=lnv[wi][:, 0:nsub], in_=ssq[wi][:, 0:nsub], func=AF.Ln, bias=epsb[:], scale=1.0 / D),
                        reads=sskeys + ["eps"], writes=[("lnv", wi)])
                    R.act(lambda e, wi=wi, nsub=nsub: e.activation(
                        out=rst[wi][:, 0:nsub], in_=lnv[wi][:, 0:nsub], func=AF.Exp, scale=-0.5),
                        reads=[("lnv", wi)], writes=[("rst", wi)])
                    for jj in range(nsub):
                        R.act(lambda e, xi=xts[jj], jj=jj, wi=wi: e.activation(
                            out=xn[jj][:], in_=xt[xi][:], func=AF.Copy, scale=rst[wi][:, jj:jj + 1]),
                            reads=[("xt", xts[jj]), ("rst", wi)], writes=[("xn", jj)])
                    if PS < 3:
                        continue
                    for kk in range(4):
                        ti = cnt["tp"] % 2
                        cnt["tp"] += 1

                        def trs(e, kk=kk, ti=ti, nsub=nsub):
                            ins = None
                            for half in range(2):
                                k = 2 * kk + half
                                for jj in range(nsub):
                                    ins = e.transpose(tp[ti][:, half * 512 + jj * 128: half * 512 + (jj + 1) * 128],
                                                      xn[jj][:, k * 128:(k + 1) * 128], ident[:])
                            return ins
                        R.pe(trs, reads=[("xn", jj) for jj in range(nsub)] + ["ident"], writes=[("tp", ti)])
                        for half in range(2):
                            k = 2 * kk + half
                            R.dve(lambda e, k=k, ti=ti, half=half, wi=wi, T=T, r=r: e.tensor_scalar(
                                out=hT[wi][:, k, 0:T], in0=tp[ti][:, half * 512: half * 512 + T],
                                scalar1=Acol[r][:, k:k + 1], scalar2=Scol[r][:, k:k + 1],
                                op0=ALU.mult, op1=ALU.add),
                                reads=[("tp", ti), ("Acol", r), ("Scol", r)], writes=[("hT", wi, k)])
                    hkeys = [("hT", wi, k) for k in range(8)]

                    if PS < 4:
                        continue
                    chunks = []
                    for c in range(NQC if PS >= 5 else 0):
                        chunks.append(("q", c, colq + c * 128))
                    for c in range(NKC if PS >= 5 else 0):
                        chunks.append(("k", c, colk + c * 128))
                    for c in range(8):
                        chunks.append(("z", c, colz + c * 128))

                    def stage1(ch):
                        kind, c, col0 = ch
                        bi = cnt["pm"] % 4
                        cnt["pm"] += 1

                        def mm(e, col0=col0, bi=bi, wi=wi, T=T):
                            ins = None
                            for k in range(8):
                                ins = e.matmul(pm[bi][:, 0:T], wbf[:, k, col0:col0 + 128], hT[wi][:, k, 0:T],
                                               start=(k == 0), stop=(k == 7))
                            return ins
                        R.pe(mm, reads=hkeys + wkeys, writes=[("pm", bi)])
                        return bi

                    def stage2(ch, bi):
                        kind, c, col0 = ch
                        oi = cnt["oc"] % NOC
                        cnt["oc"] += 1
                        if kind == "z":
                            R.act(lambda e, bi=bi, oi=oi, T=T: e.activation(out=oc[oi][:, 0:T], in_=pm[bi][:, 0:T], func=AF.Silu),
                                  reads=[("pm", bi)], writes=[("oc", oi)])
                            dst = zT[l]
                            dkey = ("zT", c, w)
                        else:
                            ri = cnt["rp"] % 2
                            cnt["rp"] += 1
                            if is_a:
                                srcap = pm[bi]
                                skey = ("pm", bi)
                            else:
                                b = cnt["b"] % 2
                                cnt["b"] += 1
                                gi = 0 if kind == "q" else 1
                                R.act(lambda e, bi=bi, b=b, T=T: e.activation(out=sqb[b][:, 0:T], in_=pm[bi][:, 0:T], func=AF.Square),
                                      reads=[("pm", bi)], writes=[("sqb", b)])
                                R.pe(lambda e, b=b, T=T: e.matmul(pss[b][:, 0:T], ones_bf[:], sqb[b][:, 0:T], start=True, stop=True),
                                     reads=[("sqb", b), "ones"], writes=[("pss", b)])
                                R.act(lambda e, b=b, T=T: e.activation(out=lnt[b][:, 0:T], in_=pss[b][:, 0:T], func=AF.Ln,
                                                                      bias=epsb[:], scale=1.0 / 128),
                                      reads=[("pss", b), "eps"], writes=[("lnt", b)])
                                R.act(lambda e, b=b, T=T: e.activation(out=rsb[b][:, 0:T], in_=lnt[b][:, 0:T], func=AF.Exp, scale=-0.5),
                                      reads=[("lnt", b)], writes=[("rsb", b)])
                                R.dve(lambda e, b=b, bi=bi, gi=gi, T=T: e.scalar_tensor_tensor(
                                    out=qn[b][:, 0:T], in0=pm[bi][:, 0:T], scalar=gqk[:, gi:gi + 1], in1=rsb[b][:, 0:T],
                                    op0=ALU.mult, op1=ALU.mult),
                                    reads=[("pm", bi), ("rsb", b), ("gqk", gi)], writes=[("qn", b)])
                                srcap = qn[b]
                                skey = ("qn", b)
                            R.dve(lambda e, srcap=srcap, ri=ri, T=T: e.tensor_copy(out=swt[ri][0:64, 0:T], in_=srcap[64:128, 0:T]),
                                  reads=[skey], writes=[("swt", ri, 0)])
                            R.dve(lambda e, srcap=srcap, ri=ri, T=T: e.tensor_copy(out=swt[ri][64:128, 0:T], in_=srcap[0:64, 0:T]),
                                  reads=[skey], writes=[("swt", ri, 1)])
                            R.dve(lambda e, srcap=srcap, ri=ri, wi=wi, T=T: e.tensor_tensor(
                                out=t1[ri][:, 0:T], in0=srcap[:, 0:T], in1=rCt[wi][:, 0:T], op=ALU.mult),
                                reads=[skey, ("rCt", wi)], writes=[("t1", ri)])
                            R.dve(lambda e, ri=ri, wi=wi, T=T: e.tensor_tensor(
                                out=t2[ri][:, 0:T], in0=swt[ri][:, 0:T], in1=rSt[wi][:, 0:T], op=ALU.mult),
                                reads=[("swt", ri, 0), ("swt", ri, 1), ("rSt", wi)], writes=[("t2", ri)])
                            R.dve(lambda e, ri=ri, oi=oi, T=T: e.tensor_tensor(
                                out=oc[oi][:, 0:T], in0=t1[ri][:, 0:T], in1=t2[ri][:, 0:T], op=ALU.add),
                                reads=[("t1", ri), ("t2", ri)], writes=[("oc", oi)])
                            dst = qT[l] if kind == "q" else kTb[l]
                            dkey = ("qT" if kind == "q" else "kTb", c, w)
                        R.dma("pool", lambda e, dst=dst, c=c, oi=oi, tok0=tok0, T=T: e.dma_start(
                            out=dst[c * 128:(c + 1) * 128, tok0:tok0 + T], in_=oc[oi][:, 0:T]),
                            reads=[("oc", oi)], writes=[dkey])

                    prev = None
                    for ch in chunks:
                        bi = stage1(ch)
                        if prev is not None:
                            stage2(*prev)
                        prev = (ch, bi)
                    vprev = None
                    for jj in range(nsub if PS >= 6 else 0):
                        vi = cnt["vo"] % 2
                        cnt["vo"] += 1
                        for c0 in range(0, FV, 512):
                            cw = min(512, FV - c0)
                            bi = cnt["pm"] % 4
                            cnt["pm"] += 1

                            def mmv(e, jj=jj, c0=c0, cw=cw, bi=bi, wi=wi):
                                ins = None
                                for k in range(8):
                                    ins = e.matmul(pm[bi][:, 0:cw], hT[wi][:, k, jj * 128:(jj + 1) * 128],
                                                   wbf[:, k, colv + c0: colv + c0 + cw], start=(k == 0), stop=(k == 7))
                                return ins
                            R.pe(mmv, reads=hkeys + wkeys, writes=[("pm", bi)])
                            if prev is not None:
                                stage2(*prev)
                                prev = None
                            if os.environ.get('KDEBUG_VCOPY', 'act') == 'dve':
                                R.dve(lambda e, vi=vi, c0=c0, cw=cw, bi=bi: e.tensor_copy(out=vo[vi][:, c0:c0 + cw], in_=pm[bi][:, 0:cw]),
                                      reads=[("pm", bi)], writes=[("vo", vi, c0)])
                            else:
                                R.act(lambda e, vi=vi, c0=c0, cw=cw, bi=bi: e.copy(out=vo[vi][:, c0:c0 + cw], in_=pm[bi][:, 0:cw]),
                                      reads=[("pm", bi)], writes=[("vo", vi, c0)])
                        R.dma("pool", lambda e, vi=vi, jj=jj, tok0=tok0: e.dma_start(
                            out=vb[l].rearrange("(h t) d -> t h d", h=NVH)[tok0 + jj * 128: tok0 + (jj + 1) * 128, :, :],
                            in_=vo[vi][:].rearrange("p (h d) -> p h d", h=NVH)),
                            reads=[("vo", vi, c0) for c0 in range(0, FV, 512)], writes=[("vb", w, jj)])
                    if prev is not None:
                        stage2(*prev)
                        prev = None
                R.flush()

            if not run_ph('X'):
                continue
            RG = [[2 * p, 2 * p + 1] for p in range(NCORES // 2)]
            for c in range(max(NKC, NVH)):
                if c < NKC:
                    R.op("pool", lambda e, c=c: e.collective_compute(
                        "AllGather", ALU.bypass, replica_groups=RG,
                        ins=[kTb[l][c * 128:(c + 1) * 128, :].opt()], outs=[kTg[l][c * 256:(c + 1) * 256, :].opt()]), kind="cc")
                if c < NVH:
                    R.op("pool", lambda e, c=c: e.collective_compute(
                        "AllGather", ALU.bypass, replica_groups=RG,
                        ins=[vb[l][c * NTOK:(c + 1) * NTOK, :].opt()], outs=[vg[l][c * 2 * NTOK:(c + 1) * 2 * NTOK, :].opt()]), kind="cc")
            R.flush()
            R.op("pool", lambda e: e.wait_ge(cc_sem, R.cc_n), kind="c")
            R.flush()

            if not run_ph('A'):
                continue
            with contextlib.ExitStack() as st:
                NU = 2 if is_a else 1
                kTs = [sb(st, f"kTs{i}", [128, 2 * NTOK], BF16) for i in range(2)]
                vs = [sb(st, f"vs{i}", [128, NKT, 128], BF16) for i in range(2)]
                qs = [[sb(st, f"qs{i}_{s}", [128, NTOK], BF16) for s in range(NU)] for i in range(2)]
                zs = [sb(st, f"zs{i}", [128, NTOK], BF16) for i in range(2)]
                NPT = 4
                pt = [sb(st, f"pt{i}", [128, 1024], BF16) for i in range(NPT)]
                accS = [sb(st, f"accS{i}", [128, 512], BF16) for i in range(2)]
                accP = [sb(st, f"accP{i}", [128, 1024], BF16) for i in range(2)]
                rr = [sb(st, f"rr{i}", [128, 512], F32) for i in range(2)]
                o_s = [sb(st, f"o_s{i}", [128, 512], F32) for i in range(2)]
                ocmb = sb(st, "ocmb", [128, 512], F32)
                sqe = sb(st, "sqe", [128, 512], BF16)
                sqo = [sb(st, f"sqo{i}", [128, 512], BF16) for i in range(2)]
                lno = sb(st, "lno", [128, 512], F32)
                rso = sb(st, "rso", [128, 512], F32)
                ono = sb(st, "ono", [128, 512], F32)
                ogs = [sb(st, f"ogs{i}", [128, 512], BF16) for i in range(2)]
                negM = [sb(st, f"negM{i}", [128, 1], F32) for i in range(4)]
                stt = sb(st, "stt", [128, 8], F32)
                stg = sb(st, "stg", [128, 40], F32)
                kmx = [sb(st, f"kmx{i}", [128, 2], F32) for i in range(2)]
                qmx = sb(st, "qmx", [128, 2], F32)
                if is_a:
                    lamt = sb(st, "lamt", [128, 256], F32)
                    lamj = sb(st, "lamj", [128, 64], F32)
                    lams = sb(st, "lams", [128, 4], F32)
                    neglam = sb(st, "neglam", [128, 1], F32)
                    gsub = sb(st, "gsub", [128, 1], F32)
                Sg = [ps(st, f"Sg{i}", [128, 1024]) for i in range(2)]
                Ob = [ps(st, f"Ob{i}", [128, 512]) for i in range(2)]
                Lb = ps(st, "Lb", [128, 512])
                accPS = ps(st, "accPS", [128, 512])
                pstat = [(Sg[0][:, 0:512], ("S", 0, 0)), (Sg[0][:, 512:1024], ("S", 0, 1)),
                         (Sg[1][:, 0:512], ("S", 1, 0)), (Sg[1][:, 512:1024], ("S", 1, 1))]

                scale = (64 ** -0.5) if is_a else (128 ** -0.5)
                if is_a:
                    lam_init = lambda_init_fn(l)
                    for i in range(2):
                        for s in range(2):
                            R.dve(lambda e, i=i, s=s: e.memset(qs[i][s][:], 0.0), writes=[("qs", i, s), ("qs2", i, s)])
                    R.dma("sp", lambda e: e.dma_start(out=lamt[:], in_=a_lambda[j, :, :]), writes=["lamt"])
                    R.dma("sp", lambda e: e.dma_start(out=gsub[:], in_=a_subln[j, :, :]), writes=["gsub0"])
                    for i in range(2):
                        R.dve(lambda e, i=i: e.scalar_tensor_tensor(
                            out=lamj[:], in0=lamt[:, (2 * i) * 64:(2 * i + 1) * 64], scalar=1.0,
                            in1=lamt[:, (2 * i + 1) * 64:(2 * i + 2) * 64], op0=ALU.mult, op1=ALU.mult,
                            accum_out=lams[:, i:i + 1]), reads=["lamt"], writes=["lamj", ("lams", i)])
                    R.act(lambda e: e.activation(out=lams[:, 2:4], in_=lams[:, 0:2], func=AF.Exp),
                          reads=[("lams", 0), ("lams", 1)], writes=["lame"])
                    R.dve(lambda e: e.scalar_tensor_tensor(out=neglam[:], in0=lams[:, 3:4], scalar=-lam_init,
                                                           in1=lams[:, 2:3], op0=ALU.add, op1=ALU.subtract),
                          reads=["lame"], writes=["neglam"])
                    R.dve(lambda e: e.tensor_scalar(out=gsub[:], in0=gsub[:], scalar1=(1.0 - lam_init), scalar2=None,
                                                    op0=ALU.mult), reads=["gsub0"], writes=["gsub"])

                pending = []
                state = {"unit": 0, "og": 0, "kvslot": -1, "kvhead": -1, "nm": 0, "sq": 0, "g": 0}

                def drain(n=None, upto=None):
                    k = len(pending) if n is None else min(n, len(pending))
                    for _ in range(k):
                        if upto is not None and pending[0][0] > upto:
                            break
                        pending.pop(0)[1]()

                def defer(fn):
                    pending.append((state["unit"] - 1, fn))

                def maxsq(srct, ncols, rkeys, ind, dst, dkey):
                    nch = 0
                    for c0 in range(0, ncols, 512):
                        cw = min(512, ncols - c0)
                        b = state["sq"] % 2
                        pb, pkey = pstat[state["sq"] % 4]
                        state["sq"] += 1
                        R.act(lambda e, b=b, c0=c0, cw=cw: e.activation(out=sqo[b][:, 0:cw], in_=srct[:, c0:c0 + cw], func=AF.Square),
                              reads=rkeys, writes=[("sqo", b)])
                        R.pe(lambda e, b=b, cw=cw, pb=pb: e.matmul(pb[:, 0:cw], ind, sqo[b][:, 0:cw], start=True, stop=True),
                             reads=[("sqo", b)], writes=[pkey])
                        R.dve(lambda e, cw=cw, pb=pb, nch=nch: e.tensor_reduce(out=stg[:, nch:nch + 1], in_=pb[:, 0:cw], axis=AX.X, op=ALU.max),
                              reads=[pkey], writes=[("stg", nch)])
                        nch += 1
                    R.dve(lambda e, nch=nch: e.tensor_reduce(out=dst, in_=stg[:, 0:nch], axis=AX.X, op=ALU.max),
                          reads=[("stg", i) for i in range(nch)], writes=[dkey])

                NH = 8
                for h in range(NH):
                    hs = h % 2
                    hk = h if is_a else h // 4
                    if hk != state["kvhead"]:
                        state["kvhead"] = hk
                        state["kvslot"] = (state["kvslot"] + 1) % 2
                        ks = state["kvslot"]
                        for rk in range(2):
                            R.dma("sp", lambda e, ks=ks, rk=rk, hk=hk: e.dma_start(
                                out=kTs[ks][:, rk * NTOK:(rk + 1) * NTOK],
                                in_=kTg[l][hk * 256 + rk * 128: hk * 256 + (rk + 1) * 128, :]),
                                writes=[("kTs", ks, rk)])
                            vgv = vg[l][hk * 2 * NTOK:(hk + 1) * 2 * NTOK, :].rearrange("(kt p) f -> p kt f", p=128)
                            for part in range(4):
                                k0 = rk * KT_R + (KT_R * part) // 4
                                k1 = rk * KT_R + (KT_R * (part + 1)) // 4
                                if k1 > k0:
                                    R.dma("sp", lambda e, ks=ks, k0=k0, k1=k1, vgv=vgv: e.dma_start(
                                        out=vs[ks][:, k0:k1, :], in_=vgv[:, k0:k1, :]),
                                        writes=[("vs", ks, rk, part)])
                        if cfg.stab:
                            for s in range(NU):
                                ind = indm[:, s, :] if is_a else ones_bf[:]
                                maxsq(kTs[ks], 2 * NTOK, [("kTs", ks, 0), ("kTs", ks, 1)], ind, kmx[ks][:, s:s + 1], ("kmx", ks, s))
                    ks = state["kvslot"]
                    if is_a:
                        for s in range(2):
                            for half in range(2):
                                p0 = 64 * half + 32 * s
                                R.dma("sp", lambda e, hs=hs, s=s, p0=p0, h=h: e.dma_start(
                                    out=qs[hs][s][p0:p0 + 32, :], in_=qT[l][h * 128 + p0: h * 128 + p0 + 32, :]),
                                    writes=[("qs", hs, s)] if half == 0 else [("qs2", hs, s)])
                    else:
                        R.dma("sp", lambda e, hs=hs, h=h: e.dma_start(out=qs[hs][0][:], in_=qT[l][h * 128:(h + 1) * 128, :]),
                              writes=[("qs", hs, 0)])
                    R.dma("sp", lambda e, hs=hs, h=h: e.dma_start(out=zs[hs][:], in_=zT[l][h * 128:(h + 1) * 128, :]),
                          writes=[("zs", hs)])

                    nm = []
                    for s in range(NU):
                        mi = state["nm"] % 4
                        state["nm"] += 1
                        nm.append(mi)
                        if cfg.stab:
                            maxsq(qs[hs][s], NTOK, [("qs", hs, s), ("qs2", hs, s)], ones_bf[:], qmx[:, s:s + 1], ("qmx", s))
                            R.dve(lambda e, ks=ks, s=s: e.tensor_tensor(out=stt[:, 3:4], in0=kmx[ks][:, s:s + 1], in1=qmx[:, s:s + 1], op=ALU.mult),
                                  reads=[("kmx", ks, s), ("qmx", s)], writes=[("stt", 3)])
                            R.act(lambda e: e.activation(out=stt[:, 4:5], in_=stt[:, 3:4], func=AF.Ln, bias=epsb[:], scale=1.0),
                                  reads=[("stt", 3), "eps"], writes=[("stt", 4)])
                            R.act(lambda e: e.activation(out=stt[:, 5:6], in_=stt[:, 4:5], func=AF.Exp, scale=0.5),
                                  reads=[("stt", 4)], writes=[("stt", 5)])
                            R.dve(lambda e, mi=mi: e.tensor_scalar(out=negM[mi][:], in0=stt[:, 5:6], scalar1=-scale, scalar2=None, op0=ALU.mult),
                                  reads=[("stt", 5)], writes=[("negM", mi)])
                        else:
                            R.dve(lambda e, mi=mi: e.memset(negM[mi][:], 0.0), writes=[("negM", mi)])

                    def vpart(kt):
                        rk = kt // KT_R
                        for pp in range(4):
                            k0 = rk * KT_R + (KT_R * pp) // 4
                            k1 = rk * KT_R + (KT_R * (pp + 1)) // 4
                            if k0 <= kt < k1:
                                return rk, pp
                        return rk, 3

                    nqt = NW + (0 if last else 1)
                    for w in range(nqt):
                        T = 512 if w < NW else 128
                        tok0 = w * 512 if w < NW else LAT
                        ktl = list(range(NKT)) if w < NW else [KT_R - 1, 2 * KT_R - 1]
                        groups = [ktl[i:i + 2] for i in range(0, len(ktl), 2)]
                        if w >= NW:
                            drain()
                        for s in range(NU):
                            ob = state["unit"] % 2
                            ab = ob
                            state["unit"] += 1
                            qsk = [("qs", hs, s), ("qs2", hs, s)] if is_a else [("qs", hs, s)]
                            mi = nm[s]

                            def QK(gi, grp, s=s, T=T, tok0=tok0, ks=ks, hs=hs, qsk=qsk):
                                for a, kt in enumerate(grp):
                                    R.pe(lambda e, gi=gi, a=a, kt=kt: e.matmul(
                                        Sg[gi][:, a * 512: a * 512 + T], kTs[ks][:, kt * 128:(kt + 1) * 128],
                                        qs[hs][s][:, tok0:tok0 + T], start=True, stop=True),
                                        reads=[("kTs", ks, kt // KT_R)] + qsk, writes=[("S", gi, a)])
                            ng = len(groups)
                            drain(upto=state["unit"] - 3)
                            step = max(1, (ng - 1) // (len(pending) + 1))
                            used = {"d": False, "p": False}
                            QK(state["g"] % 2, groups[0])
                            for gidx, grp in enumerate(groups):
                                g = state["g"]
                                state["g"] += 1
                                gi = g % 2
                                pi = g % NPT
                                na = len(grp)
                                if gidx + 1 < ng:
                                    QK((g + 1) % 2, groups[gidx + 1])
                                S3 = Sg[gi][:].rearrange("p (a t) -> p a t", a=2)[:, 0:na, 0:T]
                                P3 = pt[pi][:].rearrange("p (a t) -> p a t", a=2)[:, 0:na, 0:T]
                                R.act(lambda e, S3=S3, P3=P3, mi=mi: e.activation(out=P3, in_=S3, func=AF.Exp, bias=negM[mi][:], scale=scale),
                                      reads=[("S", gi, a) for a in range(na)] + [("negM", mi)], writes=[("pt", pi)])
                                for a, kt in enumerate(grp):
                                    rk, part = vpart(kt)
                                    first = (gidx == 0 and a == 0)
                                    lastmm = (gidx == ng - 1 and a == na - 1)
                                    R.pe(lambda e, a=a, kt=kt, ob=ob, T=T, pi=pi, ks=ks, first=first, lastmm=lastmm: e.matmul(
                                        Ob[ob][:, 0:T], vs[ks][:, kt, :], pt[pi][:, a * 512: a * 512 + T], start=first, stop=lastmm),
                                        reads=[("vs", ks, rk, part), ("pt", pi)], writes=[("O", ob)])
                                use_d = ("DDPDP"[gidx % 5] == "D")
                                if use_d:
                                    for a in range(na):
                                        src_ = pt[pi][:, a * 512: a * 512 + T]
                                        if not used["d"]:
                                            used["d"] = True
                                            R.dve(lambda e, src_=src_, T=T: e.tensor_copy(out=accPS[:, 0:T], in_=src_),
                                                  reads=[("pt", pi)], writes=["accPS"])
                                        else:
                                            R.dve(lambda e, src_=src_, T=T: e.tensor_tensor(out=accPS[:, 0:T], in0=accPS[:, 0:T], in1=src_, op=ALU.add),
                                                  reads=[("pt", pi), "accPS"], writes=["accPS"])
                                else:
                                    A3 = accP[ab][:].rearrange("p (a t) -> p a t", a=2)[:, 0:na, 0:T]
                                    akey = ("accP", ab)
                                    if not used["p"]:
                                        used["p"] = True
                                        R.pool(lambda e, A3=A3, P3=P3: e.tensor_copy(out=A3, in_=P3), reads=[("pt", pi)], writes=[akey])
                                    else:
                                        R.pool(lambda e, A3=A3, P3=P3: e.tensor_tensor(out=A3, in0=A3, in1=P3, op=ALU.add),
                                               reads=[("pt", pi), akey], writes=[akey])
                                if gidx >= 1 and gidx % step == 0:
                                    drain(1)
                            na0 = len(groups[0])
                            srcs = []
                            if used["d"]:
                                R.dve(lambda e, ab=ab, T=T: e.tensor_copy(out=accS[ab][:, 0:T], in_=accPS[:, 0:T]),
                                      reads=["accPS"], writes=[("accS", ab)])
                                srcs += [(accS[ab], ("accS", ab), 0)]
                            if used["p"]:
                                srcs += [(accP[ab], ("accP", ab), a) for a in range(na0)]

                            def Lstage(srcs=srcs, T=T):
                                def mmL(e):
                                    ins = None
                                    for i, (t_, k_, a) in enumerate(srcs):
                                        ins = e.matmul(Lb[:, 0:T], ones_bf[:], t_[:, a * 512: a * 512 + T],
                                                       start=(i == 0), stop=(i == len(srcs) - 1))
                                    return ins
                                R.pe(mmL, reads=list({k_ for (_, k_, _) in srcs}), writes=["Lb"])
                            defer(Lstage)
                            defer(lambda s=s, T=T: R.dve(
                                lambda e: e.reciprocal(out=rr[s][:, 0:T], in_=Lb[:, 0:T]),
                                reads=["Lb"], writes=[("rr", s)]))
                            defer(lambda ob=ob, s=s, T=T: R.dve(
                                lambda e: e.tensor_tensor(out=o_s[s][:, 0:T], in0=Ob[ob][:, 0:T], in1=rr[s][:, 0:T], op=ALU.mult),
                                reads=[("O", ob), ("rr", s)], writes=[("o_s", s)]))
                        oi = state["og"] % 2
                        state["og"] += 1
                        if is_a:
                            defer(lambda T=T: R.dve(
                                lambda e: e.scalar_tensor_tensor(out=ocmb[:, 0:T], in0=o_s[1][:, 0:T], scalar=neglam[:],
                                                                 in1=o_s[0][:, 0:T], op0=ALU.mult, op1=ALU.add),
                                reads=[("o_s", 0), ("o_s", 1), "neglam"], writes=["ocmb"]))
                            defer(lambda T=T: R.act(
                                lambda e: e.activation(out=sqe[:, 0:T], in_=ocmb[:, 0:T], func=AF.Square),
                                reads=["ocmb"], writes=["sqe"]))
                            defer(lambda T=T: R.pe(
                                lambda e: e.matmul(Lb[:, 0:T], ones_bf[:], sqe[:, 0:T], start=True, stop=True),
                                reads=["sqe"], writes=["Lb"]))
                            defer(lambda T=T: R.act(
                                lambda e: e.activation(out=lno[:, 0:T], in_=Lb[:, 0:T], func=AF.Ln, bias=epsb[:], scale=1.0 / 128),
                                reads=["Lb"], writes=["lno"]))
                            defer(lambda T=T: R.act(
                                lambda e: e.activation(out=rso[:, 0:T], in_=lno[:, 0:T], func=AF.Exp, scale=-0.5),
                                reads=["lno"], writes=["rso"]))
                            defer(lambda T=T: R.dve(
                                lambda e: e.scalar_tensor_tensor(out=ono[:, 0:T], in0=ocmb[:, 0:T], scalar=gsub[:],
                                                                 in1=rso[:, 0:T], op0=ALU.mult, op1=ALU.mult),
                                reads=["ocmb", "rso", "gsub"], writes=["ono"]))
                            fin_src, fin_key = ono, "ono"
                        else:
                            fin_src, fin_key = o_s[0], ("o_s", 0)
                        defer(lambda T=T, oi=oi, tok0=tok0, fin_src=fin_src, fin_key=fin_key, hs=hs: R.dve(
                            lambda e: e.tensor_tensor(out=ogs[oi][:, 0:T], in0=fin_src[:, 0:T], in1=zs[hs][:, tok0:tok0 + T], op=ALU.mult),
                            reads=[fin_key, ("zs", hs)], writes=[("ogs", oi)]))
                        defer(lambda T=T, oi=oi, tok0=tok0, h=h, w=w: R.dma(
                            "pool", lambda e: e.dma_start(out=ogT[l][h * 128:(h + 1) * 128, tok0:tok0 + T], in_=ogs[oi][:, 0:T]),
                            reads=[("ogs", oi)], writes=[("ogT", h, w)]))
                drain()
                R.flush()

            if not run_ph('O'):
                continue
            with contextlib.ExitStack() as st:
                wo = sb(st, "wo", [128, 8, D], BF16)
                og = [sb(st, f"og{i}", [128, 8, 512], BF16) for i in range(2)]
                NXO = 6
                xo = [sb(st, f"xo{i}", [128, D], F32) for i in range(NXO)]
                yt = [sb(st, f"yt{i}", [128, D], F32) for i in range(2)]
                xw = [sb(st, f"xw{i}", [128, D], F32) for i in range(3)]
                sqy = sb(st, "sqy", [128, 512], BF16)
                ssy = [sb(st, f"ssy{i}", [128, 4], F32) for i in range(2)]
                py = [ps(st, f"py{i}", [128, 512]) for i in range(4)]
                wov = w_out[l].rearrange("(k p) n -> p k n", p=128)
                for k in range(8):
                    R.dma("pool", lambda e, k=k: e.dma_start(out=wo[:, k, :], in_=wov[:, k, :]), writes=[("wo", k)])
                wokeys = [("wo", k) for k in range(8)]
                ogv = ogT[l].rearrange("(k p) t -> p k t", p=128)
                tcount = 0
                nwt = NW + (0 if last else 1)
                for w in range(nwt):
                    T = 512 if w < NW else 128
                    nsub = T // 128
                    tok0 = w * 512 if w < NW else LAT
                    r = 0 if w < NW else 1
                    wi = w % 2
                    R.dma("sp", lambda e, wi=wi, tok0=tok0, T=T: e.dma_start(out=og[wi][:, :, 0:T], in_=ogv[:, :, tok0:tok0 + T]),
                          writes=[("og", wi)])
                    for jj in range(nsub):
                        xi = tcount % NXO
                        yi = tcount % 2
                        xwi = tcount % 3
                        tcount += 1
                        t0 = tok0 + jj * 128
                        R.dma("sp", lambda e, xi=xi, t0=t0: e.dma_start(out=xo[xi][:], in_=src[t0:t0 + 128, :]), writes=[("xo", xi)])
                        for nh in range(2):
                            bi = (2 * yi + nh)

                            def mmo(e, nh=nh, jj=jj, wi=wi, bi=bi):
                                ins = None
                                for k in range(8):
                                    ins = e.matmul(py[bi][:], og[wi][:, k, jj * 128:(jj + 1) * 128], wo[:, k, nh * 512:(nh + 1) * 512],
                                                   start=(k == 0), stop=(k == 7))
                                return ins
                            R.pe(mmo, reads=[("og", wi)] + wokeys, writes=[("py", bi)])
                            R.act(lambda e, bi=bi, yi=yi, nh=nh: e.activation(out=sqy[:], in_=py[bi][:], func=AF.Square,
                                                                              accum_out=ssy[yi][:, nh:nh + 1]),
                                  reads=[("py", bi)], writes=["sqy", ("ssy", yi, nh)])
                        R.dve(lambda e, yi=yi: e.tensor_tensor(out=ssy[yi][:, 2:3], in0=ssy[yi][:, 0:1], in1=ssy[yi][:, 1:2], op=ALU.add),
                              reads=[("ssy", yi, 0), ("ssy", yi, 1)], writes=[("ssy", yi, 2)])
                        R.act(lambda e, yi=yi: e.activation(out=ssy[yi][:, 3:4], in_=ssy[yi][:, 2:3], func=AF.Ln, bias=epsb[:], scale=1.0 / D),
                              reads=[("ssy", yi, 2), "eps"], writes=[("ssy", yi, 3)])
                        R.act(lambda e, yi=yi: e.activation(out=ssy[yi][:, 2:3], in_=ssy[yi][:, 3:4], func=AF.Exp, scale=-0.5),
                              reads=[("ssy", yi, 3)], writes=[("ssy", yi, 4)])
                        for nh in range(2):
                            bi = (2 * yi + nh)
                            R.dve(lambda e, bi=bi, yi=yi, nh=nh, r=r: e.scalar_tensor_tensor(
                                out=yt[yi][:, nh * 512:(nh + 1) * 512], in0=py[bi][:], scalar=ssy[yi][:, 2:3],
                                in1=Gb[r][:, nh * 512:(nh + 1) * 512], op0=ALU.mult, op1=ALU.mult),
                                reads=[("py", bi), ("ssy", yi, 4), ("Gb", r)], writes=[("yt", yi, nh)])
                        R.dve(lambda e, yi=yi, xi=xi, xwi=xwi: e.tensor_tensor(out=xw[xwi][:], in0=yt[yi][:], in1=xo[xi][:], op=ALU.add),
                              reads=[("yt", yi, 0), ("yt", yi, 1), ("xo", xi)], writes=[("xw", xwi)])
                        if last:
                            R.dma("pool", lambda e, xwi=xwi, t0=t0: e.dma_start(out=out[t0:t0 + 128, :], in_=xw[xwi][:]),
                                  reads=[("xw", xwi)], writes=[("out", t0)])
                        else:
                            R.dma("pool", lambda e, xwi=xwi, t0=t0: e.dma_start(out=xs[t0:t0 + 128, :], in_=xw[xwi][:]),
                                  reads=[("xw", xwi)], writes=[("xs", t0)])
                R.flush()
    return nc


def _rope_tables(cfg, hf, head_dim, dup):
    LAT, NTOK = cfg.LAT, cfg.NTOK
    t = np.arange(hf * LAT, (hf + 1) * LAT)
    rows = (t // GRID_W).astype(np.float32)
    cols = (t % GRID_W).astype(np.float32)
    axis_dim = head_dim // 2
    freqs = (ROPE_THETA ** (-np.arange(0, axis_dim, 2, dtype=np.float32) / np.float32(axis_dim))).astype(np.float32)
    ang = np.concatenate([rows[:, None] * freqs, cols[:, None] * freqs], axis=-1).astype(np.float32)
    cos = np.cos(ang).astype(np.float32).T
    sin = np.sin(ang).astype(np.float32).T
    half = head_dim // 2
    C = np.ones((128, NTOK), np.float32)
    S = np.zeros((128, NTOK), np.float32)
    if dup:
        for blk in range(4):
            C[blk * 32:(blk + 1) * 32, :LAT] = cos
            S[blk * 32:(blk + 1) * 32, :LAT] = -sin if blk < 2 else sin
    else:
        C[0:64, :LAT] = cos
        C[64:128, :LAT] = cos
        S[0:64, :LAT] = -sin
        S[64:128, :LAT] = sin
    return C, S


def _perm_a_cols():
    perm = np.arange(4096)
    p128 = np.zeros(128, np.int64)
    for n in range(128):
        blk = n // 32
        s = blk % 2
        d = (n % 32) + (32 if blk >= 2 else 0)
        p128[n] = s * 64 + d
    for base in (0, 1024):
        for h in range(8):
            perm[base + h * 128: base + (h + 1) * 128] = base + h * 128 + p128
    return perm


def make_in_maps(cfg, x, c, ctx, c_ctx, ada_w, ada_b, pre_g, post_g, w_out, a_w_in, a_lambda, a_subln_g, b_w_in, b_qk_g):
    DEPTH = cfg.DEPTH
    f = lambda a: np.ascontiguousarray(np.asarray(a, dtype=np.float32))
    x, c, ctx, c_ctx = f(x), f(c), f(ctx), f(c_ctx)
    NA = (DEPTH + 1) // 2
    NB = max(DEPTH // 2, 1)
    shared = {
        "ada_w": f(ada_w)[:DEPTH],
        "ada_b": np.ascontiguousarray(np.broadcast_to(f(ada_b)[:DEPTH, None, :], (DEPTH, 128, 3 * D))),
        "pre_g": np.ascontiguousarray(np.broadcast_to(f(pre_g)[:DEPTH, None, :], (DEPTH, 128, D))),
        "post_g": np.ascontiguousarray(np.broadcast_to(f(post_g)[:DEPTH, None, :], (DEPTH, 128, D))),
        "w_out": f(w_out)[:DEPTH],
        "a_w_in": np.ascontiguousarray(f(a_w_in)[:NA][:, :, _perm_a_cols()]),
        "a_lambda": np.ascontiguousarray(np.broadcast_to(f(a_lambda)[:NA].reshape(NA, 1, 256), (NA, 128, 256))),
        "a_subln": np.ascontiguousarray(f(a_subln_g)[:NA].reshape(NA, 128, 1)),
        "b_w_in": f(b_w_in)[:NB],
        "b_qk_g": np.ascontiguousarray(f(b_qk_g)[:NB].reshape(NB, 2, 128, 1)),
        "ident": np.eye(128, dtype=np.float32).astype(ml_dtypes.bfloat16),
    }
    ind = np.zeros((2, 128, 128), np.float32)
    for s in range(2):
        for p in range(128):
            if (p // 32) % 2 == s:
                ind[s, p, :] = 1.0
    shared["indmat"] = ind.astype(ml_dtypes.bfloat16)
    maps = []
    for i in range(NCORES):
        b, hf = i // 2, i % 2
        LAT = cfg.LAT
        xin = np.concatenate([x[b, hf * LAT:(hf + 1) * LAT], ctx[b, hf * CTXH:(hf + 1) * CTXH]], axis=0)
        cin = np.stack([c[b].reshape(8, 128).T, c_ctx.reshape(8, 128).T], axis=1)
        ac, as_ = _rope_tables(cfg, hf, 64, True)
        bc, bs = _rope_tables(cfg, hf, 128, False)
        m = dict(shared)
        m.update({"xin": np.ascontiguousarray(xin), "cin": np.ascontiguousarray(cin),
                  "ropeA_C": ac, "ropeA_S": as_, "ropeB_C": bc, "ropeB_S": bs})
        maps.append(m)
    return maps


_CACHE = {}


def run(cfg, inputs, trace=False):
    key = (cfg.SEQ, cfg.DEPTH, cfg.debug, cfg.stab, cfg.stop)
    if key not in _CACHE:
        _CACHE[key] = build_program(cfg)
    nc = _CACHE[key]
    maps = make_in_maps(cfg, **inputs)
    res = run_bass_kernel_spmd(nc, maps, core_ids=list(range(NCORES)))
    return res


def kernel(x, c, ctx, c_ctx, ada_w, ada_b, pre_g, post_g, w_out, a_w_in, a_lambda, a_subln_g, b_w_in, b_qk_g):
    x = np.asarray(x)
    B, S, _ = x.shape
    cfg = Cfg(S, 4)
    res = run(cfg, dict(x=x, c=c, ctx=ctx, c_ctx=c_ctx, ada_w=ada_w, ada_b=ada_b, pre_g=pre_g, post_g=post_g,
                        w_out=w_out, a_w_in=a_w_in, a_lambda=a_lambda, a_subln_g=a_subln_g, b_w_in=b_w_in, b_qk_g=b_qk_g))
    outp = np.empty((B, S, D), np.float32)
    for i in range(NCORES):
        b, hf = i // 2, i % 2
        outp[b, hf * cfg.LAT:(hf + 1) * cfg.LAT] = np.asarray(res.results[i]["out"], dtype=np.float32)
    return outp
```

```python
import math
import numpy as np
import ml_dtypes
import concourse.bass as bass
import concourse.mybir as mybir
from concourse.bass_utils import run_bass_kernel_spmd

F32 = mybir.dt.float32
BF16 = mybir.dt.bfloat16
AF = mybir.ActivationFunctionType
ALU = mybir.AluOpType
AX = mybir.AxisListType

D = 1024
NCORES = 8
CTXH = 128
EPS = 1e-6
ROPE_THETA = 10000.0
GRID_W = 64


def lambda_init_fn(i):
    return 0.8 - 0.6 * math.exp(-0.3 * i)


class Op:
    __slots__ = ("eng", "fn", "deps", "kind", "signal", "val", "sem", "prewait")

    def __init__(self, eng, fn, deps, kind):
        self.eng = eng
        self.fn = fn
        self.deps = deps
        self.kind = kind
        self.signal = kind != "c"
        self.val = None
        self.sem = None
        self.prewait = None


class Rec:
    ENGS = ("pe", "act", "dve", "pool", "sp")
    NPOOL = 8

    def __init__(self, nc, sems, dma_sems, cc_sem):
        self.nc = nc
        self.sems = sems
        self.dma_sems = dma_sems
        self.cc_sem = cc_sem
        self.cnt = {e: 0 for e in self.ENGS}
        self.dma_n = {e: 0 for e in self.ENGS}
        self.cc_n = 0
        self.ops = []
        self.last_w = {}
        self.readers = {}
        self.nops = 0

    def op(self, eng, fn, reads=(), writes=(), kind="c"):
        deps = set()
        for k in reads:
            w = self.last_w.get(k)
            if w is not None:
                deps.add(w)
        for k in writes:
            w = self.last_w.get(k)
            if w is not None:
                deps.add(w)
            for r in self.readers.get(k, ()):
                deps.add(r)
        o = Op(eng, fn, deps, kind)
        for k in reads:
            self.readers.setdefault(k, []).append(o)
        for k in writes:
            self.last_w[k] = o
            self.readers[k] = []
        self.ops.append(o)
        return o

    def pe(self, fn, reads=(), writes=()):
        return self.op("pe", fn, reads, writes)

    def act(self, fn, reads=(), writes=()):
        return self.op("act", fn, reads, writes)

    def dve(self, fn, reads=(), writes=()):
        return self.op("dve", fn, reads, writes)

    def pool(self, fn, reads=(), writes=()):
        return self.op("pool", fn, reads, writes)

    def dma(self, eng, fn, reads=(), writes=()):
        return self.op(eng, fn, reads, writes, kind="d")

    def flush(self):
        nc = self.nc
        ops = self.ops
        live = set(id(o) for o in ops)
        for o in ops:
            nd = set()
            for d in o.deps:
                if id(d) not in live:
                    continue
                if d.eng == "pe" and o.eng == "pe" and d.kind == "c":
                    continue
                nd.add(d)
                d.signal = True
            o.deps = nd
        per = {e: [] for e in self.ENGS}
        for o in ops:
            per[o.eng].append(o)
            if o.kind == "d":
                n = self.dma_n[o.eng]
                self.dma_n[o.eng] = n + 1
                o.sem = self.dma_sems[o.eng][n % self.NPOOL]
                o.val = 16 * (n // self.NPOOL + 1)
                if n >= self.NPOOL:
                    o.prewait = (o.sem, 16 * (n // self.NPOOL))
            elif o.kind == "cc":
                self.cc_n += 1
                o.sem = self.cc_sem
                o.val = self.cc_n
            elif o.signal:
                self.cnt[o.eng] += 1
                o.sem = self.sems[o.eng]
                o.val = self.cnt[o.eng]
        dma_final = {}
        for e in self.ENGS:
            n = self.dma_n[e]
            fin = []
            for i in range(min(n, self.NPOOL)):
                last = ((n - 1 - i) // self.NPOOL) * self.NPOOL + i
                fin.append((self.dma_sems[e][i], 16 * (last // self.NPOOL + 1)))
            dma_final[e] = fin
        self.nops += len(ops)

        def emit(e_ops, eng_name):
            def body(e):
                waited = {}
                for o in e_ops:
                    ws = []
                    if o.prewait is not None:
                        ws.append(o.prewait)
                    for d in o.deps:
                        ws.append((d.sem, d.val))
                    for (s, v) in ws:
                        key = id(s)
                        if waited.get(key, 0) >= v:
                            continue
                        waited[key] = v
                        e.wait_ge(s, v)
                    ins = o.fn(e)
                    if o.kind == "d":
                        ins.then_inc(o.sem, 16)
                    elif o.kind == "cc":
                        ins.then_inc(o.sem)
                    elif o.signal:
                        ins.then_inc(o.sem, 1)
                for (s, v) in dma_final[eng_name]:
                    if waited.get(id(s), 0) < v:
                        e.wait_ge(s, v)
            return body

        with nc.Block() as block:
            reg = {"pe": block.tensor, "act": block.scalar, "dve": block.vector,
                   "pool": block.gpsimd, "sp": block.sync}
            for e in self.ENGS:
                if per[e] or dma_final[e]:
                    reg[e](emit(per[e], e))
        self.ops = []
        self.last_w = {}
        self.readers = {}


class Cfg:
    def __init__(self, seq, depth, debug=False, stab=True, stop=None):
        self.stop = stop
        self.SEQ = seq
        self.DEPTH = depth
        self.LAT = seq // 2
        self.NTOK = self.LAT + CTXH
        self.NW = self.LAT // 512
        self.KT_R = self.NTOK // 128
        self.NKT = 2 * self.KT_R
        self.debug = debug
        self.stab = stab


def layer_dims(l):
    if l % 2 == 0:
        return True, 1024, 1024, 1024, 4096, 0, 1024, 2048, 3072
    return False, 1024, 256, 256, 2560, 0, 1024, 1280, 1536


def build_program(cfg):
    nc = bass.Bass("TRN2", target_bir_lowering=False)
    NTOK, LAT, NW, KT_R, NKT, DEPTH = cfg.NTOK, cfg.LAT, cfg.NW, cfg.KT_R, cfg.NKT, cfg.DEPTH
    dbg_kind = "ExternalOutput" if cfg.debug else "Internal"

    def din(name, shape, dt=F32):
        return nc.dram_tensor(name, list(shape), dt, kind="ExternalInput")

    xin = din("xin", [NTOK, D])
    cin = din("cin", [128, 2, 8])
    ada_w = din("ada_w", [DEPTH, D, 3 * D])
    ada_b = din("ada_b", [DEPTH, 128, 3 * D])
    pre_g = din("pre_g", [DEPTH, 128, D])
    post_g = din("post_g", [DEPTH, 128, D])
    w_out = din("w_out", [DEPTH, D, D])
    NA = (DEPTH + 1) // 2
    NB = max(DEPTH // 2, 1)
    a_w_in = din("a_w_in", [NA, D, 4096])
    a_lambda = din("a_lambda", [NA, 128, 256])
    a_subln = din("a_subln", [NA, 128, 1])
    b_w_in = din("b_w_in", [NB, D, 2560])
    b_qk_g = din("b_qk_g", [NB, 2, 128, 1])
    ropeC = [din("ropeA_C", [128, NTOK]), din("ropeB_C", [128, NTOK])]
    ropeS = [din("ropeA_S", [128, NTOK]), din("ropeB_S", [128, NTOK])]
    ident_in = din("ident", [128, 128], BF16)
    ind_in = din("indmat", [2, 128, 128], BF16)
    out = nc.dram_tensor("out", [LAT, D], F32, kind="ExternalOutput")

    xs = nc.dram_tensor("xs", [NTOK, D], F32, kind=dbg_kind)
    qT, zT, ogT, kTb, kTg, vb, vg = [], [], [], [], [], [], []
    for l in range(DEPTH):
        is_a, FQ, FK, FV, *_ = layer_dims(l)
        qT.append(nc.dram_tensor(f"qT{l}", [FQ, NTOK], BF16, kind=dbg_kind))
        zT.append(nc.dram_tensor(f"zT{l}", [D, NTOK], BF16, kind=dbg_kind))
        ogT.append(nc.dram_tensor(f"ogT{l}", [D, NTOK], BF16, kind=dbg_kind))
        kTb.append(nc.dram_tensor(f"kTb{l}", [FK, NTOK], BF16))
        kTg.append(nc.dram_tensor(f"kTg{l}", [2 * FK, NTOK], BF16))
        vb.append(nc.dram_tensor(f"vb{l}", [(FV // 128) * NTOK, 128], BF16))
        vg.append(nc.dram_tensor(f"vg{l}", [(FV // 128) * 2 * NTOK, 128], BF16))

    import contextlib
    es = contextlib.ExitStack()
    with es:
        def sem(name):
            return es.enter_context(nc.semaphore(name))

        sems = {e: sem(f"s_{e}") for e in Rec.ENGS}
        dma_sems = {e: [sem(f"d_{e}{i}") for i in range(Rec.NPOOL)] for e in ("sp", "pool", "act")}
        dma_sems["pe"] = dma_sems["dve"] = []
        cc_sem = sem("cc")
        R = Rec(nc, sems, dma_sems, cc_sem)

        l_tag = ["g"]

        def sb(st, name, shape, dt):
            return st.enter_context(nc.sbuf_tensor(f"sb{l_tag[0]}_{name}", list(shape), dt))

        def ps(st, name, shape, dt=F32):
            return st.enter_context(nc.psum_tensor(f"ps{l_tag[0]}_{name}", list(shape), dt))

        ident = sb(es, "ident", [128, 128], BF16)
        identf = sb(es, "identf", [128, 128], F32)
        ones_bf = sb(es, "ones_bf", [128, 128], BF16)
        onesf = sb(es, "onesf", [128, 128], F32)
        indm = sb(es, "indm", [128, 2, 128], BF16)
        epsb = sb(es, "epsb", [128, 1], F32)
        cint = sb(es, "cint", [128, 2, 8], F32)
        scs = sb(es, "scs", [128, 2, 8], F32)
        scb = sb(es, "scb", [128, 2, 8, 128], F32)
        Gb = [sb(es, f"Gb{r}", [128, D], F32) for r in range(2)]
        Acol = [sb(es, f"Acol{r}", [128, 8], F32) for r in range(2)]
        Scol = [sb(es, f"Scol{r}", [128, 8], F32) for r in range(2)]

        R.dma("sp", lambda e: e.dma_start(out=ident[:], in_=ident_in[:, :]), writes=["ident"])
        R.dma("sp", lambda e: e.dma_start(out=indm[:], in_=ind_in.ap().rearrange("s p m -> p s m")),
              writes=["indm"])
        R.dma("sp", lambda e: e.dma_start(out=cint[:], in_=cin[:, :, :]), writes=["cint"])
        R.dve(lambda e: e.tensor_copy(out=identf[:], in_=ident[:]), reads=["ident"], writes=["identf"])
        R.dve(lambda e: e.memset(ones_bf[:], 1.0), writes=["ones"])
        R.dve(lambda e: e.memset(onesf[:], 1.0), writes=["onesf"])
        R.dve(lambda e: e.memset(epsb[:], EPS), writes=["eps"])
        import os
        ZS = int(os.environ.get('KDEBUG_ZSTEP', '9'))
        if ZS >= 2:
            R.act(lambda e: e.activation(out=scs[:], in_=cint[:], func=AF.Silu), reads=["cint"], writes=["scs"])
        for r in range(2 if ZS >= 3 else 0):
            for k in range(8):
                R.dve(lambda e, r=r, k=k: e.tensor_scalar(
                    out=scb[:, r, k, :], in0=onesf[:], scalar1=scs[:, r, k:k + 1], scalar2=None,
                    op0=ALU.mult), reads=["onesf", "scs"], writes=[("scb", r, k)])
        R.flush()

        for l in range(DEPTH):
            if cfg.stop == 'Z':
                break
            is_a, FQ, FK, FV, NCOL, colq, colk, colv, colz = layer_dims(l)
            l_tag[0] = str(l)
            j = l // 2
            last = l == DEPTH - 1
            w_in = a_w_in if is_a else b_w_in
            rC, rS = (ropeC[0], ropeS[0]) if is_a else (ropeC[1], ropeS[1])
            src = xin if l == 0 else xs
            NQC = FQ // 128
            NKC = FK // 128
            NVH = FV // 128

            PH = ['M', 'P', 'X', 'A', 'O']
            run_ph = lambda t: cfg.stop is None or PH.index(t) <= PH.index(cfg.stop)
            with contextlib.ExitStack() as st:
                awt = [sb(st, f"awt{i}", [128, 8, 512], F32) for i in range(2)]
                modt = [sb(st, f"modt{r}", [128, 3 * D], F32) for r in range(2)]
                adab = sb(st, "adab", [128, 3 * D], F32)
                pgb = sb(st, "pgb", [128, D], F32)
                qgb = sb(st, "qgb", [128, D], F32)
                tmpA = sb(st, "tmpA", [128, D], F32)
                junk = sb(st, "junkM", [128, 128], F32)
                pm = [ps(st, f"pmM{i}", [128, 512]) for i in range(4)]
                R.dma("sp", lambda e: e.dma_start(out=adab[:], in_=ada_b[l, :, :]), writes=["adab"])
                R.dma("sp", lambda e: e.dma_start(out=pgb[:], in_=pre_g[l, :, :]), writes=["pgb"])
                R.dma("sp", lambda e: e.dma_start(out=qgb[:], in_=post_g[l, :, :]), writes=["qgb"])
                awv = ada_w[l].rearrange("(k p) n -> p k n", p=128)
                for c in range(6):
                    R.dma("sp", lambda e, c=c: e.dma_start(out=awt[c % 2][:], in_=awv[:, :, c * 512:(c + 1) * 512]),
                          writes=[("awt", c % 2)])
                    for r in range(2):
                        bank = pm[(2 * c + r) % 4]

                        def mm(e, c=c, r=r, bank=bank):
                            ins = None
                            for k in range(8):
                                ins = e.matmul(bank[:], scb[:, r, k, :], awt[c % 2][:, k, :],
                                               start=(k == 0), stop=(k == 7))
                            return ins
                        R.pe(mm, reads=[("awt", c % 2)], writes=[("pmM", (2 * c + r) % 4)])
                        R.dve(lambda e, c=c, r=r, bank=bank: e.tensor_tensor(
                            out=modt[r][:, c * 512:(c + 1) * 512], in0=bank[:],
                            in1=adab[:, c * 512:(c + 1) * 512], op=ALU.add),
                            reads=[("pmM", (2 * c + r) % 4), "adab"], writes=[("modt", r, c)])
                import os
                MS = int(os.environ.get('KDEBUG_MSTEP', '9'))
                for r in range(2 if MS >= 2 else 0):
                    allmod = [("modt", r, c) for c in range(6)]
                    R.dve(lambda e, r=r: e.scalar_tensor_tensor(
                        out=tmpA[:], in0=modt[r][:, D:2 * D], scalar=1.0, in1=pgb[:],
                        op0=ALU.add, op1=ALU.mult), reads=allmod + ["pgb"], writes=["tmpA"])
                    for k in range(8):
                        R.dve(lambda e, r=r, k=k: e.scalar_tensor_tensor(
                            out=junk[:], in0=tmpA[:, k * 128:(k + 1) * 128], scalar=1.0, in1=identf[:],
                            op0=ALU.mult, op1=ALU.mult, accum_out=Acol[r][:, k:k + 1]),
                            reads=["tmpA"], writes=["junkM", ("Acol", r)])
                    for k in range(8):
                        R.dve(lambda e, r=r, k=k: e.scalar_tensor_tensor(
                            out=junk[:], in0=modt[r][:, k * 128:(k + 1) * 128], scalar=1.0, in1=identf[:],
                            op0=ALU.mult, op1=ALU.mult, accum_out=Scol[r][:, k:k + 1]),
                            reads=allmod, writes=["junkM", ("Scol", r)])
                    R.dve(lambda e, r=r: e.tensor_tensor(out=Gb[r][:], in0=modt[r][:, 2 * D:3 * D], in1=qgb[:],
                                                         op=ALU.mult), reads=allmod + ["qgb"], writes=[("Gb", r)])
                R.flush()

            if not run_ph('P'):
                continue
            with contextlib.ExitStack() as st:
                wbf = sb(st, "wbf", [128, 8, NCOL], BF16)
                NXT = 8
                xt = [sb(st, f"xt{i}", [128, D], F32) for i in range(NXT)]
                xn = [sb(st, f"xn{i}", [128, D], BF16) for i in range(4)]
                sqj = sb(st, "sqj", [128, D], BF16)
                ssq = [sb(st, f"ssq{i}", [128, 4], F32) for i in range(2)]
                lnv = [sb(st, f"lnv{i}", [128, 4], F32) for i in range(2)]
                rst = [sb(st, f"rst{i}", [128, 4], F32) for i in range(2)]
                hT = [sb(st, f"hT{i}", [128, 8, 512], BF16) for i in range(2)]
                rCt = [sb(st, f"rCt{i}", [128, 512], F32) for i in range(2)]
                rSt = [sb(st, f"rSt{i}", [128, 512], F32) for i in range(2)]
                swt = [sb(st, f"swt{i}", [128, 512], F32) for i in range(2)]
                t1 = [sb(st, f"t1_{i}", [128, 512], F32) for i in range(2)]
                t2 = [sb(st, f"t2_{i}", [128, 512], F32) for i in range(2)]
                NOC = 4
                oc = [sb(st, f"oc{i}", [128, 512], BF16) for i in range(NOC)]
                vo = [sb(st, f"vo{i}", [128, FV], BF16) for i in range(2)]
                if not is_a:
                    sqb = [sb(st, f"sqb{i}", [128, 512], BF16) for i in range(2)]
                    lnt = [sb(st, f"lnt{i}", [128, 512], F32) for i in range(2)]
                    rsb = [sb(st, f"rsb{i}", [128, 512], F32) for i in range(2)]
                    qn = [sb(st, f"qn{i}", [128, 512], F32) for i in range(2)]
                    gqk = sb(st, "gqk", [128, 2], F32)
                tp = [ps(st, f"tp{i}", [128, 1024], BF16) for i in range(2)]
                pm = [ps(st, f"pmP{i}", [128, 512]) for i in range(4)]
                if not is_a:
                    pss = [ps(st, f"pss{i}", [128, 512]) for i in range(2)]

                wv = w_in[j].rearrange("(k p) n -> p k n", p=128)
                CW = 1024
                for k in range(8):
                    for c0 in range(0, NCOL, CW):
                        c1 = min(NCOL, c0 + CW)
                        R.dma("pool", lambda e, k=k, c0=c0, c1=c1: e.dma_start(out=wbf[:, k, c0:c1], in_=wv[:, k, c0:c1]),
                              writes=[("wbf", k, c0)])
                wkeys = [("wbf", k, c0) for k in range(8) for c0 in range(0, NCOL, CW)]
                if not is_a:
                    for i in range(2):
                        R.dma("sp", lambda e, i=i: e.dma_start(out=gqk[:, i:i + 1], in_=b_qk_g[j, i, :, :]),
                              writes=[("gqk", i)])

                cnt = {"xt": 0, "oc": 0, "pm": 0, "vo": 0, "tp": 0, "rp": 0, "b": 0}
                PS = int(os.environ.get('KDEBUG_PSTEP', '9'))
                for w in range(NW + 1 if PS >= 2 else 0):
                    T = 512 if w < NW else 128
                    nsub = T // 128
                    tok0 = w * 512 if w < NW else LAT
                    r = 0 if w < NW else 1
                    wi = w % 2
                    xts = []
                    for jj in range(nsub):
                        xi = cnt["xt"] % NXT
                        cnt["xt"] += 1
                        xts.append(xi)
                        R.dma("sp", lambda e, xi=xi, jj=jj, tok0=tok0: e.dma_start(
                            out=xt[xi][:], in_=src[tok0 + jj * 128: tok0 + (jj + 1) * 128, :]),
                            reads=[("xs", tok0 + jj * 128)], writes=[("xt", xi)])
                        R.act(lambda e, xi=xi, jj=jj, wi=wi: e.activation(
                            out=sqj[:], in_=xt[xi][:], func=AF.Square, accum_out=ssq[wi][:, jj:jj + 1]),
                            reads=[("xt", xi)], writes=["sqj", ("ssq", wi, jj)])
                    R.dma("sp", lambda e, wi=wi, tok0=tok0, T=T: e.dma_start(out=rCt[wi][:, 0:T], in_=rC[:, tok0:tok0 + T]),
                          writes=[("rCt", wi)])
                    R.dma("sp", lambda e, wi=wi, tok0=tok0, T=T: e.dma_start(out=rSt[wi][:, 0:T], in_=rS[:, tok0:tok0 + T]),
                          writes=[("rSt", wi)])
                    sskeys = [("ssq", wi, jj) for jj in range(nsub)]
                    R.act(lambda e, wi=wi, nsub=nsub: e.activation(
                        out=lnv[wi][:, 0:nsub], in_=ssq[wi][:, 0:nsub], func=AF.Ln, bias=epsb[:], scale=1.0 / D),
                        reads=sskeys + ["eps"], writes=[("lnv", wi)])
                    R.act(lambda e, wi=wi, nsub=nsub: e.activation(
                        out=rst[wi][:, 0:nsub], in_=lnv[wi][:, 0:nsub], func=AF.Exp, scale=-0.5),
                        reads=[("lnv", wi)], writes=[("rst", wi)])
                    for jj in range(nsub):
                        R.act(lambda e, xi=xts[jj], jj=jj, wi=wi: e.activation(
                            out=xn[jj][:], in_=xt[xi][:], func=AF.Copy, scale=rst[wi][:, jj:jj + 1]),
                            reads=[("xt", xts[jj]), ("rst", wi)], writes=[("xn", jj)])
                    if PS < 3:
                        continue
                    for kk in range(4):
                        ti = cnt["tp"] % 2
                        cnt["tp"] += 1

                        def trs(e, kk=kk, ti=ti, nsub=nsub):
                            ins = None
                            for half in range(2):
                                k = 2 * kk + half
                                for jj in range(nsub):
                                    ins = e.transpose(tp[ti][:, half * 512 + jj * 128: half * 512 + (jj + 1) * 128],
                                                      xn[jj][:, k * 128:(k + 1) * 128], ident[:])
                            return ins
                        R.pe(trs, reads=[("xn", jj) for jj in range(nsub)] + ["ident"], writes=[("tp", ti)])
                        for half in range(2):
                            k = 2 * kk + half
                            R.dve(lambda e, k=k, ti=ti, half=half, wi=wi, T=T, r=r: e.tensor_scalar(
                                out=hT[wi][:, k, 0:T], in0=tp[ti][:, half * 512: half * 512 + T],
                                scalar1=Acol[r][:, k:k + 1], scalar2=Scol[r][:, k:k + 1],
                                op0=ALU.mult, op1=ALU.add),
                                reads=[("tp", ti), ("Acol", r), ("Scol", r)], writes=[("hT", wi, k)])
                    hkeys = [("hT", wi, k) for k in range(8)]

                    if PS < 4:
                        continue
                    chunks = []
                    for c in range(NQC if PS >= 5 else 0):
                        chunks.append(("q", c, colq + c * 128))
                    for c in range(NKC if PS >= 5 else 0):
                        chunks.append(("k", c, colk + c * 128))
                    for c in range(8):
                        chunks.append(("z", c, colz + c * 128))

                    def stage1(ch):
                        kind, c, col0 = ch
                        bi = cnt["pm"] % 4
                        cnt["pm"] += 1

                        def mm(e, col0=col0, bi=bi, wi=wi, T=T):
                            ins = None
                            for k in range(8):
                                ins = e.matmul(pm[bi][:, 0:T], wbf[:, k, col0:col0 + 128], hT[wi][:, k, 0:T],
                                               start=(k == 0), stop=(k == 7))
                            return ins
                        R.pe(mm, reads=hkeys + wkeys, writes=[("pm", bi)])
                        return bi

                    def stage2(ch, bi):
                        kind, c, col0 = ch
                        oi = cnt["oc"] % NOC
                        cnt["oc"] += 1
                        if kind == "z":
                            R.act(lambda e, bi=bi, oi=oi, T=T: e.activation(out=oc[oi][:, 0:T], in_=pm[bi][:, 0:T], func=AF.Silu),
                                  reads=[("pm", bi)], writes=[("oc", oi)])
                            dst = zT[l]
                            dkey = ("zT", c, w)
                        else:
                            ri = cnt["rp"] % 2
                            cnt["rp"] += 1
                            if is_a:
                                srcap = pm[bi]
                                skey = ("pm", bi)
                            else:
                                b = cnt["b"] % 2
                                cnt["b"] += 1
                                gi = 0 if kind == "q" else 1
                                R.act(lambda e, bi=bi, b=b, T=T: e.activation(out=sqb[b][:, 0:T], in_=pm[bi][:, 0:T], func=AF.Square),
                                      reads=[("pm", bi)], writes=[("sqb", b)])
                                R.pe(lambda e, b=b, T=T: e.matmul(pss[b][:, 0:T], ones_bf[:], sqb[b][:, 0:T], start=True, stop=True),
                                     reads=[("sqb", b), "ones"], writes=[("pss", b)])
                                R.act(lambda e, b=b, T=T: e.activation(out=lnt[b][:, 0:T], in_=pss[b][:, 0:T], func=AF.Ln,
                                                                      bias=epsb[:], scale=1.0 / 128),
                                      reads=[("pss", b), "eps"], writes=[("lnt", b)])
                                R.act(lambda e, b=b, T=T: e.activation(out=rsb[b][:, 0:T], in_=lnt[b][:, 0:T], func=AF.Exp, scale=-0.5),
                                      reads=[("lnt", b)], writes=[("rsb", b)])
                                R.dve(lambda e, b=b, bi=bi, gi=gi, T=T: e.scalar_tensor_tensor(
                                    out=qn[b][:, 0:T], in0=pm[bi][:, 0:T], scalar=gqk[:, gi:gi + 1], in1=rsb[b][:, 0:T],
                                    op0=ALU.mult, op1=ALU.mult),
                                    reads=[("pm", bi), ("rsb", b), ("gqk", gi)], writes=[("qn", b)])
                                srcap = qn[b]
                                skey = ("qn", b)
                            R.dve(lambda e, srcap=srcap, ri=ri, T=T: e.tensor_copy(out=swt[ri][0:64, 0:T], in_=srcap[64:128, 0:T]),
                                  reads=[skey], writes=[("swt", ri, 0)])
                            R.dve(lambda e, srcap=srcap, ri=ri, T=T: e.tensor_copy(out=swt[ri][64:128, 0:T], in_=srcap[0:64, 0:T]),
                                  reads=[skey], writes=[("swt", ri, 1)])
                            R.dve(lambda e, srcap=srcap, ri=ri, wi=wi, T=T: e.tensor_tensor(
                                out=t1[ri][:, 0:T], in0=srcap[:, 0:T], in1=rCt[wi][:, 0:T], op=ALU.mult),
                                reads=[skey, ("rCt", wi)], writes=[("t1", ri)])
                            R.pool(lambda e, ri=ri, wi=wi, T=T: e.tensor_tensor(
                                out=t2[ri][:, 0:T], in0=swt[ri][:, 0:T], in1=rSt[wi][:, 0:T], op=ALU.mult),
                                reads=[("swt", ri, 0), ("swt", ri, 1), ("rSt", wi)], writes=[("t2", ri)])
                            R.pool(lambda e, ri=ri, oi=oi, T=T: e.tensor_tensor(
                                out=oc[oi][:, 0:T], in0=t1[ri][:, 0:T], in1=t2[ri][:, 0:T], op=ALU.add),
                                reads=[("t1", ri), ("t2", ri)], writes=[("oc", oi)])
                            dst = qT[l] if kind == "q" else kTb[l]
                            dkey = ("qT" if kind == "q" else "kTb", c, w)
                        R.dma("pool", lambda e, dst=dst, c=c, oi=oi, tok0=tok0, T=T: e.dma_start(
                            out=dst[c * 128:(c + 1) * 128, tok0:tok0 + T], in_=oc[oi][:, 0:T]),
                            reads=[("oc", oi)], writes=[dkey])

                    prev = None
                    for ch in chunks:
                        bi = stage1(ch)
                        if prev is not None:
                            stage2(*prev)
                        prev = (ch, bi)
                    vprev = None
                    for jj in range(nsub if PS >= 6 else 0):
                        vi = cnt["vo"] % 2
                        cnt["vo"] += 1
                        for c0 in range(0, FV, 512):
                            cw = min(512, FV - c0)
                            bi = cnt["pm"] % 4
                            cnt["pm"] += 1

                            def mmv(e, jj=jj, c0=c0, cw=cw, bi=bi, wi=wi):
                                ins = None
                                for k in range(8):
                                    ins = e.matmul(pm[bi][:, 0:cw], hT[wi][:, k, jj * 128:(jj + 1) * 128],
                                                   wbf[:, k, colv + c0: colv + c0 + cw], start=(k == 0), stop=(k == 7))
                                return ins
                            R.pe(mmv, reads=hkeys + wkeys, writes=[("pm", bi)])
                            if prev is not None:
                                stage2(*prev)
                                prev = None
                            if os.environ.get('KDEBUG_VCOPY', 'act') == 'dve':
                                R.dve(lambda e, vi=vi, c0=c0, cw=cw, bi=bi: e.tensor_copy(out=vo[vi][:, c0:c0 + cw], in_=pm[bi][:, 0:cw]),
                                      reads=[("pm", bi)], writes=[("vo", vi, c0)])
                            else:
                                R.act(lambda e, vi=vi, c0=c0, cw=cw, bi=bi: e.copy(out=vo[vi][:, c0:c0 + cw], in_=pm[bi][:, 0:cw]),
                                      reads=[("pm", bi)], writes=[("vo", vi, c0)])
                        R.dma("pool", lambda e, vi=vi, jj=jj, tok0=tok0: e.dma_start(
                            out=vb[l].rearrange("(h t) d -> t h d", h=NVH)[tok0 + jj * 128: tok0 + (jj + 1) * 128, :, :],
                            in_=vo[vi][:].rearrange("p (h d) -> p h d", h=NVH)),
                            reads=[("vo", vi, c0) for c0 in range(0, FV, 512)], writes=[("vb", w, jj)])
                    if prev is not None:
                        stage2(*prev)
                        prev = None
                R.flush()

            if not run_ph('X'):
                continue
            RG = [[2 * p, 2 * p + 1] for p in range(NCORES // 2)]
            for c in range(max(NKC, NVH)):
                if c < NKC:
                    R.op("pool", lambda e, c=c: e.collective_compute(
                        "AllGather", ALU.bypass, replica_groups=RG,
                        ins=[kTb[l][c * 128:(c + 1) * 128, :].opt()], outs=[kTg[l][c * 256:(c + 1) * 256, :].opt()]), kind="cc")
                if c < NVH:
                    R.op("pool", lambda e, c=c: e.collective_compute(
                        "AllGather", ALU.bypass, replica_groups=RG,
                        ins=[vb[l][c * NTOK:(c + 1) * NTOK, :].opt()], outs=[vg[l][c * 2 * NTOK:(c + 1) * 2 * NTOK, :].opt()]), kind="cc")
            R.flush()
            R.op("pool", lambda e: e.wait_ge(cc_sem, R.cc_n), kind="c")
            R.flush()

            if not run_ph('A'):
                continue
            with contextlib.ExitStack() as st:
                NU = 2 if is_a else 1
                kTs = [sb(st, f"kTs{i}", [128, 2 * NTOK], BF16) for i in range(2)]
                vs = [sb(st, f"vs{i}", [128, NKT, 128], BF16) for i in range(2)]
                qs = [[sb(st, f"qs{i}_{s}", [128, NTOK], BF16) for s in range(NU)] for i in range(2)]
                zs = [sb(st, f"zs{i}", [128, NTOK], BF16) for i in range(2)]
                NPT = 8
                pt = [sb(st, f"pt{i}", [128, 1024], BF16) for i in range(NPT)]
                accS = [sb(st, f"accS{i}", [128, 512], BF16) for i in range(2)]
                accP = [sb(st, f"accP{i}", [128, 1024], BF16) for i in range(2)]
                rr = [sb(st, f"rr{i}", [128, 512], F32) for i in range(2)]
                o_s = [sb(st, f"o_s{i}", [128, 512], F32) for i in range(2)]
                ocmb = sb(st, "ocmb", [128, 512], F32)
                sqe = sb(st, "sqe", [128, 512], BF16)
                sqo = [sb(st, f"sqo{i}", [128, 512], BF16) for i in range(2)]
                lno = sb(st, "lno", [128, 512], F32)
                rso = sb(st, "rso", [128, 512], F32)
                ono = sb(st, "ono", [128, 512], F32)
                ogs = [sb(st, f"ogs{i}", [128, 512], BF16) for i in range(2)]
                negM = [sb(st, f"negM{i}", [128, 1], F32) for i in range(4)]
                stt = sb(st, "stt", [128, 8], F32)
                stg = sb(st, "stg", [128, 64], F32)
                kmx = [sb(st, f"kmx{i}", [128, 2], F32) for i in range(2)]
                qmx = sb(st, "qmx", [128, 2], F32)
                if is_a:
                    lamt = sb(st, "lamt", [128, 256], F32)
                    lamj = sb(st, "lamj", [128, 64], F32)
                    lams = sb(st, "lams", [128, 4], F32)
                    neglam = sb(st, "neglam", [128, 1], F32)
                    gsub = sb(st, "gsub", [128, 1], F32)
                Sg = [ps(st, f"Sg{i}", [128, 1024]) for i in range(2)]
                Ob = [ps(st, f"Ob{i}", [128, 512]) for i in range(2)]
                Lb = ps(st, "Lb", [128, 512])
                accPS = ps(st, "accPS", [128, 512])
                pstat = [(Sg[0][:, 0:512], ("S", 0, 0)), (Sg[0][:, 512:1024], ("S", 0, 1)),
                         (Sg[1][:, 0:512], ("S", 1, 0)), (Sg[1][:, 512:1024], ("S", 1, 1))]

                scale = (64 ** -0.5) if is_a else (128 ** -0.5)
                if is_a:
                    lam_init = lambda_init_fn(l)
                    for i in range(2):
                        for s in range(2):
                            R.dve(lambda e, i=i, s=s: e.memset(qs[i][s][:], 0.0), writes=[("qs", i, s), ("qs2", i, s)])
                    R.dma("sp", lambda e: e.dma_start(out=lamt[:], in_=a_lambda[j, :, :]), writes=["lamt"])
                    R.dma("sp", lambda e: e.dma_start(out=gsub[:], in_=a_subln[j, :, :]), writes=["gsub0"])
                    for i in range(2):
                        R.dve(lambda e, i=i: e.scalar_tensor_tensor(
                            out=lamj[:], in0=lamt[:, (2 * i) * 64:(2 * i + 1) * 64], scalar=1.0,
                            in1=lamt[:, (2 * i + 1) * 64:(2 * i + 2) * 64], op0=ALU.mult, op1=ALU.mult,
                            accum_out=lams[:, i:i + 1]), reads=["lamt"], writes=["lamj", ("lams", i)])
                    R.act(lambda e: e.activation(out=lams[:, 2:4], in_=lams[:, 0:2], func=AF.Exp),
                          reads=[("lams", 0), ("lams", 1)], writes=["lame"])
                    R.dve(lambda e: e.scalar_tensor_tensor(out=neglam[:], in0=lams[:, 3:4], scalar=-lam_init,
                                                           in1=lams[:, 2:3], op0=ALU.add, op1=ALU.subtract),
                          reads=["lame"], writes=["neglam"])
                    R.dve(lambda e: e.tensor_scalar(out=gsub[:], in0=gsub[:], scalar1=(1.0 - lam_init), scalar2=None,
                                                    op0=ALU.mult), reads=["gsub0"], writes=["gsub"])

                FAST_RECIP = os.environ.get("KDEBUG_FASTRECIP", "0") == "1"
                pending = []
                state = {"unit": 0, "og": 0, "kvslot": -1, "kvhead": -1, "nm": 0, "sq": 0, "g": 0, "pb": 0}

                def drain(n=None, upto=None):
                    k = len(pending) if n is None else min(n, len(pending))
                    for _ in range(k):
                        if upto is not None and pending[0][0] > upto:
                            break
                        pending.pop(0)[1]()

                def defer(fn):
                    pending.append((state["unit"] - 1, fn))

                def flush_tail():
                    if state.get("tail") is not None:
                        t_ = state["tail"]
                        state["tail"] = None
                        t_()

                def maxsq(srct, ncols, rkeys, outs):
                    nch = 0
                    for c0 in range(0, ncols, 512):
                        cw = min(512, ncols - c0)
                        b = state["sq"] % 2
                        state["sq"] += 1
                        R.act(lambda e, b=b, c0=c0, cw=cw: e.activation(out=sqo[b][:, 0:cw], in_=srct[:, c0:c0 + cw], func=AF.Square),
                              reads=rkeys, writes=[("sqo", b)])
                        for oi_, (ind, dst, dkey) in enumerate(outs):
                            pb, pkey = pstat[state["pb"] % 4]
                            state["pb"] += 1
                            R.pe(lambda e, b=b, cw=cw, pb=pb, ind=ind: e.matmul(pb[:, 0:cw], ind, sqo[b][:, 0:cw], start=True, stop=True),
                                 reads=[("sqo", b)], writes=[pkey])
                            col = oi_ * 20 + nch
                            R.dve(lambda e, cw=cw, pb=pb, col=col: e.tensor_reduce(out=stg[:, col:col + 1], in_=pb[:, 0:cw], axis=AX.X, op=ALU.max),
                                  reads=[pkey], writes=[("stg", col)])
                        nch += 1
                    for oi_, (ind, dst, dkey) in enumerate(outs):
                        R.dve(lambda e, nch=nch, oi_=oi_, dst=dst: e.tensor_reduce(out=dst, in_=stg[:, oi_ * 20: oi_ * 20 + nch], axis=AX.X, op=ALU.max),
                              reads=[("stg", oi_ * 20 + i) for i in range(nch)], writes=[dkey])

                NH = 8
                for h in range(NH):
                    hs = h % 2
                    hk = h if is_a else h // 4
                    if hk != state["kvhead"]:
                        state["kvhead"] = hk
                        state["kvslot"] = (state["kvslot"] + 1) % 2
                        ks = state["kvslot"]
                        for rk in range(2):
                            R.dma("sp", lambda e, ks=ks, rk=rk, hk=hk: e.dma_start(
                                out=kTs[ks][:, rk * NTOK:(rk + 1) * NTOK],
                                in_=kTg[l][hk * 256 + rk * 128: hk * 256 + (rk + 1) * 128, :]),
                                writes=[("kTs", ks, rk)])
                            vgv = vg[l][hk * 2 * NTOK:(hk + 1) * 2 * NTOK, :].rearrange("(kt p) f -> p kt f", p=128)
                            for part in range(4):
                                k0 = rk * KT_R + (KT_R * part) // 4
                                k1 = rk * KT_R + (KT_R * (part + 1)) // 4
                                if k1 > k0:
                                    R.dma("sp", lambda e, ks=ks, k0=k0, k1=k1, vgv=vgv: e.dma_start(
                                        out=vs[ks][:, k0:k1, :], in_=vgv[:, k0:k1, :]),
                                        writes=[("vs", ks, rk, part)])
                        if cfg.stab:
                            maxsq(kTs[ks], 2 * NTOK, [("kTs", ks, 0), ("kTs", ks, 1)],
                                  [((indm[:, s, :] if is_a else ones_bf[:]), kmx[ks][:, s:s + 1], ("kmx", ks, s)) for s in range(NU)])
                    ks = state["kvslot"]
                    if is_a:
                        for s in range(2):
                            for half in range(2):
                                p0 = 64 * half + 32 * s
                                R.dma("sp", lambda e, hs=hs, s=s, p0=p0, h=h: e.dma_start(
                                    out=qs[hs][s][p0:p0 + 32, :], in_=qT[l][h * 128 + p0: h * 128 + p0 + 32, :]),
                                    writes=[("qs", hs, s)] if half == 0 else [("qs2", hs, s)])
                    else:
                        R.dma("sp", lambda e, hs=hs, h=h: e.dma_start(out=qs[hs][0][:], in_=qT[l][h * 128:(h + 1) * 128, :]),
                              writes=[("qs", hs, 0)])
                    R.dma("sp", lambda e, hs=hs, h=h: e.dma_start(out=zs[hs][:], in_=zT[l][h * 128:(h + 1) * 128, :]),
                          writes=[("zs", hs)])

                    nm = []
                    for s in range(NU):
                        mi = state["nm"] % 4
                        state["nm"] += 1
                        nm.append(mi)
                        if cfg.stab:
                            maxsq(qs[hs][s], NTOK, [("qs", hs, s), ("qs2", hs, s)], [(ones_bf[:], qmx[:, s:s + 1], ("qmx", s))])
                            R.dve(lambda e, ks=ks, s=s: e.tensor_tensor(out=stt[:, 3:4], in0=kmx[ks][:, s:s + 1], in1=qmx[:, s:s + 1], op=ALU.mult),
                                  reads=[("kmx", ks, s), ("qmx", s)], writes=[("stt", 3)])
                            R.act(lambda e: e.activation(out=stt[:, 4:5], in_=stt[:, 3:4], func=AF.Ln, bias=epsb[:], scale=1.0),
                                  reads=[("stt", 3), "eps"], writes=[("stt", 4)])
                            R.act(lambda e: e.activation(out=stt[:, 5:6], in_=stt[:, 4:5], func=AF.Exp, scale=0.5),
                                  reads=[("stt", 4)], writes=[("stt", 5)])
                            R.dve(lambda e, mi=mi: e.tensor_scalar(out=negM[mi][:], in0=stt[:, 5:6], scalar1=-scale, scalar2=None, op0=ALU.mult),
                                  reads=[("stt", 5)], writes=[("negM", mi)])
                        else:
                            R.dve(lambda e, mi=mi: e.memset(negM[mi][:], 0.0), writes=[("negM", mi)])

                    def vpart(kt):
                        rk = kt // KT_R
                        for pp in range(4):
                            k0 = rk * KT_R + (KT_R * pp) // 4
                            k1 = rk * KT_R + (KT_R * (pp + 1)) // 4
                            if k0 <= kt < k1:
                                return rk, pp
                        return rk, 3

                    nqt = NW + (0 if last else 1)
                    for w in range(nqt):
                        T = 512 if w < NW else 128
                        tok0 = w * 512 if w < NW else LAT
                        ktl = list(range(NKT)) if w < NW else [KT_R - 1, 2 * KT_R - 1]
                        groups = [ktl[i:i + 2] for i in range(0, len(ktl), 2)]
                        if w >= NW:
                            flush_tail()
                            drain()
                        for s in range(NU):
                            ob = state["unit"] % 2
                            ab = ob
                            state["unit"] += 1
                            qsk = [("qs", hs, s), ("qs2", hs, s)] if is_a else [("qs", hs, s)]
                            mi = nm[s]

                            def QK(gi, grp, s=s, T=T, tok0=tok0, ks=ks, hs=hs, qsk=qsk):
                                for a, kt in enumerate(grp):
                                    R.pe(lambda e, gi=gi, a=a, kt=kt: e.matmul(
                                        Sg[gi][:, a * 512: a * 512 + T], kTs[ks][:, kt * 128:(kt + 1) * 128],
                                        qs[hs][s][:, tok0:tok0 + T], start=True, stop=True),
                                        reads=[("kTs", ks, kt // KT_R)] + qsk, writes=[("S", gi, a)])
                            ng = len(groups)
                            drain(upto=state["unit"] - 3)
                            step = max(1, (ng - 1) // (len(pending) + 1))
                            used = {"d": False, "p": False}
                            QK(state["g"] % 2, groups[0])
                            flush_tail()
                            prevPV = None
                            for gidx, grp in enumerate(groups):
                                g = state["g"]
                                state["g"] += 1
                                gi = g % 2
                                pi = g % NPT
                                na = len(grp)
                                if gidx + 1 < ng:
                                    QK((g + 1) % 2, groups[gidx + 1])
                                S3 = Sg[gi][:].rearrange("p (a t) -> p a t", a=2)[:, 0:na, 0:T]
                                P3 = pt[pi][:].rearrange("p (a t) -> p a t", a=2)[:, 0:na, 0:T]
                                R.act(lambda e, S3=S3, P3=P3, mi=mi: e.activation(out=P3, in_=S3, func=AF.Exp, bias=negM[mi][:], scale=scale),
                                      reads=[("S", gi, a) for a in range(na)] + [("negM", mi)], writes=[("pt", pi)])
                                if prevPV is not None:
                                    prevPV()

                                def PV(grp=grp, gidx=gidx, na=na, pi=pi, ob=ob, T=T, ks=ks, ng=ng):
                                    for a, kt in enumerate(grp):
                                        rk, part = vpart(kt)
                                        first = (gidx == 0 and a == 0)
                                        lastmm = (gidx == ng - 1 and a == na - 1)
                                        R.pe(lambda e, a=a, kt=kt, first=first, lastmm=lastmm: e.matmul(
                                            Ob[ob][:, 0:T], vs[ks][:, kt, :], pt[pi][:, a * 512: a * 512 + T], start=first, stop=lastmm),
                                            reads=[("vs", ks, rk, part), ("pt", pi)], writes=[("O", ob)])
                                prevPV = PV
                                use_d = (gidx % 2 == 0)
                                if use_d:
                                    for a in range(na):
                                        src_ = pt[pi][:, a * 512: a * 512 + T]
                                        if not used["d"]:
                                            used["d"] = True
                                            R.dve(lambda e, src_=src_, T=T: e.tensor_copy(out=accPS[:, 0:T], in_=src_),
                                                  reads=[("pt", pi)], writes=["accPS"])
                                        else:
                                            R.dve(lambda e, src_=src_, T=T: e.tensor_tensor(out=accPS[:, 0:T], in0=accPS[:, 0:T], in1=src_, op=ALU.add),
                                                  reads=[("pt", pi), "accPS"], writes=["accPS"])
                                else:
                                    A3 = accP[ab][:].rearrange("p (a t) -> p a t", a=2)[:, 0:na, 0:T]
                                    akey = ("accP", ab)
                                    if not used["p"]:
                                        used["p"] = True
                                        R.pool(lambda e, A3=A3, P3=P3: e.tensor_copy(out=A3, in_=P3), reads=[("pt", pi)], writes=[akey])
                                    else:
                                        R.pool(lambda e, A3=A3, P3=P3: e.tensor_tensor(out=A3, in0=A3, in1=P3, op=ALU.add),
                                               reads=[("pt", pi), akey], writes=[akey])
                                if gidx >= 1 and gidx % step == 0:
                                    drain(1)
                            state["tail"] = prevPV
                            na0 = len(groups[0])
                            srcs = []
                            if used["d"]:
                                R.dve(lambda e, ab=ab, T=T: e.tensor_copy(out=accS[ab][:, 0:T], in_=accPS[:, 0:T]),
                                      reads=["accPS"], writes=[("accS", ab)])
                                srcs += [(accS[ab], ("accS", ab), 0)]
                            if used["p"]:
                                srcs += [(accP[ab], ("accP", ab), a) for a in range(na0)]

                            def Lstage(srcs=srcs, T=T):
                                def mmL(e):
                                    ins = None
                                    for i, (t_, k_, a) in enumerate(srcs):
                                        ins = e.matmul(Lb[:, 0:T], ones_bf[:], t_[:, a * 512: a * 512 + T],
                                                       start=(i == 0), stop=(i == len(srcs) - 1))
                                    return ins
                                R.pe(mmL, reads=list({k_ for (_, k_, _) in srcs}), writes=["Lb"])
                            defer(Lstage)
                            defer(lambda s=s, T=T: R.dve(
                                lambda e: (e.reciprocal_approx_fast(out=rr[s][:, 0:T], in_=Lb[:, 0:T]) if FAST_RECIP
                                           else e.reciprocal(out=rr[s][:, 0:T], in_=Lb[:, 0:T])),
                                reads=["Lb"], writes=[("rr", s)]))
                            defer(lambda ob=ob, s=s, T=T: R.dve(
                                lambda e: e.tensor_tensor(out=o_s[s][:, 0:T], in0=Ob[ob][:, 0:T], in1=rr[s][:, 0:T], op=ALU.mult),
                                reads=[("O", ob), ("rr", s)], writes=[("o_s", s)]))
                        oi = state["og"] % 2
                        state["og"] += 1
                        if is_a:
                            defer(lambda T=T: R.dve(
                                lambda e: e.scalar_tensor_tensor(out=ocmb[:, 0:T], in0=o_s[1][:, 0:T], scalar=neglam[:],
                                                                 in1=o_s[0][:, 0:T], op0=ALU.mult, op1=ALU.add),
                                reads=[("o_s", 0), ("o_s", 1), "neglam"], writes=["ocmb"]))
                            defer(lambda T=T: R.act(
                                lambda e: e.activation(out=sqe[:, 0:T], in_=ocmb[:, 0:T], func=AF.Square),
                                reads=["ocmb"], writes=["sqe"]))
                            defer(lambda T=T: R.pe(
                                lambda e: e.matmul(Lb[:, 0:T], ones_bf[:], sqe[:, 0:T], start=True, stop=True),
                                reads=["sqe"], writes=["Lb"]))
                            defer(lambda T=T: R.act(
                                lambda e: e.activation(out=lno[:, 0:T], in_=Lb[:, 0:T], func=AF.Ln, bias=epsb[:], scale=1.0 / 128),
                                reads=["Lb"], writes=["lno"]))
                            defer(lambda T=T: R.act(
                                lambda e: e.activation(out=rso[:, 0:T], in_=lno[:, 0:T], func=AF.Exp, scale=-0.5),
                                reads=["lno"], writes=["rso"]))
                            defer(lambda T=T: R.dve(
                                lambda e: e.scalar_tensor_tensor(out=ono[:, 0:T], in0=ocmb[:, 0:T], scalar=gsub[:],
                                                                 in1=rso[:, 0:T], op0=ALU.mult, op1=ALU.mult),
                                reads=["ocmb", "rso", "gsub"], writes=["ono"]))
                            fin_src, fin_key = ono, "ono"
                        else:
                            fin_src, fin_key = o_s[0], ("o_s", 0)
                        defer(lambda T=T, oi=oi, tok0=tok0, fin_src=fin_src, fin_key=fin_key, hs=hs: R.dve(
                            lambda e: e.tensor_tensor(out=ogs[oi][:, 0:T], in0=fin_src[:, 0:T], in1=zs[hs][:, tok0:tok0 + T], op=ALU.mult),
                            reads=[fin_key, ("zs", hs)], writes=[("ogs", oi)]))
                        defer(lambda T=T, oi=oi, tok0=tok0, h=h, w=w: R.dma(
                            "pool", lambda e: e.dma_start(out=ogT[l][h * 128:(h + 1) * 128, tok0:tok0 + T], in_=ogs[oi][:, 0:T]),
                            reads=[("ogs", oi)], writes=[("ogT", h, w)]))
                flush_tail()
                drain()
                R.flush()

            if not run_ph('O'):
                continue
            with contextlib.ExitStack() as st:
                wo = sb(st, "wo", [128, 8, D], BF16)
                og = [sb(st, f"og{i}", [128, 8, 512], BF16) for i in range(2)]
                NXO = 6
                xo = [sb(st, f"xo{i}", [128, D], F32) for i in range(NXO)]
                yt = [sb(st, f"yt{i}", [128, D], F32) for i in range(2)]
                xw = [sb(st, f"xw{i}", [128, D], F32) for i in range(3)]
                sqy = sb(st, "sqy", [128, 512], BF16)
                ssy = [sb(st, f"ssy{i}", [128, 4], F32) for i in range(2)]
                py = [ps(st, f"py{i}", [128, 512]) for i in range(4)]
                wov = w_out[l].rearrange("(k p) n -> p k n", p=128)
                for k in range(8):
                    R.dma("pool", lambda e, k=k: e.dma_start(out=wo[:, k, :], in_=wov[:, k, :]), writes=[("wo", k)])
                wokeys = [("wo", k) for k in range(8)]
                ogv = ogT[l].rearrange("(k p) t -> p k t", p=128)
                tcount = 0
                nwt = NW + (0 if last else 1)
                for w in range(nwt):
                    T = 512 if w < NW else 128
                    nsub = T // 128
                    tok0 = w * 512 if w < NW else LAT
                    r = 0 if w < NW else 1
                    wi = w % 2
                    R.dma("sp", lambda e, wi=wi, tok0=tok0, T=T: e.dma_start(out=og[wi][:, :, 0:T], in_=ogv[:, :, tok0:tok0 + T]),
                          writes=[("og", wi)])
                    for jj in range(nsub):
                        xi = tcount % NXO
                        yi = tcount % 2
                        xwi = tcount % 3
                        tcount += 1
                        t0 = tok0 + jj * 128
                        R.dma("sp", lambda e, xi=xi, t0=t0: e.dma_start(out=xo[xi][:], in_=src[t0:t0 + 128, :]), writes=[("xo", xi)])
                        for nh in range(2):
                            bi = (2 * yi + nh)

                            def mmo(e, nh=nh, jj=jj, wi=wi, bi=bi):
                                ins = None
                                for k in range(8):
                                    ins = e.matmul(py[bi][:], og[wi][:, k, jj * 128:(jj + 1) * 128], wo[:, k, nh * 512:(nh + 1) * 512],
                                                   start=(k == 0), stop=(k == 7))
                                return ins
                            R.pe(mmo, reads=[("og", wi)] + wokeys, writes=[("py", bi)])
                            R.act(lambda e, bi=bi, yi=yi, nh=nh: e.activation(out=sqy[:], in_=py[bi][:], func=AF.Square,
                                                                              accum_out=ssy[yi][:, nh:nh + 1]),
                                  reads=[("py", bi)], writes=["sqy", ("ssy", yi, nh)])
                        R.dve(lambda e, yi=yi: e.tensor_tensor(out=ssy[yi][:, 2:3], in0=ssy[yi][:, 0:1], in1=ssy[yi][:, 1:2], op=ALU.add),
                              reads=[("ssy", yi, 0), ("ssy", yi, 1)], writes=[("ssy", yi, 2)])
                        R.act(lambda e, yi=yi: e.activation(out=ssy[yi][:, 3:4], in_=ssy[yi][:, 2:3], func=AF.Ln, bias=epsb[:], scale=1.0 / D),
                              reads=[("ssy", yi, 2), "eps"], writes=[("ssy", yi, 3)])
                        R.act(lambda e, yi=yi: e.activation(out=ssy[yi][:, 2:3], in_=ssy[yi][:, 3:4], func=AF.Exp, scale=-0.5),
                              reads=[("ssy", yi, 3)], writes=[("ssy", yi, 4)])
                        for nh in range(2):
                            bi = (2 * yi + nh)
                            R.dve(lambda e, bi=bi, yi=yi, nh=nh, r=r: e.scalar_tensor_tensor(
                                out=yt[yi][:, nh * 512:(nh + 1) * 512], in0=py[bi][:], scalar=ssy[yi][:, 2:3],
                                in1=Gb[r][:, nh * 512:(nh + 1) * 512], op0=ALU.mult, op1=ALU.mult),
                                reads=[("py", bi), ("ssy", yi, 4), ("Gb", r)], writes=[("yt", yi, nh)])
                        R.dve(lambda e, yi=yi, xi=xi, xwi=xwi: e.tensor_tensor(out=xw[xwi][:], in0=yt[yi][:], in1=xo[xi][:], op=ALU.add),
                              reads=[("yt", yi, 0), ("yt", yi, 1), ("xo", xi)], writes=[("xw", xwi)])
                        if last:
                            R.dma("pool", lambda e, xwi=xwi, t0=t0: e.dma_start(out=out[t0:t0 + 128, :], in_=xw[xwi][:]),
                                  reads=[("xw", xwi)], writes=[("out", t0)])
                        else:
                            R.dma("pool", lambda e, xwi=xwi, t0=t0: e.dma_start(out=xs[t0:t0 + 128, :], in_=xw[xwi][:]),
                                  reads=[("xw", xwi)], writes=[("xs", t0)])
                R.flush()
    return nc


def _rope_tables(cfg, hf, head_dim, dup):
    LAT, NTOK = cfg.LAT, cfg.NTOK
    t = np.arange(hf * LAT, (hf + 1) * LAT)
    rows = (t // GRID_W).astype(np.float32)
    cols = (t % GRID_W).astype(np.float32)
    axis_dim = head_dim // 2
    freqs = (ROPE_THETA ** (-np.arange(0, axis_dim, 2, dtype=np.float32) / np.float32(axis_dim))).astype(np.float32)
    ang = np.concatenate([rows[:, None] * freqs, cols[:, None] * freqs], axis=-1).astype(np.float32)
    cos = np.cos(ang).astype(np.float32).T
    sin = np.sin(ang).astype(np.float32).T
    half = head_dim // 2
    C = np.ones((128, NTOK), np.float32)
    S = np.zeros((128, NTOK), np.float32)
    if dup:
        for blk in range(4):
            C[blk * 32:(blk + 1) * 32, :LAT] = cos
            S[blk * 32:(blk + 1) * 32, :LAT] = -sin if blk < 2 else sin
    else:
        C[0:64, :LAT] = cos
        C[64:128, :LAT] = cos
        S[0:64, :LAT] = -sin
        S[64:128, :LAT] = sin
    return C, S


def _perm_a_cols():
    perm = np.arange(4096)
    p128 = np.zeros(128, np.int64)
    for n in range(128):
        blk = n // 32
        s = blk % 2
        d = (n % 32) + (32 if blk >= 2 else 0)
        p128[n] = s * 64 + d
    for base in (0, 1024):
        for h in range(8):
            perm[base + h * 128: base + (h + 1) * 128] = base + h * 128 + p128
    return perm


def make_in_maps(cfg, x, c, ctx, c_ctx, ada_w, ada_b, pre_g, post_g, w_out, a_w_in, a_lambda, a_subln_g, b_w_in, b_qk_g):
    DEPTH = cfg.DEPTH
    f = lambda a: np.ascontiguousarray(np.asarray(a, dtype=np.float32))
    x, c, ctx, c_ctx = f(x), f(c), f(ctx), f(c_ctx)
    NA = (DEPTH + 1) // 2
    NB = max(DEPTH // 2, 1)
    shared = {
        "ada_w": f(ada_w)[:DEPTH],
        "ada_b": np.ascontiguousarray(np.broadcast_to(f(ada_b)[:DEPTH, None, :], (DEPTH, 128, 3 * D))),
        "pre_g": np.ascontiguousarray(np.broadcast_to(f(pre_g)[:DEPTH, None, :], (DEPTH, 128, D))),
        "post_g": np.ascontiguousarray(np.broadcast_to(f(post_g)[:DEPTH, None, :], (DEPTH, 128, D))),
        "w_out": f(w_out)[:DEPTH],
        "a_w_in": np.ascontiguousarray(f(a_w_in)[:NA][:, :, _perm_a_cols()]),
        "a_lambda": np.ascontiguousarray(np.broadcast_to(f(a_lambda)[:NA].reshape(NA, 1, 256), (NA, 128, 256))),
        "a_subln": np.ascontiguousarray(f(a_subln_g)[:NA].reshape(NA, 128, 1)),
        "b_w_in": f(b_w_in)[:NB],
        "b_qk_g": np.ascontiguousarray(f(b_qk_g)[:NB].reshape(NB, 2, 128, 1)),
        "ident": np.eye(128, dtype=np.float32).astype(ml_dtypes.bfloat16),
    }
    ind = np.zeros((2, 128, 128), np.float32)
    for s in range(2):
        for p in range(128):
            if (p // 32) % 2 == s:
                ind[s, p, :] = 1.0
    shared["indmat"] = ind.astype(ml_dtypes.bfloat16)
    maps = []
    for i in range(NCORES):
        b, hf = i // 2, i % 2
        LAT = cfg.LAT
        xin = np.concatenate([x[b, hf * LAT:(hf + 1) * LAT], ctx[b, hf * CTXH:(hf + 1) * CTXH]], axis=0)
        cin = np.stack([c[b].reshape(8, 128).T, c_ctx.reshape(8, 128).T], axis=1)
        ac, as_ = _rope_tables(cfg, hf, 64, True)
        bc, bs = _rope_tables(cfg, hf, 128, False)
        m = dict(shared)
        m.update({"xin": np.ascontiguousarray(xin), "cin": np.ascontiguousarray(cin),
                  "ropeA_C": ac, "ropeA_S": as_, "ropeB_C": bc, "ropeB_S": bs})
        maps.append(m)
    return maps


_CACHE = {}


def run(cfg, inputs, trace=False):
    key = (cfg.SEQ, cfg.DEPTH, cfg.debug, cfg.stab, cfg.stop)
    if key not in _CACHE:
        _CACHE[key] = build_program(cfg)
    nc = _CACHE[key]
    maps = make_in_maps(cfg, **inputs)
    res = run_bass_kernel_spmd(nc, maps, core_ids=list(range(NCORES)))
    return res


def kernel(x, c, ctx, c_ctx, ada_w, ada_b, pre_g, post_g, w_out, a_w_in, a_lambda, a_subln_g, b_w_in, b_qk_g):
    x = np.asarray(x)
    B, S, _ = x.shape
    cfg = Cfg(S, 4)
    res = run(cfg, dict(x=x, c=c, ctx=ctx, c_ctx=c_ctx, ada_w=ada_w, ada_b=ada_b, pre_g=pre_g, post_g=post_g,
                        w_out=w_out, a_w_in=a_w_in, a_lambda=a_lambda, a_subln_g=a_subln_g, b_w_in=b_w_in, b_qk_g=b_qk_g))
    outp = np.empty((B, S, D), np.float32)
    for i in range(NCORES):
        b, hf = i // 2, i % 2
        outp[b, hf * cfg.LAT:(hf + 1) * cfg.LAT] = np.asarray(res.results[i]["out"], dtype=np.float32)
    return outp
```

```python
import math
import numpy as np
import ml_dtypes
import concourse.bass as bass
import concourse.mybir as mybir
from concourse.bass_utils import run_bass_kernel_spmd

F32 = mybir.dt.float32
BF16 = mybir.dt.bfloat16
AF = mybir.ActivationFunctionType
ALU = mybir.AluOpType
AX = mybir.AxisListType

D = 1024
NCORES = 8
CTXH = 128
EPS = 1e-6
ROPE_THETA = 10000.0
GRID_W = 64


def lambda_init_fn(i):
    return 0.8 - 0.6 * math.exp(-0.3 * i)


class Op:
    __slots__ = ("eng", "fn", "deps", "kind", "signal", "val", "sem", "prewait")

    def __init__(self, eng, fn, deps, kind):
        self.eng = eng
        self.fn = fn
        self.deps = deps
        self.kind = kind
        self.signal = kind != "c"
        self.val = None
        self.sem = None
        self.prewait = None


class Rec:
    ENGS = ("pe", "act", "dve", "pool", "sp")
    NPOOL = 8

    def __init__(self, nc, sems, dma_sems, cc_sem):
        self.nc = nc
        self.sems = sems
        self.dma_sems = dma_sems
        self.cc_sem = cc_sem
        self.cnt = {e: 0 for e in self.ENGS}
        self.dma_n = {e: 0 for e in self.ENGS}
        self.cc_n = 0
        self.ops = []
        self.last_w = {}
        self.readers = {}
        self.nops = 0

    def op(self, eng, fn, reads=(), writes=(), kind="c"):
        deps = set()
        for k in reads:
            w = self.last_w.get(k)
            if w is not None:
                deps.add(w)
        for k in writes:
            w = self.last_w.get(k)
            if w is not None:
                deps.add(w)
            for r in self.readers.get(k, ()):
                deps.add(r)
        o = Op(eng, fn, deps, kind)
        for k in reads:
            self.readers.setdefault(k, []).append(o)
        for k in writes:
            self.last_w[k] = o
            self.readers[k] = []
        self.ops.append(o)
        return o

    def pe(self, fn, reads=(), writes=()):
        return self.op("pe", fn, reads, writes)

    def act(self, fn, reads=(), writes=()):
        return self.op("act", fn, reads, writes)

    def dve(self, fn, reads=(), writes=()):
        return self.op("dve", fn, reads, writes)

    def pool(self, fn, reads=(), writes=()):
        return self.op("pool", fn, reads, writes)

    def dma(self, eng, fn, reads=(), writes=()):
        return self.op(eng, fn, reads, writes, kind="d")

    def flush(self):
        nc = self.nc
        ops = self.ops
        live = set(id(o) for o in ops)
        for o in ops:
            nd = set()
            for d in o.deps:
                if id(d) not in live:
                    continue
                if d.eng == "pe" and o.eng == "pe" and d.kind == "c":
                    continue
                nd.add(d)
                d.signal = True
            o.deps = nd
        per = {e: [] for e in self.ENGS}
        for o in ops:
            per[o.eng].append(o)
            if o.kind == "d":
                n = self.dma_n[o.eng]
                self.dma_n[o.eng] = n + 1
                o.sem = self.dma_sems[o.eng][n % self.NPOOL]
                o.val = 16 * (n // self.NPOOL + 1)
                if n >= self.NPOOL:
                    o.prewait = (o.sem, 16 * (n // self.NPOOL))
            elif o.kind == "cc":
                self.cc_n += 1
                o.sem = self.cc_sem
                o.val = self.cc_n
            elif o.signal:
                self.cnt[o.eng] += 1
                o.sem = self.sems[o.eng]
                o.val = self.cnt[o.eng]
        dma_final = {}
        for e in self.ENGS:
            n = self.dma_n[e]
            fin = []
            for i in range(min(n, self.NPOOL)):
                last = ((n - 1 - i) // self.NPOOL) * self.NPOOL + i
                fin.append((self.dma_sems[e][i], 16 * (last // self.NPOOL + 1)))
            dma_final[e] = fin
        self.nops += len(ops)

        def emit(e_ops, eng_name):
            def body(e):
                waited = {}
                for o in e_ops:
                    ws = []
                    if o.prewait is not None:
                        ws.append(o.prewait)
                    for d in o.deps:
                        ws.append((d.sem, d.val))
                    for (s, v) in ws:
                        key = id(s)
                        if waited.get(key, 0) >= v:
                            continue
                        waited[key] = v
                        e.wait_ge(s, v)
                    ins = o.fn(e)
                    if o.kind == "d":
                        ins.then_inc(o.sem, 16)
                    elif o.kind == "cc":
                        ins.then_inc(o.sem)
                    elif o.signal:
                        ins.then_inc(o.sem, 1)
                for (s, v) in dma_final[eng_name]:
                    if waited.get(id(s), 0) < v:
                        e.wait_ge(s, v)
            return body

        with nc.Block() as block:
            reg = {"pe": block.tensor, "act": block.scalar, "dve": block.vector,
                   "pool": block.gpsimd, "sp": block.sync}
            for e in self.ENGS:
                if per[e] or dma_final[e]:
                    reg[e](emit(per[e], e))
        self.ops = []
        self.last_w = {}
        self.readers = {}


class Cfg:
    def __init__(self, seq, depth, debug=False, stab=True, stop=None):
        self.stop = stop
        self.SEQ = seq
        self.DEPTH = depth
        self.LAT = seq // 2
        self.NTOK = self.LAT + CTXH
        self.NW = self.LAT // 512
        self.KT_R = self.NTOK // 128
        self.NKT = 2 * self.KT_R
        self.debug = debug
        self.stab = stab


def layer_dims(l):
    if l % 2 == 0:
        return True, 1024, 1024, 1024, 4096, 0, 1024, 2048, 3072
    return False, 1024, 256, 256, 2560, 0, 1024, 1280, 1536


def build_program(cfg):
    nc = bass.Bass("TRN2", target_bir_lowering=False)
    NTOK, LAT, NW, KT_R, NKT, DEPTH = cfg.NTOK, cfg.LAT, cfg.NW, cfg.KT_R, cfg.NKT, cfg.DEPTH
    dbg_kind = "ExternalOutput" if cfg.debug else "Internal"

    def din(name, shape, dt=F32):
        return nc.dram_tensor(name, list(shape), dt, kind="ExternalInput")

    xin = din("xin", [NTOK, D])
    cin = din("cin", [128, 2, 8])
    ada_w = din("ada_w", [DEPTH, D, 3 * D])
    ada_b = din("ada_b", [DEPTH, 128, 3 * D])
    pre_g = din("pre_g", [DEPTH, 128, D])
    post_g = din("post_g", [DEPTH, 128, D])
    w_out = din("w_out", [DEPTH, D, D])
    NA = (DEPTH + 1) // 2
    NB = max(DEPTH // 2, 1)
    a_w_in = din("a_w_in", [NA, D, 4096])
    a_lambda = din("a_lambda", [NA, 128, 256])
    a_subln = din("a_subln", [NA, 128, 1])
    b_w_in = din("b_w_in", [NB, D, 2560])
    b_qk_g = din("b_qk_g", [NB, 2, 128, 1])
    ropeC = [din("ropeA_C", [128, NTOK]), din("ropeB_C", [128, NTOK])]
    ropeS = [din("ropeA_S", [128, NTOK]), din("ropeB_S", [128, NTOK])]
    ident_in = din("ident", [128, 128], BF16)
    ind_in = din("indmat", [2, 128, 128], BF16)
    out = nc.dram_tensor("out", [LAT, D], F32, kind="ExternalOutput")

    xs = nc.dram_tensor("xs", [NTOK, D], F32, kind=dbg_kind)
    qT, zT, ogT, kTb, kTg, vb, vg = [], [], [], [], [], [], []
    for l in range(DEPTH):
        is_a, FQ, FK, FV, *_ = layer_dims(l)
        qT.append(nc.dram_tensor(f"qT{l}", [FQ, NTOK], BF16, kind=dbg_kind))
        zT.append(nc.dram_tensor(f"zT{l}", [D, NTOK], BF16, kind=dbg_kind))
        ogT.append(nc.dram_tensor(f"ogT{l}", [D, NTOK], BF16, kind=dbg_kind))
        kTb.append(nc.dram_tensor(f"kTb{l}", [FK, NTOK], BF16))
        kTg.append(nc.dram_tensor(f"kTg{l}", [2 * FK, NTOK], BF16))
        vb.append(nc.dram_tensor(f"vb{l}", [(FV // 128) * NTOK, 128], BF16))
        vg.append(nc.dram_tensor(f"vg{l}", [(FV // 128) * 2 * NTOK, 128], BF16))

    import contextlib
    es = contextlib.ExitStack()
    with es:
        def sem(name):
            return es.enter_context(nc.semaphore(name))

        sems = {e: sem(f"s_{e}") for e in Rec.ENGS}
        dma_sems = {e: [sem(f"d_{e}{i}") for i in range(Rec.NPOOL)] for e in ("sp", "pool", "act")}
        dma_sems["pe"] = dma_sems["dve"] = []
        cc_sem = sem("cc")
        R = Rec(nc, sems, dma_sems, cc_sem)

        l_tag = ["g"]

        def sb(st, name, shape, dt):
            return st.enter_context(nc.sbuf_tensor(f"sb{l_tag[0]}_{name}", list(shape), dt))

        def ps(st, name, shape, dt=F32):
            return st.enter_context(nc.psum_tensor(f"ps{l_tag[0]}_{name}", list(shape), dt))

        ident = sb(es, "ident", [128, 128], BF16)
        identf = sb(es, "identf", [128, 128], F32)
        ones_bf = sb(es, "ones_bf", [128, 128], BF16)
        onesf = sb(es, "onesf", [128, 128], F32)
        indm = sb(es, "indm", [128, 2, 128], BF16)
        epsb = sb(es, "epsb", [128, 1], F32)
        cint = sb(es, "cint", [128, 2, 8], F32)
        scs = sb(es, "scs", [128, 2, 8], F32)
        scb = sb(es, "scb", [128, 2, 8, 128], F32)
        Gb = [sb(es, f"Gb{r}", [128, D], F32) for r in range(2)]
        Acol = [sb(es, f"Acol{r}", [128, 8], F32) for r in range(2)]
        Scol = [sb(es, f"Scol{r}", [128, 8], F32) for r in range(2)]

        R.dma("sp", lambda e: e.dma_start(out=ident[:], in_=ident_in[:, :]), writes=["ident"])
        R.dma("sp", lambda e: e.dma_start(out=indm[:], in_=ind_in.ap().rearrange("s p m -> p s m")),
              writes=["indm"])
        R.dma("sp", lambda e: e.dma_start(out=cint[:], in_=cin[:, :, :]), writes=["cint"])
        R.dve(lambda e: e.tensor_copy(out=identf[:], in_=ident[:]), reads=["ident"], writes=["identf"])
        R.dve(lambda e: e.memset(ones_bf[:], 1.0), writes=["ones"])
        R.dve(lambda e: e.memset(onesf[:], 1.0), writes=["onesf"])
        R.dve(lambda e: e.memset(epsb[:], EPS), writes=["eps"])
        import os
        ZS = int(os.environ.get('KDEBUG_ZSTEP', '9'))
        if ZS >= 2:
            R.act(lambda e: e.activation(out=scs[:], in_=cint[:], func=AF.Silu), reads=["cint"], writes=["scs"])
        for r in range(2 if ZS >= 3 else 0):
            for k in range(8):
                R.dve(lambda e, r=r, k=k: e.tensor_scalar(
                    out=scb[:, r, k, :], in0=onesf[:], scalar1=scs[:, r, k:k + 1], scalar2=None,
                    op0=ALU.mult), reads=["onesf", "scs"], writes=[("scb", r, k)])
        R.flush()

        for l in range(DEPTH):
            if cfg.stop == 'Z':
                break
            is_a, FQ, FK, FV, NCOL, colq, colk, colv, colz = layer_dims(l)
            l_tag[0] = str(l)
            j = l // 2
            last = l == DEPTH - 1
            w_in = a_w_in if is_a else b_w_in
            rC, rS = (ropeC[0], ropeS[0]) if is_a else (ropeC[1], ropeS[1])
            src = xin if l == 0 else xs
            NQC = FQ // 128
            NKC = FK // 128
            NVH = FV // 128

            PH = ['M', 'P', 'X', 'A', 'O']
            run_ph = lambda t: cfg.stop is None or PH.index(t) <= PH.index(cfg.stop)
            with contextlib.ExitStack() as st:
                awt = [sb(st, f"awt{i}", [128, 8, 512], F32) for i in range(2)]
                modt = [sb(st, f"modt{r}", [128, 3 * D], F32) for r in range(2)]
                adab = sb(st, "adab", [128, 3 * D], F32)
                pgb = sb(st, "pgb", [128, D], F32)
                qgb = sb(st, "qgb", [128, D], F32)
                tmpA = sb(st, "tmpA", [128, D], F32)
                junk = sb(st, "junkM", [128, 128], F32)
                pm = [ps(st, f"pmM{i}", [128, 512]) for i in range(4)]
                R.dma("sp", lambda e: e.dma_start(out=adab[:], in_=ada_b[l, :, :]), writes=["adab"])
                R.dma("sp", lambda e: e.dma_start(out=pgb[:], in_=pre_g[l, :, :]), writes=["pgb"])
                R.dma("sp", lambda e: e.dma_start(out=qgb[:], in_=post_g[l, :, :]), writes=["qgb"])
                awv = ada_w[l].rearrange("(k p) n -> p k n", p=128)
                for c in range(6):
                    R.dma("sp", lambda e, c=c: e.dma_start(out=awt[c % 2][:], in_=awv[:, :, c * 512:(c + 1) * 512]),
                          writes=[("awt", c % 2)])
                    for r in range(2):
                        bank = pm[(2 * c + r) % 4]

                        def mm(e, c=c, r=r, bank=bank):
                            ins = None
                            for k in range(8):
                                ins = e.matmul(bank[:], scb[:, r, k, :], awt[c % 2][:, k, :],
                                               start=(k == 0), stop=(k == 7))
                            return ins
                        R.pe(mm, reads=[("awt", c % 2)], writes=[("pmM", (2 * c + r) % 4)])
                        R.dve(lambda e, c=c, r=r, bank=bank: e.tensor_tensor(
                            out=modt[r][:, c * 512:(c + 1) * 512], in0=bank[:],
                            in1=adab[:, c * 512:(c + 1) * 512], op=ALU.add),
                            reads=[("pmM", (2 * c + r) % 4), "adab"], writes=[("modt", r, c)])
                import os
                MS = int(os.environ.get('KDEBUG_MSTEP', '9'))
                for r in range(2 if MS >= 2 else 0):
                    allmod = [("modt", r, c) for c in range(6)]
                    R.dve(lambda e, r=r: e.scalar_tensor_tensor(
                        out=tmpA[:], in0=modt[r][:, D:2 * D], scalar=1.0, in1=pgb[:],
                        op0=ALU.add, op1=ALU.mult), reads=allmod + ["pgb"], writes=["tmpA"])
                    for k in range(8):
                        R.dve(lambda e, r=r, k=k: e.scalar_tensor_tensor(
                            out=junk[:], in0=tmpA[:, k * 128:(k + 1) * 128], scalar=1.0, in1=identf[:],
                            op0=ALU.mult, op1=ALU.mult, accum_out=Acol[r][:, k:k + 1]),
                            reads=["tmpA"], writes=["junkM", ("Acol", r)])
                    for k in range(8):
                        R.dve(lambda e, r=r, k=k: e.scalar_tensor_tensor(
                            out=junk[:], in0=modt[r][:, k * 128:(k + 1) * 128], scalar=1.0, in1=identf[:],
                            op0=ALU.mult, op1=ALU.mult, accum_out=Scol[r][:, k:k + 1]),
                            reads=allmod, writes=["junkM", ("Scol", r)])
                    R.dve(lambda e, r=r: e.tensor_tensor(out=Gb[r][:], in0=modt[r][:, 2 * D:3 * D], in1=qgb[:],
                                                         op=ALU.mult), reads=allmod + ["qgb"], writes=[("Gb", r)])
                R.flush()

            if not run_ph('P'):
                continue
            with contextlib.ExitStack() as st:
                wbf = sb(st, "wbf", [128, 8, NCOL], BF16)
                NXT = 8
                xt = [sb(st, f"xt{i}", [128, D], F32) for i in range(NXT)]
                xn = [sb(st, f"xn{i}", [128, D], BF16) for i in range(4)]
                sqj = sb(st, "sqj", [128, D], BF16)
                ssq = [sb(st, f"ssq{i}", [128, 4], F32) for i in range(2)]
                lnv = [sb(st, f"lnv{i}", [128, 4], F32) for i in range(2)]
                rst = [sb(st, f"rst{i}", [128, 4], F32) for i in range(2)]
                hT = [sb(st, f"hT{i}", [128, 8, 512], BF16) for i in range(2)]
                rCt = [sb(st, f"rCt{i}", [128, 512], F32) for i in range(2)]
                rSt = [sb(st, f"rSt{i}", [128, 512], F32) for i in range(2)]
                swt = [sb(st, f"swt{i}", [128, 512], F32) for i in range(2)]
                t1 = [sb(st, f"t1_{i}", [128, 512], F32) for i in range(2)]
                t2 = [sb(st, f"t2_{i}", [128, 512], F32) for i in range(2)]
                NOC = 4
                oc = [sb(st, f"oc{i}", [128, 512], BF16) for i in range(NOC)]
                vo = [sb(st, f"vo{i}", [128, FV], BF16) for i in range(2)]
                if not is_a:
                    sqb = [sb(st, f"sqb{i}", [128, 512], BF16) for i in range(2)]
                    lnt = [sb(st, f"lnt{i}", [128, 512], F32) for i in range(2)]
                    rsb = [sb(st, f"rsb{i}", [128, 512], F32) for i in range(2)]
                    qn = [sb(st, f"qn{i}", [128, 512], F32) for i in range(2)]
                    gqk = sb(st, "gqk", [128, 2], F32)
                tp = [ps(st, f"tp{i}", [128, 1024], BF16) for i in range(2)]
                pm = [ps(st, f"pmP{i}", [128, 512]) for i in range(4)]
                if not is_a:
                    pss = [ps(st, f"pss{i}", [128, 512]) for i in range(2)]

                wv = w_in[j].rearrange("(k p) n -> p k n", p=128)
                CW = 1024
                for k in range(8):
                    for c0 in range(0, NCOL, CW):
                        c1 = min(NCOL, c0 + CW)
                        R.dma("pool", lambda e, k=k, c0=c0, c1=c1: e.dma_start(out=wbf[:, k, c0:c1], in_=wv[:, k, c0:c1]),
                              writes=[("wbf", k, c0)])
                wkeys = [("wbf", k, c0) for k in range(8) for c0 in range(0, NCOL, CW)]
                if not is_a:
                    for i in range(2):
                        R.dma("sp", lambda e, i=i: e.dma_start(out=gqk[:, i:i + 1], in_=b_qk_g[j, i, :, :]),
                              writes=[("gqk", i)])

                cnt = {"xt": 0, "oc": 0, "pm": 0, "vo": 0, "tp": 0, "rp": 0, "b": 0}
                PS = int(os.environ.get('KDEBUG_PSTEP', '9'))
                xts_by_w = {}

                def issue_loads(w_):
                    T_ = 512 if w_ < NW else 128
                    tok0_ = w_ * 512 if w_ < NW else LAT
                    wi_ = w_ % 2
                    lst = []
                    for jj in range(T_ // 128):
                        xi = cnt["xt"] % NXT
                        cnt["xt"] += 1
                        lst.append(xi)
                        R.dma("sp", lambda e, xi=xi, jj=jj, tok0_=tok0_: e.dma_start(
                            out=xt[xi][:], in_=src[tok0_ + jj * 128: tok0_ + (jj + 1) * 128, :]),
                            reads=[("xs", tok0_ + jj * 128)], writes=[("xt", xi)])
                    xts_by_w[w_] = lst
                    R.dma("sp", lambda e, wi_=wi_, tok0_=tok0_, T_=T_: e.dma_start(out=rCt[wi_][:, 0:T_], in_=rC[:, tok0_:tok0_ + T_]),
                          writes=[("rCt", wi_)])
                    R.dma("sp", lambda e, wi_=wi_, tok0_=tok0_, T_=T_: e.dma_start(out=rSt[wi_][:, 0:T_], in_=rS[:, tok0_:tok0_ + T_]),
                          writes=[("rSt", wi_)])
                for w in range(NW + 1 if PS >= 2 else 0):
                    T = 512 if w < NW else 128
                    nsub = T // 128
                    tok0 = w * 512 if w < NW else LAT
                    r = 0 if w < NW else 1
                    wi = w % 2
                    if w == 0:
                        issue_loads(0)
                    if w + 1 <= NW:
                        issue_loads(w + 1)
                    xts = xts_by_w[w]
                    for jj in range(nsub):
                        xi = xts[jj]
                        R.act(lambda e, xi=xi, jj=jj, wi=wi: e.activation(
                            out=sqj[:], in_=xt[xi][:], func=AF.Square, accum_out=ssq[wi][:, jj:jj + 1]),
                            reads=[("xt", xi)], writes=["sqj", ("ssq", wi, jj)])
                    sskeys = [("ssq", wi, jj) for jj in range(nsub)]
                    R.act(lambda e, wi=wi, nsub=nsub: e.activation(
                        out=lnv[wi][:, 0:nsub], in_=ssq[wi][:, 0:nsub], func=AF.Ln, bias=epsb[:], scale=1.0 / D),
                        reads=sskeys + ["eps"], writes=[("lnv", wi)])
                    R.act(lambda e, wi=wi, nsub=nsub: e.activation(
                        out=rst[wi][:, 0:nsub], in_=lnv[wi][:, 0:nsub], func=AF.Exp, scale=-0.5),
                        reads=[("lnv", wi)], writes=[("rst", wi)])
                    for jj in range(nsub):
                        R.act(lambda e, xi=xts[jj], jj=jj, wi=wi: e.activation(
                            out=xn[jj][:], in_=xt[xi][:], func=AF.Copy, scale=rst[wi][:, jj:jj + 1]),
                            reads=[("xt", xts[jj]), ("rst", wi)], writes=[("xn", jj)])
                    if PS < 3:
                        continue
                    for kk in range(4):
                        ti = cnt["tp"] % 2
                        cnt["tp"] += 1

                        def trs(e, kk=kk, ti=ti, nsub=nsub):
                            ins = None
                            for half in range(2):
                                k = 2 * kk + half
                                for jj in range(nsub):
                                    ins = e.transpose(tp[ti][:, half * 512 + jj * 128: half * 512 + (jj + 1) * 128],
                                                      xn[jj][:, k * 128:(k + 1) * 128], ident[:])
                            return ins
                        R.pe(trs, reads=[("xn", jj) for jj in range(nsub)] + ["ident"], writes=[("tp", ti)])
                        for half in range(2):
                            k = 2 * kk + half
                            R.dve(lambda e, k=k, ti=ti, half=half, wi=wi, T=T, r=r: e.tensor_scalar(
                                out=hT[wi][:, k, 0:T], in0=tp[ti][:, half * 512: half * 512 + T],
                                scalar1=Acol[r][:, k:k + 1], scalar2=Scol[r][:, k:k + 1],
                                op0=ALU.mult, op1=ALU.add),
                                reads=[("tp", ti), ("Acol", r), ("Scol", r)], writes=[("hT", wi, k)])
                    hkeys = [("hT", wi, k) for k in range(8)]

                    if PS < 4:
                        continue
                    chunks = []
                    qk = [("q", c, colq + c * 128) for c in range(NQC)] + [("k", c, colk + c * 128) for c in range(NKC)]
                    zc = [("z", c, colz + c * 128) for c in range(8)]
                    per = max(1, len(qk) // len(zc))
                    while qk or zc:
                        for _ in range(per):
                            if qk:
                                chunks.append(qk.pop(0))
                        if zc:
                            chunks.append(zc.pop(0))

                    def stage1(ch):
                        kind, c, col0 = ch
                        bi = cnt["pm"] % 4
                        cnt["pm"] += 1

                        def mm(e, col0=col0, bi=bi, wi=wi, T=T):
                            ins = None
                            for k in range(8):
                                ins = e.matmul(pm[bi][:, 0:T], wbf[:, k, col0:col0 + 128], hT[wi][:, k, 0:T],
                                               start=(k == 0), stop=(k == 7))
                            return ins
                        R.pe(mm, reads=hkeys + wkeys, writes=[("pm", bi)])
                        return bi

                    def stage2(ch, bi):
                        kind, c, col0 = ch
                        oi = cnt["oc"] % NOC
                        cnt["oc"] += 1
                        if kind == "z":
                            R.act(lambda e, bi=bi, oi=oi, T=T: e.activation(out=oc[oi][:, 0:T], in_=pm[bi][:, 0:T], func=AF.Silu),
                                  reads=[("pm", bi)], writes=[("oc", oi)])
                            dst = zT[l]
                            dkey = ("zT", c, w)
                        else:
                            ri = cnt["rp"] % 2
                            cnt["rp"] += 1
                            if is_a:
                                srcap = pm[bi]
                                skey = ("pm", bi)
                            else:
                                b = cnt["b"] % 2
                                cnt["b"] += 1
                                gi = 0 if kind == "q" else 1
                                R.act(lambda e, bi=bi, b=b, T=T: e.activation(out=sqb[b][:, 0:T], in_=pm[bi][:, 0:T], func=AF.Square),
                                      reads=[("pm", bi)], writes=[("sqb", b)])
                                R.pe(lambda e, b=b, T=T: e.matmul(pss[b][:, 0:T], ones_bf[:], sqb[b][:, 0:T], start=True, stop=True),
                                     reads=[("sqb", b), "ones"], writes=[("pss", b)])
                                R.act(lambda e, b=b, T=T: e.activation(out=lnt[b][:, 0:T], in_=pss[b][:, 0:T], func=AF.Ln,
                                                                      bias=epsb[:], scale=1.0 / 128),
                                      reads=[("pss", b), "eps"], writes=[("lnt", b)])
                                R.act(lambda e, b=b, T=T: e.activation(out=rsb[b][:, 0:T], in_=lnt[b][:, 0:T], func=AF.Exp, scale=-0.5),
                                      reads=[("lnt", b)], writes=[("rsb", b)])
                                R.dve(lambda e, b=b, bi=bi, gi=gi, T=T: e.scalar_tensor_tensor(
                                    out=qn[b][:, 0:T], in0=pm[bi][:, 0:T], scalar=gqk[:, gi:gi + 1], in1=rsb[b][:, 0:T],
                                    op0=ALU.mult, op1=ALU.mult),
                                    reads=[("pm", bi), ("rsb", b), ("gqk", gi)], writes=[("qn", b)])
                                srcap = qn[b]
                                skey = ("qn", b)
                            R.dve(lambda e, srcap=srcap, ri=ri, T=T: e.tensor_copy(out=swt[ri][0:64, 0:T], in_=srcap[64:128, 0:T]),
                                  reads=[skey], writes=[("swt", ri, 0)])
                            R.dve(lambda e, srcap=srcap, ri=ri, T=T: e.tensor_copy(out=swt[ri][64:128, 0:T], in_=srcap[0:64, 0:T]),
                                  reads=[skey], writes=[("swt", ri, 1)])
                            R.dve(lambda e, srcap=srcap, ri=ri, wi=wi, T=T: e.tensor_tensor(
                                out=t1[ri][:, 0:T], in0=srcap[:, 0:T], in1=rCt[wi][:, 0:T], op=ALU.mult),
                                reads=[skey, ("rCt", wi)], writes=[("t1", ri)])
                            R.pool(lambda e, ri=ri, wi=wi, T=T: e.tensor_tensor(
                                out=t2[ri][:, 0:T], in0=swt[ri][:, 0:T], in1=rSt[wi][:, 0:T], op=ALU.mult),
                                reads=[("swt", ri, 0), ("swt", ri, 1), ("rSt", wi)], writes=[("t2", ri)])
                            R.pool(lambda e, ri=ri, oi=oi, T=T: e.tensor_tensor(
                                out=oc[oi][:, 0:T], in0=t1[ri][:, 0:T], in1=t2[ri][:, 0:T], op=ALU.add),
                                reads=[("t1", ri), ("t2", ri)], writes=[("oc", oi)])
                            dst = qT[l] if kind == "q" else kTb[l]
                            dkey = ("qT" if kind == "q" else "kTb", c, w)
                        R.dma("sp", lambda e, dst=dst, c=c, oi=oi, tok0=tok0, T=T: e.dma_start(
                            out=dst[c * 128:(c + 1) * 128, tok0:tok0 + T], in_=oc[oi][:, 0:T]),
                            reads=[("oc", oi)], writes=[dkey])

                    prev = None
                    for ch in chunks:
                        bi = stage1(ch)
                        if prev is not None:
                            stage2(*prev)
                        prev = (ch, bi)
                    vprev = None
                    for jj in range(nsub if PS >= 6 else 0):
                        vi = cnt["vo"] % 2
                        cnt["vo"] += 1
                        for c0 in range(0, FV, 512):
                            cw = min(512, FV - c0)
                            bi = cnt["pm"] % 4
                            cnt["pm"] += 1

                            def mmv(e, jj=jj, c0=c0, cw=cw, bi=bi, wi=wi):
                                ins = None
                                for k in range(8):
                                    ins = e.matmul(pm[bi][:, 0:cw], hT[wi][:, k, jj * 128:(jj + 1) * 128],
                                                   wbf[:, k, colv + c0: colv + c0 + cw], start=(k == 0), stop=(k == 7))
                                return ins
                            R.pe(mmv, reads=hkeys + wkeys, writes=[("pm", bi)])
                            if prev is not None:
                                stage2(*prev)
                                prev = None
                            if os.environ.get('KDEBUG_VCOPY', 'act') == 'dve':
                                R.dve(lambda e, vi=vi, c0=c0, cw=cw, bi=bi: e.tensor_copy(out=vo[vi][:, c0:c0 + cw], in_=pm[bi][:, 0:cw]),
                                      reads=[("pm", bi)], writes=[("vo", vi, c0)])
                            else:
                                R.act(lambda e, vi=vi, c0=c0, cw=cw, bi=bi: e.copy(out=vo[vi][:, c0:c0 + cw], in_=pm[bi][:, 0:cw]),
                                      reads=[("pm", bi)], writes=[("vo", vi, c0)])
                        R.dma("sp", lambda e, vi=vi, jj=jj, tok0=tok0: e.dma_start(
                            out=vb[l].rearrange("(h t) d -> t h d", h=NVH)[tok0 + jj * 128: tok0 + (jj + 1) * 128, :, :],
                            in_=vo[vi][:].rearrange("p (h d) -> p h d", h=NVH)),
                            reads=[("vo", vi, c0) for c0 in range(0, FV, 512)], writes=[("vb", w, jj)])
                    if prev is not None:
                        stage2(*prev)
                        prev = None
                R.flush()

            if not run_ph('X'):
                continue
            RG = [[2 * p, 2 * p + 1] for p in range(NCORES // 2)]
            for c in range(max(NKC, NVH)):
                if c < NKC:
                    R.op("pool", lambda e, c=c: e.collective_compute(
                        "AllGather", ALU.bypass, replica_groups=RG,
                        ins=[kTb[l][c * 128:(c + 1) * 128, :].opt()], outs=[kTg[l][c * 256:(c + 1) * 256, :].opt()]),
                        writes=[("kTg", c)], kind="cc")
                if c < NVH:
                    R.op("pool", lambda e, c=c: e.collective_compute(
                        "AllGather", ALU.bypass, replica_groups=RG,
                        ins=[vb[l][c * NTOK:(c + 1) * NTOK, :].opt()], outs=[vg[l][c * 2 * NTOK:(c + 1) * 2 * NTOK, :].opt()]),
                        writes=[("vg", c)], kind="cc")
            if cfg.stop == 'X':
                R.flush()
                R.op("pool", lambda e: e.wait_ge(cc_sem, R.cc_n), kind="c")
                R.flush()

            if not run_ph('A'):
                continue
            with contextlib.ExitStack() as st:
                NU = 2 if is_a else 1
                kTs = [sb(st, f"kTs{i}", [128, 2 * NTOK], BF16) for i in range(2)]
                vs = [sb(st, f"vs{i}", [128, NKT, 128], BF16) for i in range(2)]
                qs = [[sb(st, f"qs{i}_{s}", [128, NTOK], BF16) for s in range(NU)] for i in range(2)]
                zs = [sb(st, f"zs{i}", [128, NTOK], BF16) for i in range(2)]
                NPT = 8
                pt = [sb(st, f"pt{i}", [128, 1024], BF16) for i in range(NPT)]
                accS = [sb(st, f"accS{i}", [128, 512], BF16) for i in range(2)]
                accP = [sb(st, f"accP{i}", [128, 1024], BF16) for i in range(2)]
                rr = [sb(st, f"rr{i}", [128, 512], F32) for i in range(2)]
                o_s = [sb(st, f"o_s{i}", [128, 512], F32) for i in range(2)]
                ocmb = sb(st, "ocmb", [128, 512], F32)
                sqe = sb(st, "sqe", [128, 512], BF16)
                sqo = [sb(st, f"sqo{i}", [128, 512], BF16) for i in range(2)]
                lno = sb(st, "lno", [128, 512], F32)
                rso = sb(st, "rso", [128, 512], F32)
                ono = sb(st, "ono", [128, 512], F32)
                ogs = [sb(st, f"ogs{i}", [128, 512], BF16) for i in range(2)]
                negM = [sb(st, f"negM{i}", [128, 1], F32) for i in range(4)]
                stt = sb(st, "stt", [128, 8], F32)
                stg = sb(st, "stg", [128, 64], F32)
                kmx = [sb(st, f"kmx{i}", [128, 2], F32) for i in range(2)]
                qmx = sb(st, "qmx", [128, 2], F32)
                if is_a:
                    lamt = sb(st, "lamt", [128, 256], F32)
                    lamj = sb(st, "lamj", [128, 64], F32)
                    lams = sb(st, "lams", [128, 4], F32)
                    neglam = sb(st, "neglam", [128, 1], F32)
                    gsub = sb(st, "gsub", [128, 1], F32)
                Sg = [ps(st, f"Sg{i}", [128, 1024]) for i in range(2)]
                Ob = [ps(st, f"Ob{i}", [128, 512]) for i in range(2)]
                Lb = ps(st, "Lb", [128, 512])
                accPS = ps(st, "accPS", [128, 512])
                pstat = [(Sg[0][:, 0:512], ("S", 0, 0)), (Sg[0][:, 512:1024], ("S", 0, 1)),
                         (Sg[1][:, 0:512], ("S", 1, 0)), (Sg[1][:, 512:1024], ("S", 1, 1))]

                scale = (64 ** -0.5) if is_a else (128 ** -0.5)
                if is_a:
                    lam_init = lambda_init_fn(l)
                    for i in range(2):
                        for s in range(2):
                            R.dve(lambda e, i=i, s=s: e.memset(qs[i][s][:], 0.0), writes=[("qs", i, s), ("qs2", i, s)])
                    R.dma("sp", lambda e: e.dma_start(out=lamt[:], in_=a_lambda[j, :, :]), writes=["lamt"])
                    R.dma("sp", lambda e: e.dma_start(out=gsub[:], in_=a_subln[j, :, :]), writes=["gsub0"])
                    for i in range(2):
                        R.dve(lambda e, i=i: e.scalar_tensor_tensor(
                            out=lamj[:], in0=lamt[:, (2 * i) * 64:(2 * i + 1) * 64], scalar=1.0,
                            in1=lamt[:, (2 * i + 1) * 64:(2 * i + 2) * 64], op0=ALU.mult, op1=ALU.mult,
                            accum_out=lams[:, i:i + 1]), reads=["lamt"], writes=["lamj", ("lams", i)])
                    R.act(lambda e: e.activation(out=lams[:, 2:4], in_=lams[:, 0:2], func=AF.Exp),
                          reads=[("lams", 0), ("lams", 1)], writes=["lame"])
                    R.dve(lambda e: e.scalar_tensor_tensor(out=neglam[:], in0=lams[:, 3:4], scalar=-lam_init,
                                                           in1=lams[:, 2:3], op0=ALU.add, op1=ALU.subtract),
                          reads=["lame"], writes=["neglam"])
                    R.dve(lambda e: e.tensor_scalar(out=gsub[:], in0=gsub[:], scalar1=(1.0 - lam_init), scalar2=None,
                                                    op0=ALU.mult), reads=["gsub0"], writes=["gsub"])

                FAST_RECIP = os.environ.get("KDEBUG_FASTRECIP", "0") == "1"
                pending = []
                state = {"unit": 0, "og": 0, "kvslot": -1, "kvhead": -1, "nm": 0, "sq": 0, "g": 0, "pb": 0}

                def drain(n=None, upto=None):
                    k = len(pending) if n is None else min(n, len(pending))
                    for _ in range(k):
                        if upto is not None and pending[0][0] > upto:
                            break
                        pending.pop(0)[1]()

                def defer(fn):
                    pending.append((state["unit"] - 1, fn))

                def flush_tail():
                    if state.get("tail") is not None:
                        t_ = state["tail"]
                        state["tail"] = None
                        t_()

                def maxsq(srct, ncols, rkeys, outs):
                    nch = 0
                    for c0 in range(0, ncols, 512):
                        cw = min(512, ncols - c0)
                        b = state["sq"] % 2
                        state["sq"] += 1
                        R.act(lambda e, b=b, c0=c0, cw=cw: e.activation(out=sqo[b][:, 0:cw], in_=srct[:, c0:c0 + cw], func=AF.Square),
                              reads=rkeys, writes=[("sqo", b)])
                        for oi_, (ind, dst, dkey) in enumerate(outs):
                            pb, pkey = pstat[state["pb"] % 4]
                            state["pb"] += 1
                            R.pe(lambda e, b=b, cw=cw, pb=pb, ind=ind: e.matmul(pb[:, 0:cw], ind, sqo[b][:, 0:cw], start=True, stop=True),
                                 reads=[("sqo", b)], writes=[pkey])
                            col = oi_ * 20 + nch
                            R.dve(lambda e, cw=cw, pb=pb, col=col: e.tensor_reduce(out=stg[:, col:col + 1], in_=pb[:, 0:cw], axis=AX.X, op=ALU.max),
                                  reads=[pkey], writes=[("stg", col)])
                        nch += 1
                    for oi_, (ind, dst, dkey) in enumerate(outs):
                        R.dve(lambda e, nch=nch, oi_=oi_, dst=dst: e.tensor_reduce(out=dst, in_=stg[:, oi_ * 20: oi_ * 20 + nch], axis=AX.X, op=ALU.max),
                              reads=[("stg", oi_ * 20 + i) for i in range(nch)], writes=[dkey])

                NH = 8
                for h in range(NH):
                    hs = h % 2
                    hk = h if is_a else h // 4
                    if hk != state["kvhead"]:
                        state["kvhead"] = hk
                        state["kvslot"] = (state["kvslot"] + 1) % 2
                        ks = state["kvslot"]
                        for rk in range(2):
                            R.dma("sp", lambda e, ks=ks, rk=rk, hk=hk: e.dma_start(
                                out=kTs[ks][:, rk * NTOK:(rk + 1) * NTOK],
                                in_=kTg[l][hk * 256 + rk * 128: hk * 256 + (rk + 1) * 128, :]),
                                reads=[("kTg", hk)], writes=[("kTs", ks, rk)])
                            vgv = vg[l][hk * 2 * NTOK:(hk + 1) * 2 * NTOK, :].rearrange("(kt p) f -> p kt f", p=128)
                            for part in range(4):
                                k0 = rk * KT_R + (KT_R * part) // 4
                                k1 = rk * KT_R + (KT_R * (part + 1)) // 4
                                if k1 > k0:
                                    R.dma("sp", lambda e, ks=ks, k0=k0, k1=k1, vgv=vgv: e.dma_start(
                                        out=vs[ks][:, k0:k1, :], in_=vgv[:, k0:k1, :]),
                                        reads=[("vg", hk)], writes=[("vs", ks, rk, part)])
                        if cfg.stab:
                            maxsq(kTs[ks], 2 * NTOK, [("kTs", ks, 0), ("kTs", ks, 1)],
                                  [((indm[:, s, :] if is_a else ones_bf[:]), kmx[ks][:, s:s + 1], ("kmx", ks, s)) for s in range(NU)])
                    ks = state["kvslot"]
                    if is_a:
                        for s in range(2):
                            for half in range(2):
                                p0 = 64 * half + 32 * s
                                R.dma("sp", lambda e, hs=hs, s=s, p0=p0, h=h: e.dma_start(
                                    out=qs[hs][s][p0:p0 + 32, :], in_=qT[l][h * 128 + p0: h * 128 + p0 + 32, :]),
                                    writes=[("qs", hs, s)] if half == 0 else [("qs2", hs, s)])
                    else:
                        R.dma("sp", lambda e, hs=hs, h=h: e.dma_start(out=qs[hs][0][:], in_=qT[l][h * 128:(h + 1) * 128, :]),
                              writes=[("qs", hs, 0)])
                    R.dma("sp", lambda e, hs=hs, h=h: e.dma_start(out=zs[hs][:], in_=zT[l][h * 128:(h + 1) * 128, :]),
                          writes=[("zs", hs)])

                    nm = []
                    for s in range(NU):
                        mi = state["nm"] % 4
                        state["nm"] += 1
                        nm.append(mi)
                        if cfg.stab:
                            maxsq(qs[hs][s], NTOK, [("qs", hs, s), ("qs2", hs, s)], [(ones_bf[:], qmx[:, s:s + 1], ("qmx", s))])
                            R.dve(lambda e, ks=ks, s=s: e.tensor_tensor(out=stt[:, 3:4], in0=kmx[ks][:, s:s + 1], in1=qmx[:, s:s + 1], op=ALU.mult),
                                  reads=[("kmx", ks, s), ("qmx", s)], writes=[("stt", 3)])
                            R.act(lambda e: e.activation(out=stt[:, 4:5], in_=stt[:, 3:4], func=AF.Ln, bias=epsb[:], scale=1.0),
                                  reads=[("stt", 3), "eps"], writes=[("stt", 4)])
                            R.act(lambda e: e.activation(out=stt[:, 5:6], in_=stt[:, 4:5], func=AF.Exp, scale=0.5),
                                  reads=[("stt", 4)], writes=[("stt", 5)])
                            R.dve(lambda e, mi=mi: e.tensor_scalar(out=negM[mi][:], in0=stt[:, 5:6], scalar1=-scale, scalar2=None, op0=ALU.mult),
                                  reads=[("stt", 5)], writes=[("negM", mi)])
                        else:
                            R.dve(lambda e, mi=mi: e.memset(negM[mi][:], 0.0), writes=[("negM", mi)])

                    def vpart(kt):
                        rk = kt // KT_R
                        for pp in range(4):
                            k0 = rk * KT_R + (KT_R * pp) // 4
                            k1 = rk * KT_R + (KT_R * (pp + 1)) // 4
                            if k0 <= kt < k1:
                                return rk, pp
                        return rk, 3

                    nqt = NW + (0 if last else 1)
                    for w in range(nqt):
                        T = 512 if w < NW else 128
                        tok0 = w * 512 if w < NW else LAT
                        ktl = list(range(NKT)) if w < NW else [KT_R - 1, 2 * KT_R - 1]
                        groups = [ktl[i:i + 2] for i in range(0, len(ktl), 2)]
                        if w >= NW:
                            flush_tail()
                            drain()
                        for s in range(NU):
                            ob = state["unit"] % 2
                            ab = ob
                            state["unit"] += 1
                            qsk = [("qs", hs, s), ("qs2", hs, s)] if is_a else [("qs", hs, s)]
                            mi = nm[s]

                            def QK(gi, grp, s=s, T=T, tok0=tok0, ks=ks, hs=hs, qsk=qsk):
                                for a, kt in enumerate(grp):
                                    R.pe(lambda e, gi=gi, a=a, kt=kt: e.matmul(
                                        Sg[gi][:, a * 512: a * 512 + T], kTs[ks][:, kt * 128:(kt + 1) * 128],
                                        qs[hs][s][:, tok0:tok0 + T], start=True, stop=True),
                                        reads=[("kTs", ks, kt // KT_R)] + qsk, writes=[("S", gi, a)])
                            ng = len(groups)
                            drain(upto=state["unit"] - 3)
                            step = max(1, (ng - 1) // (len(pending) + 1))
                            used = {"d": False, "p": False}
                            QK(state["g"] % 2, groups[0])
                            flush_tail()
                            prevPV = None
                            for gidx, grp in enumerate(groups):
                                g = state["g"]
                                state["g"] += 1
                                gi = g % 2
                                pi = g % NPT
                                na = len(grp)
                                if gidx + 1 < ng:
                                    QK((g + 1) % 2, groups[gidx + 1])
                                S3 = Sg[gi][:].rearrange("p (a t) -> p a t", a=2)[:, 0:na, 0:T]
                                P3 = pt[pi][:].rearrange("p (a t) -> p a t", a=2)[:, 0:na, 0:T]
                                R.act(lambda e, S3=S3, P3=P3, mi=mi: e.activation(out=P3, in_=S3, func=AF.Exp, bias=negM[mi][:], scale=scale),
                                      reads=[("S", gi, a) for a in range(na)] + [("negM", mi)], writes=[("pt", pi)])
                                if prevPV is not None:
                                    prevPV()

                                def PV(grp=grp, gidx=gidx, na=na, pi=pi, ob=ob, T=T, ks=ks, ng=ng):
                                    for a, kt in enumerate(grp):
                                        rk, part = vpart(kt)
                                        first = (gidx == 0 and a == 0)
                                        lastmm = (gidx == ng - 1 and a == na - 1)
                                        R.pe(lambda e, a=a, kt=kt, first=first, lastmm=lastmm: e.matmul(
                                            Ob[ob][:, 0:T], vs[ks][:, kt, :], pt[pi][:, a * 512: a * 512 + T], start=first, stop=lastmm),
                                            reads=[("vs", ks, rk, part), ("pt", pi)], writes=[("O", ob)])
                                prevPV = PV
                                use_d = (gidx % 2 == 0)
                                if use_d:
                                    for a in range(na):
                                        src_ = pt[pi][:, a * 512: a * 512 + T]
                                        if not used["d"]:
                                            used["d"] = True
                                            R.dve(lambda e, src_=src_, T=T: e.tensor_copy(out=accPS[:, 0:T], in_=src_),
                                                  reads=[("pt", pi)], writes=["accPS"])
                                        else:
                                            R.dve(lambda e, src_=src_, T=T: e.tensor_tensor(out=accPS[:, 0:T], in0=accPS[:, 0:T], in1=src_, op=ALU.add),
                                                  reads=[("pt", pi), "accPS"], writes=["accPS"])
                                else:
                                    A3 = accP[ab][:].rearrange("p (a t) -> p a t", a=2)[:, 0:na, 0:T]
                                    akey = ("accP", ab)
                                    if not used["p"]:
                                        used["p"] = True
                                        R.pool(lambda e, A3=A3, P3=P3: e.tensor_copy(out=A3, in_=P3), reads=[("pt", pi)], writes=[akey])
                                    else:
                                        R.pool(lambda e, A3=A3, P3=P3: e.tensor_tensor(out=A3, in0=A3, in1=P3, op=ALU.add),
                                               reads=[("pt", pi), akey], writes=[akey])
                                if gidx >= 1 and gidx % step == 0:
                                    drain(1)
                            state["tail"] = prevPV
                            na0 = len(groups[0])
                            srcs = []
                            if used["d"]:
                                R.dve(lambda e, ab=ab, T=T: e.tensor_copy(out=accS[ab][:, 0:T], in_=accPS[:, 0:T]),
                                      reads=["accPS"], writes=[("accS", ab)])
                                srcs += [(accS[ab], ("accS", ab), 0)]
                            if used["p"]:
                                srcs += [(accP[ab], ("accP", ab), a) for a in range(na0)]

                            def Lstage(srcs=srcs, T=T):
                                def mmL(e):
                                    ins = None
                                    for i, (t_, k_, a) in enumerate(srcs):
                                        ins = e.matmul(Lb[:, 0:T], ones_bf[:], t_[:, a * 512: a * 512 + T],
                                                       start=(i == 0), stop=(i == len(srcs) - 1))
                                    return ins
                                R.pe(mmL, reads=list({k_ for (_, k_, _) in srcs}), writes=["Lb"])
                            defer(Lstage)
                            defer(lambda s=s, T=T: R.dve(
                                lambda e: (e.reciprocal_approx_fast(out=rr[s][:, 0:T], in_=Lb[:, 0:T]) if FAST_RECIP
                                           else e.reciprocal(out=rr[s][:, 0:T], in_=Lb[:, 0:T])),
                                reads=["Lb"], writes=[("rr", s)]))
                            defer(lambda ob=ob, s=s, T=T: R.dve(
                                lambda e: e.tensor_tensor(out=o_s[s][:, 0:T], in0=Ob[ob][:, 0:T], in1=rr[s][:, 0:T], op=ALU.mult),
                                reads=[("O", ob), ("rr", s)], writes=[("o_s", s)]))
                        oi = state["og"] % 2
                        state["og"] += 1
                        if is_a:
                            defer(lambda T=T: R.dve(
                                lambda e: e.scalar_tensor_tensor(out=ocmb[:, 0:T], in0=o_s[1][:, 0:T], scalar=neglam[:],
                                                                 in1=o_s[0][:, 0:T], op0=ALU.mult, op1=ALU.add),
                                reads=[("o_s", 0), ("o_s", 1), "neglam"], writes=["ocmb"]))
                            defer(lambda T=T: R.act(
                                lambda e: e.activation(out=sqe[:, 0:T], in_=ocmb[:, 0:T], func=AF.Square),
                                reads=["ocmb"], writes=["sqe"]))
                            defer(lambda T=T: R.pe(
                                lambda e: e.matmul(Lb[:, 0:T], ones_bf[:], sqe[:, 0:T], start=True, stop=True),
                                reads=["sqe"], writes=["Lb"]))
                            defer(lambda T=T: R.act(
                                lambda e: e.activation(out=lno[:, 0:T], in_=Lb[:, 0:T], func=AF.Ln, bias=epsb[:], scale=1.0 / 128),
                                reads=["Lb"], writes=["lno"]))
                            defer(lambda T=T: R.act(
                                lambda e: e.activation(out=rso[:, 0:T], in_=lno[:, 0:T], func=AF.Exp, scale=-0.5),
                                reads=["lno"], writes=["rso"]))
                            defer(lambda T=T: R.dve(
                                lambda e: e.scalar_tensor_tensor(out=ono[:, 0:T], in0=ocmb[:, 0:T], scalar=gsub[:],
                                                                 in1=rso[:, 0:T], op0=ALU.mult, op1=ALU.mult),
                                reads=["ocmb", "rso", "gsub"], writes=["ono"]))
                            fin_src, fin_key = ono, "ono"
                        else:
                            fin_src, fin_key = o_s[0], ("o_s", 0)
                        defer(lambda T=T, oi=oi, tok0=tok0, fin_src=fin_src, fin_key=fin_key, hs=hs: R.dve(
                            lambda e: e.tensor_tensor(out=ogs[oi][:, 0:T], in0=fin_src[:, 0:T], in1=zs[hs][:, tok0:tok0 + T], op=ALU.mult),
                            reads=[fin_key, ("zs", hs)], writes=[("ogs", oi)]))
                        defer(lambda T=T, oi=oi, tok0=tok0, h=h, w=w: R.dma(
                            "pool", lambda e: e.dma_start(out=ogT[l][h * 128:(h + 1) * 128, tok0:tok0 + T], in_=ogs[oi][:, 0:T]),
                            reads=[("ogs", oi)], writes=[("ogT", h, w)]))
                flush_tail()
                drain()
                R.flush()

            if not run_ph('O'):
                continue
            with contextlib.ExitStack() as st:
                wo = sb(st, "wo", [128, 8, D], BF16)
                og = [sb(st, f"og{i}", [128, 8, 512], BF16) for i in range(2)]
                NXO = 6
                xo = [sb(st, f"xo{i}", [128, D], F32) for i in range(NXO)]
                yt = [sb(st, f"yt{i}", [128, D], F32) for i in range(2)]
                xw = [sb(st, f"xw{i}", [128, D], F32) for i in range(3)]
                sqy = sb(st, "sqy", [128, 512], BF16)
                ssy = [sb(st, f"ssy{i}", [128, 4], F32) for i in range(2)]
                py = [ps(st, f"py{i}", [128, 512]) for i in range(4)]
                wov = w_out[l].rearrange("(k p) n -> p k n", p=128)
                for k in range(8):
                    R.dma("pool", lambda e, k=k: e.dma_start(out=wo[:, k, :], in_=wov[:, k, :]), writes=[("wo", k)])
                wokeys = [("wo", k) for k in range(8)]
                ogv = ogT[l].rearrange("(k p) t -> p k t", p=128)
                tcount = 0
                nwt = NW + (0 if last else 1)
                for w in range(nwt):
                    T = 512 if w < NW else 128
                    nsub = T // 128
                    tok0 = w * 512 if w < NW else LAT
                    r = 0 if w < NW else 1
                    wi = w % 2
                    R.dma("sp", lambda e, wi=wi, tok0=tok0, T=T: e.dma_start(out=og[wi][:, :, 0:T], in_=ogv[:, :, tok0:tok0 + T]),
                          writes=[("og", wi)])
                    for jj in range(nsub):
                        xi = tcount % NXO
                        yi = tcount % 2
                        xwi = tcount % 3
                        tcount += 1
                        t0 = tok0 + jj * 128
                        R.dma("sp", lambda e, xi=xi, t0=t0: e.dma_start(out=xo[xi][:], in_=src[t0:t0 + 128, :]), writes=[("xo", xi)])
                        for nh in range(2):
                            bi = (2 * yi + nh)

                            def mmo(e, nh=nh, jj=jj, wi=wi, bi=bi):
                                ins = None
                                for k in range(8):
                                    ins = e.matmul(py[bi][:], og[wi][:, k, jj * 128:(jj + 1) * 128], wo[:, k, nh * 512:(nh + 1) * 512],
                                                   start=(k == 0), stop=(k == 7))
                                return ins
                            R.pe(mmo, reads=[("og", wi)] + wokeys, writes=[("py", bi)])
                            R.act(lambda e, bi=bi, yi=yi, nh=nh: e.activation(out=sqy[:], in_=py[bi][:], func=AF.Square,
                                                                              accum_out=ssy[yi][:, nh:nh + 1]),
                                  reads=[("py", bi)], writes=["sqy", ("ssy", yi, nh)])
                        R.dve(lambda e, yi=yi: e.tensor_tensor(out=ssy[yi][:, 2:3], in0=ssy[yi][:, 0:1], in1=ssy[yi][:, 1:2], op=ALU.add),
                              reads=[("ssy", yi, 0), ("ssy", yi, 1)], writes=[("ssy", yi, 2)])
                        R.act(lambda e, yi=yi: e.activation(out=ssy[yi][:, 3:4], in_=ssy[yi][:, 2:3], func=AF.Ln, bias=epsb[:], scale=1.0 / D),
                              reads=[("ssy", yi, 2), "eps"], writes=[("ssy", yi, 3)])
                        R.act(lambda e, yi=yi: e.activation(out=ssy[yi][:, 2:3], in_=ssy[yi][:, 3:4], func=AF.Exp, scale=-0.5),
                              reads=[("ssy", yi, 3)], writes=[("ssy", yi, 4)])
                        for nh in range(2):
                            bi = (2 * yi + nh)
                            R.dve(lambda e, bi=bi, yi=yi, nh=nh, r=r: e.scalar_tensor_tensor(
                                out=yt[yi][:, nh * 512:(nh + 1) * 512], in0=py[bi][:], scalar=ssy[yi][:, 2:3],
                                in1=Gb[r][:, nh * 512:(nh + 1) * 512], op0=ALU.mult, op1=ALU.mult),
                                reads=[("py", bi), ("ssy", yi, 4), ("Gb", r)], writes=[("yt", yi, nh)])
                        R.dve(lambda e, yi=yi, xi=xi, xwi=xwi: e.tensor_tensor(out=xw[xwi][:], in0=yt[yi][:], in1=xo[xi][:], op=ALU.add),
                              reads=[("yt", yi, 0), ("yt", yi, 1), ("xo", xi)], writes=[("xw", xwi)])
                        if last:
                            R.dma("pool", lambda e, xwi=xwi, t0=t0: e.dma_start(out=out[t0:t0 + 128, :], in_=xw[xwi][:]),
                                  reads=[("xw", xwi)], writes=[("out", t0)])
                        else:
                            R.dma("pool", lambda e, xwi=xwi, t0=t0: e.dma_start(out=xs[t0:t0 + 128, :], in_=xw[xwi][:]),
                                  reads=[("xw", xwi)], writes=[("xs", t0)])
                R.flush()
    return nc


def _rope_tables(cfg, hf, head_dim, dup):
    LAT, NTOK = cfg.LAT, cfg.NTOK
    t = np.arange(hf * LAT, (hf + 1) * LAT)
    rows = (t // GRID_W).astype(np.float32)
    cols = (t % GRID_W).astype(np.float32)
    axis_dim = head_dim // 2
    freqs = (ROPE_THETA ** (-np.arange(0, axis_dim, 2, dtype=np.float32) / np.float32(axis_dim))).astype(np.float32)
    ang = np.concatenate([rows[:, None] * freqs, cols[:, None] * freqs], axis=-1).astype(np.float32)
    cos = np.cos(ang).astype(np.float32).T
    sin = np.sin(ang).astype(np.float32).T
    half = head_dim // 2
    C = np.ones((128, NTOK), np.float32)
    S = np.zeros((128, NTOK), np.float32)
    if dup:
        for blk in range(4):
            C[blk * 32:(blk + 1) * 32, :LAT] = cos
            S[blk * 32:(blk + 1) * 32, :LAT] = -sin if blk < 2 else sin
    else:
        C[0:64, :LAT] = cos
        C[64:128, :LAT] = cos
        S[0:64, :LAT] = -sin
        S[64:128, :LAT] = sin
    return C, S


def _perm_a_cols():
    perm = np.arange(4096)
    p128 = np.zeros(128, np.int64)
    for n in range(128):
        blk = n // 32
        s = blk % 2
        d = (n % 32) + (32 if blk >= 2 else 0)
        p128[n] = s * 64 + d
    for base in (0, 1024):
        for h in range(8):
            perm[base + h * 128: base + (h + 1) * 128] = base + h * 128 + p128
    return perm


def make_in_maps(cfg, x, c, ctx, c_ctx, ada_w, ada_b, pre_g, post_g, w_out, a_w_in, a_lambda, a_subln_g, b_w_in, b_qk_g):
    DEPTH = cfg.DEPTH
    f = lambda a: np.ascontiguousarray(np.asarray(a, dtype=np.float32))
    x, c, ctx, c_ctx = f(x), f(c), f(ctx), f(c_ctx)
    NA = (DEPTH + 1) // 2
    NB = max(DEPTH // 2, 1)
    shared = {
        "ada_w": f(ada_w)[:DEPTH],
        "ada_b": np.ascontiguousarray(np.broadcast_to(f(ada_b)[:DEPTH, None, :], (DEPTH, 128, 3 * D))),
        "pre_g": np.ascontiguousarray(np.broadcast_to(f(pre_g)[:DEPTH, None, :], (DEPTH, 128, D))),
        "post_g": np.ascontiguousarray(np.broadcast_to(f(post_g)[:DEPTH, None, :], (DEPTH, 128, D))),
        "w_out": f(w_out)[:DEPTH],
        "a_w_in": np.ascontiguousarray(f(a_w_in)[:NA][:, :, _perm_a_cols()]),
        "a_lambda": np.ascontiguousarray(np.broadcast_to(f(a_lambda)[:NA].reshape(NA, 1, 256), (NA, 128, 256))),
        "a_subln": np.ascontiguousarray(f(a_subln_g)[:NA].reshape(NA, 128, 1)),
        "b_w_in": f(b_w_in)[:NB],
        "b_qk_g": np.ascontiguousarray(f(b_qk_g)[:NB].reshape(NB, 2, 128, 1)),
        "ident": np.eye(128, dtype=np.float32).astype(ml_dtypes.bfloat16),
    }
    ind = np.zeros((2, 128, 128), np.float32)
    for s in range(2):
        for p in range(128):
            if (p // 32) % 2 == s:
                ind[s, p, :] = 1.0
    shared["indmat"] = ind.astype(ml_dtypes.bfloat16)
    maps = []
    for i in range(NCORES):
        b, hf = i // 2, i % 2
        LAT = cfg.LAT
        xin = np.concatenate([x[b, hf * LAT:(hf + 1) * LAT], ctx[b, hf * CTXH:(hf + 1) * CTXH]], axis=0)
        cin = np.stack([c[b].reshape(8, 128).T, c_ctx.reshape(8, 128).T], axis=1)
        ac, as_ = _rope_tables(cfg, hf, 64, True)
        bc, bs = _rope_tables(cfg, hf, 128, False)
        m = dict(shared)
        m.update({"xin": np.ascontiguousarray(xin), "cin": np.ascontiguousarray(cin),
                  "ropeA_C": ac, "ropeA_S": as_, "ropeB_C": bc, "ropeB_S": bs})
        maps.append(m)
    return maps


_CACHE = {}


def run(cfg, inputs, trace=False):
    key = (cfg.SEQ, cfg.DEPTH, cfg.debug, cfg.stab, cfg.stop)
    if key not in _CACHE:
        _CACHE[key] = build_program(cfg)
    nc = _CACHE[key]
    maps = make_in_maps(cfg, **inputs)
    res = run_bass_kernel_spmd(nc, maps, core_ids=list(range(NCORES)))
    return res


def kernel(x, c, ctx, c_ctx, ada_w, ada_b, pre_g, post_g, w_out, a_w_in, a_lambda, a_subln_g, b_w_in, b_qk_g):
    x = np.asarray(x)
    B, S, _ = x.shape
    cfg = Cfg(S, 4)
    res = run(cfg, dict(x=x, c=c, ctx=ctx, c_ctx=c_ctx, ada_w=ada_w, ada_b=ada_b, pre_g=pre_g, post_g=post_g,
                        w_out=w_out, a_w_in=a_w_in, a_lambda=a_lambda, a_subln_g=a_subln_g, b_w_in=b_w_in, b_qk_g=b_qk_g))
    outp = np.empty((B, S, D), np.float32)
    for i in range(NCORES):
        b, hf = i // 2, i % 2
        outp[b, hf * cfg.LAT:(hf + 1) * cfg.LAT] = np.asarray(res.results[i]["out"], dtype=np.float32)
    return outp
```

```python
import math
import numpy as np
import ml_dtypes
import concourse.bass as bass
import concourse.mybir as mybir
from concourse.bass_utils import run_bass_kernel_spmd

F32 = mybir.dt.float32
BF16 = mybir.dt.bfloat16
AF = mybir.ActivationFunctionType
ALU = mybir.AluOpType
AX = mybir.AxisListType

D = 1024
NCORES = 8
CTXH = 128
EPS = 1e-6
ROPE_THETA = 10000.0
GRID_W = 64


def lambda_init_fn(i):
    return 0.8 - 0.6 * math.exp(-0.3 * i)


class Op:
    __slots__ = ("eng", "fn", "deps", "kind", "signal", "val", "sem", "prewait")

    def __init__(self, eng, fn, deps, kind):
        self.eng = eng
        self.fn = fn
        self.deps = deps
        self.kind = kind
        self.signal = kind != "c"
        self.val = None
        self.sem = None
        self.prewait = None


class Rec:
    ENGS = ("pe", "act", "dve", "pool", "sp")
    NPOOL = 8

    def __init__(self, nc, sems, dma_sems, cc_sem):
        self.nc = nc
        self.sems = sems
        self.dma_sems = dma_sems
        self.cc_sem = cc_sem
        self.cnt = {e: 0 for e in self.ENGS}
        self.dma_n = {e: 0 for e in self.ENGS}
        self.cc_n = 0
        self.ops = []
        self.last_w = {}
        self.readers = {}
        self.nops = 0

    def op(self, eng, fn, reads=(), writes=(), kind="c"):
        deps = set()
        for k in reads:
            w = self.last_w.get(k)
            if w is not None:
                deps.add(w)
        for k in writes:
            w = self.last_w.get(k)
            if w is not None:
                deps.add(w)
            for r in self.readers.get(k, ()):
                deps.add(r)
        o = Op(eng, fn, deps, kind)
        for k in reads:
            self.readers.setdefault(k, []).append(o)
        for k in writes:
            self.last_w[k] = o
            self.readers[k] = []
        self.ops.append(o)
        return o

    def pe(self, fn, reads=(), writes=()):
        return self.op("pe", fn, reads, writes)

    def act(self, fn, reads=(), writes=()):
        return self.op("act", fn, reads, writes)

    def dve(self, fn, reads=(), writes=()):
        return self.op("dve", fn, reads, writes)

    def pool(self, fn, reads=(), writes=()):
        return self.op("pool", fn, reads, writes)

    def dma(self, eng, fn, reads=(), writes=()):
        return self.op(eng, fn, reads, writes, kind="d")

    def flush(self):
        nc = self.nc
        ops = self.ops
        live = set(id(o) for o in ops)
        for o in ops:
            nd = set()
            for d in o.deps:
                if id(d) not in live:
                    continue
                if d.eng == "pe" and o.eng == "pe" and d.kind == "c":
                    continue
                nd.add(d)
                d.signal = True
            o.deps = nd
        per = {e: [] for e in self.ENGS}
        for o in ops:
            per[o.eng].append(o)
            if o.kind == "d":
                n = self.dma_n[o.eng]
                self.dma_n[o.eng] = n + 1
                o.sem = self.dma_sems[o.eng][n % self.NPOOL]
                o.val = 16 * (n // self.NPOOL + 1)
                if n >= self.NPOOL:
                    o.prewait = (o.sem, 16 * (n // self.NPOOL))
            elif o.kind == "cc":
                self.cc_n += 1
                o.sem = self.cc_sem
                o.val = self.cc_n
            elif o.signal:
                self.cnt[o.eng] += 1
                o.sem = self.sems[o.eng]
                o.val = self.cnt[o.eng]
        dma_final = {}
        for e in self.ENGS:
            n = self.dma_n[e]
            fin = []
            for i in range(min(n, self.NPOOL)):
                last = ((n - 1 - i) // self.NPOOL) * self.NPOOL + i
                fin.append((self.dma_sems[e][i], 16 * (last // self.NPOOL + 1)))
            dma_final[e] = fin
        self.nops += len(ops)

        def emit(e_ops, eng_name):
            def body(e):
                waited = {}
                for o in e_ops:
                    ws = []
                    if o.prewait is not None:
                        ws.append(o.prewait)
                    for d in o.deps:
                        ws.append((d.sem, d.val))
                    for (s, v) in ws:
                        key = id(s)
                        if waited.get(key, 0) >= v:
                            continue
                        waited[key] = v
                        e.wait_ge(s, v)
                    ins = o.fn(e)
                    if o.kind == "d":
                        ins.then_inc(o.sem, 16)
                    elif o.kind == "cc":
                        ins.then_inc(o.sem)
                    elif o.signal:
                        ins.then_inc(o.sem, 1)
                for (s, v) in dma_final[eng_name]:
                    if waited.get(id(s), 0) < v:
                        e.wait_ge(s, v)
            return body

        with nc.Block() as block:
            reg = {"pe": block.tensor, "act": block.scalar, "dve": block.vector,
                   "pool": block.gpsimd, "sp": block.sync}
            for e in self.ENGS:
                if per[e] or dma_final[e]:
                    reg[e](emit(per[e], e))
        self.ops = []
        self.last_w = {}
        self.readers = {}


class Cfg:
    def __init__(self, seq, depth, debug=False, stab=True, stop=None):
        self.stop = stop
        self.SEQ = seq
        self.DEPTH = depth
        self.LAT = seq // 2
        self.NTOK = self.LAT + CTXH
        self.NW = self.LAT // 512
        self.KT_R = self.NTOK // 128
        self.NKT = 2 * self.KT_R
        self.debug = debug
        self.stab = stab


def layer_dims(l):
    if l % 2 == 0:
        return True, 1024, 1024, 1024, 4096, 0, 1024, 2048, 3072
    return False, 1024, 256, 256, 2560, 0, 1024, 1280, 1536


def build_program(cfg):
    nc = bass.Bass("TRN2", target_bir_lowering=False)
    NTOK, LAT, NW, KT_R, NKT, DEPTH = cfg.NTOK, cfg.LAT, cfg.NW, cfg.KT_R, cfg.NKT, cfg.DEPTH
    dbg_kind = "ExternalOutput" if cfg.debug else "Internal"

    def din(name, shape, dt=F32):
        return nc.dram_tensor(name, list(shape), dt, kind="ExternalInput")

    xin = din("xin", [NTOK, D])
    cin = din("cin", [128, 2, 8])
    ada_w = din("ada_w", [DEPTH, D, 3 * D])
    ada_b = din("ada_b", [DEPTH, 128, 3 * D])
    pre_g = din("pre_g", [DEPTH, 128, D])
    post_g = din("post_g", [DEPTH, 128, D])
    w_out = din("w_out", [DEPTH, D, D])
    NA = (DEPTH + 1) // 2
    NB = max(DEPTH // 2, 1)
    a_w_in = din("a_w_in", [NA, D, 4096])
    a_lambda = din("a_lambda", [NA, 128, 256])
    a_subln = din("a_subln", [NA, 128, 1])
    b_w_in = din("b_w_in", [NB, D, 2560])
    b_qk_g = din("b_qk_g", [NB, 2, 128, 1])
    ropeC = [din("ropeA_C", [128, NTOK]), din("ropeB_C", [128, NTOK])]
    ropeS = [din("ropeA_S", [128, NTOK]), din("ropeB_S", [128, NTOK])]
    ident_in = din("ident", [128, 128], BF16)
    ind_in = din("indmat", [2, 128, 128], BF16)
    out = nc.dram_tensor("out", [LAT, D], F32, kind="ExternalOutput")

    xs = nc.dram_tensor("xs", [NTOK, D], F32, kind=dbg_kind)
    qT, zT, ogT, kTb, kTg, vb, vg = [], [], [], [], [], [], []
    for l in range(DEPTH):
        is_a, FQ, FK, FV, *_ = layer_dims(l)
        qT.append(nc.dram_tensor(f"qT{l}", [FQ, NTOK], BF16, kind=dbg_kind))
        zT.append(nc.dram_tensor(f"zT{l}", [D, NTOK], BF16, kind=dbg_kind))
        ogT.append(nc.dram_tensor(f"ogT{l}", [D, NTOK], BF16, kind=dbg_kind))
        kTb.append(nc.dram_tensor(f"kTb{l}", [FK, NTOK], BF16))
        kTg.append(nc.dram_tensor(f"kTg{l}", [2 * FK, NTOK], BF16))
        vb.append(nc.dram_tensor(f"vb{l}", [(FV // 128) * NTOK, 128], BF16))
        vg.append(nc.dram_tensor(f"vg{l}", [(FV // 128) * 2 * NTOK, 128], BF16))

    import contextlib
    es = contextlib.ExitStack()
    with es:
        def sem(name):
            return es.enter_context(nc.semaphore(name))

        sems = {e: sem(f"s_{e}") for e in Rec.ENGS}
        dma_sems = {e: [sem(f"d_{e}{i}") for i in range(Rec.NPOOL)] for e in ("sp", "pool", "act")}
        dma_sems["pe"] = dma_sems["dve"] = []
        cc_sem = sem("cc")
        R = Rec(nc, sems, dma_sems, cc_sem)

        l_tag = ["g"]

        def sb(st, name, shape, dt):
            return st.enter_context(nc.sbuf_tensor(f"sb{l_tag[0]}_{name}", list(shape), dt))

        def ps(st, name, shape, dt=F32):
            return st.enter_context(nc.psum_tensor(f"ps{l_tag[0]}_{name}", list(shape), dt))

        ident = sb(es, "ident", [128, 128], BF16)
        identf = sb(es, "identf", [128, 128], F32)
        ones_bf = sb(es, "ones_bf", [128, 128], BF16)
        onesf = sb(es, "onesf", [128, 128], F32)
        indm = sb(es, "indm", [128, 2, 128], BF16)
        epsb = sb(es, "epsb", [128, 1], F32)
        cint = sb(es, "cint", [128, 2, 8], F32)
        scs = sb(es, "scs", [128, 2, 8], F32)
        scb = sb(es, "scb", [128, 2, 8, 128], F32)
        Gb = [sb(es, f"Gb{r}", [128, D], F32) for r in range(2)]
        Acol = [sb(es, f"Acol{r}", [128, 8], F32) for r in range(2)]
        Scol = [sb(es, f"Scol{r}", [128, 8], F32) for r in range(2)]

        R.dma("sp", lambda e: e.dma_start(out=ident[:], in_=ident_in[:, :]), writes=["ident"])
        R.dma("sp", lambda e: e.dma_start(out=indm[:], in_=ind_in.ap().rearrange("s p m -> p s m")),
              writes=["indm"])
        R.dma("sp", lambda e: e.dma_start(out=cint[:], in_=cin[:, :, :]), writes=["cint"])
        R.dve(lambda e: e.tensor_copy(out=identf[:], in_=ident[:]), reads=["ident"], writes=["identf"])
        R.dve(lambda e: e.memset(ones_bf[:], 1.0), writes=["ones"])
        R.dve(lambda e: e.memset(onesf[:], 1.0), writes=["onesf"])
        R.dve(lambda e: e.memset(epsb[:], EPS), writes=["eps"])
        import os
        ZS = int(os.environ.get('KDEBUG_ZSTEP', '9'))
        if ZS >= 2:
            R.act(lambda e: e.activation(out=scs[:], in_=cint[:], func=AF.Silu), reads=["cint"], writes=["scs"])
        for r in range(2 if ZS >= 3 else 0):
            for k in range(8):
                R.dve(lambda e, r=r, k=k: e.tensor_scalar(
                    out=scb[:, r, k, :], in0=onesf[:], scalar1=scs[:, r, k:k + 1], scalar2=None,
                    op0=ALU.mult), reads=["onesf", "scs"], writes=[("scb", r, k)])
        R.flush()

        for l in range(DEPTH):
            if cfg.stop == 'Z':
                break
            is_a, FQ, FK, FV, NCOL, colq, colk, colv, colz = layer_dims(l)
            l_tag[0] = str(l)
            j = l // 2
            last = l == DEPTH - 1
            w_in = a_w_in if is_a else b_w_in
            rC, rS = (ropeC[0], ropeS[0]) if is_a else (ropeC[1], ropeS[1])
            src = xin if l == 0 else xs
            NQC = FQ // 128
            NKC = FK // 128
            NVH = FV // 128

            PH = ['M', 'P', 'X', 'A', 'O']
            run_ph = lambda t: cfg.stop is None or PH.index(t) <= PH.index(cfg.stop)
            with contextlib.ExitStack() as st:
                awt = [sb(st, f"awt{i}", [128, 8, 512], F32) for i in range(2)]
                modt = [sb(st, f"modt{r}", [128, 3 * D], F32) for r in range(2)]
                adab = sb(st, "adab", [128, 3 * D], F32)
                pgb = sb(st, "pgb", [128, D], F32)
                qgb = sb(st, "qgb", [128, D], F32)
                tmpA = sb(st, "tmpA", [128, D], F32)
                junk = sb(st, "junkM", [128, 128], F32)
                pm = [ps(st, f"pmM{i}", [128, 512]) for i in range(4)]
                R.dma("sp", lambda e: e.dma_start(out=adab[:], in_=ada_b[l, :, :]), writes=["adab"])
                R.dma("sp", lambda e: e.dma_start(out=pgb[:], in_=pre_g[l, :, :]), writes=["pgb"])
                R.dma("sp", lambda e: e.dma_start(out=qgb[:], in_=post_g[l, :, :]), writes=["qgb"])
                awv = ada_w[l].rearrange("(k p) n -> p k n", p=128)
                for c in range(6):
                    R.dma("sp", lambda e, c=c: e.dma_start(out=awt[c % 2][:], in_=awv[:, :, c * 512:(c + 1) * 512]),
                          writes=[("awt", c % 2)])
                    for r in range(2):
                        bank = pm[(2 * c + r) % 4]

                        def mm(e, c=c, r=r, bank=bank):
                            ins = None
                            for k in range(8):
                                ins = e.matmul(bank[:], scb[:, r, k, :], awt[c % 2][:, k, :],
                                               start=(k == 0), stop=(k == 7))
                            return ins
                        R.pe(mm, reads=[("awt", c % 2)], writes=[("pmM", (2 * c + r) % 4)])
                        R.dve(lambda e, c=c, r=r, bank=bank: e.tensor_tensor(
                            out=modt[r][:, c * 512:(c + 1) * 512], in0=bank[:],
                            in1=adab[:, c * 512:(c + 1) * 512], op=ALU.add),
                            reads=[("pmM", (2 * c + r) % 4), "adab"], writes=[("modt", r, c)])
                import os
                MS = int(os.environ.get('KDEBUG_MSTEP', '9'))
                for r in range(2 if MS >= 2 else 0):
                    allmod = [("modt", r, c) for c in range(6)]
                    R.dve(lambda e, r=r: e.scalar_tensor_tensor(
                        out=tmpA[:], in0=modt[r][:, D:2 * D], scalar=1.0, in1=pgb[:],
                        op0=ALU.add, op1=ALU.mult), reads=allmod + ["pgb"], writes=["tmpA"])
                    for k in range(8):
                        R.dve(lambda e, r=r, k=k: e.scalar_tensor_tensor(
                            out=junk[:], in0=tmpA[:, k * 128:(k + 1) * 128], scalar=1.0, in1=identf[:],
                            op0=ALU.mult, op1=ALU.mult, accum_out=Acol[r][:, k:k + 1]),
                            reads=["tmpA"], writes=["junkM", ("Acol", r)])
                    for k in range(8):
                        R.dve(lambda e, r=r, k=k: e.scalar_tensor_tensor(
                            out=junk[:], in0=modt[r][:, k * 128:(k + 1) * 128], scalar=1.0, in1=identf[:],
                            op0=ALU.mult, op1=ALU.mult, accum_out=Scol[r][:, k:k + 1]),
                            reads=allmod, writes=["junkM", ("Scol", r)])
                    R.dve(lambda e, r=r: e.tensor_tensor(out=Gb[r][:], in0=modt[r][:, 2 * D:3 * D], in1=qgb[:],
                                                         op=ALU.mult), reads=allmod + ["qgb"], writes=[("Gb", r)])
                R.flush()

            if not run_ph('P'):
                continue
            with contextlib.ExitStack() as st:
                wbf = sb(st, "wbf", [128, 8, NCOL], BF16)
                NXT = 8
                xt = [sb(st, f"xt{i}", [128, D], F32) for i in range(NXT)]
                xn = [sb(st, f"xn{i}", [128, D], BF16) for i in range(4)]
                sqj = sb(st, "sqj", [128, D], BF16)
                ssq = [sb(st, f"ssq{i}", [128, 4], F32) for i in range(2)]
                lnv = [sb(st, f"lnv{i}", [128, 4], F32) for i in range(2)]
                rst = [sb(st, f"rst{i}", [128, 4], F32) for i in range(2)]
                hT = [sb(st, f"hT{i}", [128, 8, 512], BF16) for i in range(2)]
                rCt = [sb(st, f"rCt{i}", [128, 512], F32) for i in range(2)]
                rSt = [sb(st, f"rSt{i}", [128, 512], F32) for i in range(2)]
                swt = [sb(st, f"swt{i}", [128, 512], F32) for i in range(2)]
                t1 = [sb(st, f"t1_{i}", [128, 512], F32) for i in range(2)]
                t2 = [sb(st, f"t2_{i}", [128, 512], F32) for i in range(2)]
                NOC = 4
                oc = [sb(st, f"oc{i}", [128, 512], BF16) for i in range(NOC)]
                vo = [sb(st, f"vo{i}", [128, FV], BF16) for i in range(2)]
                if not is_a:
                    sqb = [sb(st, f"sqb{i}", [128, 512], BF16) for i in range(2)]
                    lnt = [sb(st, f"lnt{i}", [128, 512], F32) for i in range(2)]
                    rsb = [sb(st, f"rsb{i}", [128, 512], F32) for i in range(2)]
                    qn = [sb(st, f"qn{i}", [128, 512], F32) for i in range(2)]
                    gqk = sb(st, "gqk", [128, 2], F32)
                tp = [ps(st, f"tp{i}", [128, 1024], BF16) for i in range(2)]
                pm = [ps(st, f"pmP{i}", [128, 512]) for i in range(4)]
                if not is_a:
                    pss = [ps(st, f"pss{i}", [128, 512]) for i in range(2)]

                wv = w_in[j].rearrange("(k p) n -> p k n", p=128)
                CW = 1024
                for k in range(8):
                    for c0 in range(0, NCOL, CW):
                        c1 = min(NCOL, c0 + CW)
                        R.dma("pool", lambda e, k=k, c0=c0, c1=c1: e.dma_start(out=wbf[:, k, c0:c1], in_=wv[:, k, c0:c1]),
                              writes=[("wbf", k, c0)])
                wkeys = [("wbf", k, c0) for k in range(8) for c0 in range(0, NCOL, CW)]
                if not is_a:
                    for i in range(2):
                        R.dma("sp", lambda e, i=i: e.dma_start(out=gqk[:, i:i + 1], in_=b_qk_g[j, i, :, :]),
                              writes=[("gqk", i)])

                cnt = {"xt": 0, "oc": 0, "pm": 0, "vo": 0, "tp": 0, "rp": 0, "b": 0}
                PS = int(os.environ.get('KDEBUG_PSTEP', '9'))
                xts_by_w = {}

                def issue_loads(w_):
                    T_ = 512 if w_ < NW else 128
                    tok0_ = w_ * 512 if w_ < NW else LAT
                    wi_ = w_ % 2
                    lst = []
                    for jj in range(T_ // 128):
                        xi = cnt["xt"] % NXT
                        cnt["xt"] += 1
                        lst.append(xi)
                        R.dma("sp", lambda e, xi=xi, jj=jj, tok0_=tok0_: e.dma_start(
                            out=xt[xi][:], in_=src[tok0_ + jj * 128: tok0_ + (jj + 1) * 128, :]),
                            reads=[("xs", tok0_ + jj * 128)], writes=[("xt", xi)])
                    xts_by_w[w_] = lst
                    R.dma("sp", lambda e, wi_=wi_, tok0_=tok0_, T_=T_: e.dma_start(out=rCt[wi_][:, 0:T_], in_=rC[:, tok0_:tok0_ + T_]),
                          writes=[("rCt", wi_)])
                    R.dma("sp", lambda e, wi_=wi_, tok0_=tok0_, T_=T_: e.dma_start(out=rSt[wi_][:, 0:T_], in_=rS[:, tok0_:tok0_ + T_]),
                          writes=[("rSt", wi_)])
                def tile(w, part, inject=None):
                    T = 512 if w < NW else 128
                    nsub = T // 128
                    tok0 = w * 512 if w < NW else LAT
                    r = 0 if w < NW else 1
                    wi = w % 2
                    hkeys = [("hT", wi, k) for k in range(8)]
                    if part == "front":
                        xts = xts_by_w[w]
                        for jj in range(nsub):
                            xi = xts[jj]
                            R.act(lambda e, xi=xi, jj=jj, wi=wi: e.activation(
                                out=sqj[:], in_=xt[xi][:], func=AF.Square, accum_out=ssq[wi][:, jj:jj + 1]),
                                reads=[("xt", xi)], writes=["sqj", ("ssq", wi, jj)])
                        sskeys = [("ssq", wi, jj) for jj in range(nsub)]
                        R.act(lambda e, wi=wi, nsub=nsub: e.activation(
                            out=lnv[wi][:, 0:nsub], in_=ssq[wi][:, 0:nsub], func=AF.Ln, bias=epsb[:], scale=1.0 / D),
                            reads=sskeys + ["eps"], writes=[("lnv", wi)])
                        R.act(lambda e, wi=wi, nsub=nsub: e.activation(
                            out=rst[wi][:, 0:nsub], in_=lnv[wi][:, 0:nsub], func=AF.Exp, scale=-0.5),
                            reads=[("lnv", wi)], writes=[("rst", wi)])
                        for jj in range(nsub):
                            R.act(lambda e, xi=xts[jj], jj=jj, wi=wi: e.activation(
                                out=xn[jj][:], in_=xt[xi][:], func=AF.Copy, scale=rst[wi][:, jj:jj + 1]),
                                reads=[("xt", xts[jj]), ("rst", wi)], writes=[("xn", jj)])
                        if PS < 3:
                            return
                        for kk in range(4):
                            ti = cnt["tp"] % 2
                            cnt["tp"] += 1

                            def trs(e, kk=kk, ti=ti, nsub=nsub):
                                ins = None
                                for half in range(2):
                                    k = 2 * kk + half
                                    for jj in range(nsub):
                                        ins = e.transpose(tp[ti][:, half * 512 + jj * 128: half * 512 + (jj + 1) * 128],
                                                          xn[jj][:, k * 128:(k + 1) * 128], ident[:])
                                return ins
                            R.pe(trs, reads=[("xn", jj) for jj in range(nsub)] + ["ident"], writes=[("tp", ti)])
                            for half in range(2):
                                k = 2 * kk + half
                                R.dve(lambda e, k=k, ti=ti, half=half, wi=wi, T=T, r=r: e.tensor_scalar(
                                    out=hT[wi][:, k, 0:T], in0=tp[ti][:, half * 512: half * 512 + T],
                                    scalar1=Acol[r][:, k:k + 1], scalar2=Scol[r][:, k:k + 1],
                                    op0=ALU.mult, op1=ALU.add),
                                    reads=[("tp", ti), ("Acol", r), ("Scol", r)], writes=[("hT", wi, k)])
                        return

                    if PS < 4:
                        return
                    chunks = []
                    qk = [("q", c, colq + c * 128) for c in range(NQC)] + [("k", c, colk + c * 128) for c in range(NKC)]
                    zc = [("z", c, colz + c * 128) for c in range(8)]
                    per = max(1, len(qk) // len(zc))
                    while qk or zc:
                        for _ in range(per):
                            if qk:
                                chunks.append(qk.pop(0))
                        if zc:
                            chunks.append(zc.pop(0))

                    def stage1(ch):
                        kind, c, col0 = ch
                        bi = cnt["pm"] % 4
                        cnt["pm"] += 1

                        def mm(e, col0=col0, bi=bi, wi=wi, T=T):
                            ins = None
                            for k in range(8):
                                ins = e.matmul(pm[bi][:, 0:T], wbf[:, k, col0:col0 + 128], hT[wi][:, k, 0:T],
                                               start=(k == 0), stop=(k == 7))
                            return ins
                        R.pe(mm, reads=hkeys + wkeys, writes=[("pm", bi)])
                        return bi

                    def stage2(ch, bi):
                        kind, c, col0 = ch
                        oi = cnt["oc"] % NOC
                        cnt["oc"] += 1
                        if kind == "z":
                            R.act(lambda e, bi=bi, oi=oi, T=T: e.activation(out=oc[oi][:, 0:T], in_=pm[bi][:, 0:T], func=AF.Silu),
                                  reads=[("pm", bi)], writes=[("oc", oi)])
                            dst = zT[l]
                            dkey = ("zT", c, w)
                        else:
                            ri = cnt["rp"] % 2
                            cnt["rp"] += 1
                            if is_a:
                                srcap = pm[bi]
                                skey = ("pm", bi)
                            else:
                                b = cnt["b"] % 2
                                cnt["b"] += 1
                                gi = 0 if kind == "q" else 1
                                R.act(lambda e, bi=bi, b=b, T=T: e.activation(out=sqb[b][:, 0:T], in_=pm[bi][:, 0:T], func=AF.Square),
                                      reads=[("pm", bi)], writes=[("sqb", b)])
                                R.pe(lambda e, b=b, T=T: e.matmul(pss[b][:, 0:T], ones_bf[:], sqb[b][:, 0:T], start=True, stop=True),
                                     reads=[("sqb", b), "ones"], writes=[("pss", b)])
                                R.act(lambda e, b=b, T=T: e.activation(out=lnt[b][:, 0:T], in_=pss[b][:, 0:T], func=AF.Ln,
                                                                      bias=epsb[:], scale=1.0 / 128),
                                      reads=[("pss", b), "eps"], writes=[("lnt", b)])
                                R.act(lambda e, b=b, T=T: e.activation(out=rsb[b][:, 0:T], in_=lnt[b][:, 0:T], func=AF.Exp, scale=-0.5),
                                      reads=[("lnt", b)], writes=[("rsb", b)])
                                R.dve(lambda e, b=b, bi=bi, gi=gi, T=T: e.scalar_tensor_tensor(
                                    out=qn[b][:, 0:T], in0=pm[bi][:, 0:T], scalar=gqk[:, gi:gi + 1], in1=rsb[b][:, 0:T],
                                    op0=ALU.mult, op1=ALU.mult),
                                    reads=[("pm", bi), ("rsb", b), ("gqk", gi)], writes=[("qn", b)])
                                srcap = qn[b]
                                skey = ("qn", b)
                            R.dve(lambda e, srcap=srcap, ri=ri, T=T: e.tensor_copy(out=swt[ri][0:64, 0:T], in_=srcap[64:128, 0:T]),
                                  reads=[skey], writes=[("swt", ri, 0)])
                            R.dve(lambda e, srcap=srcap, ri=ri, T=T: e.tensor_copy(out=swt[ri][64:128, 0:T], in_=srcap[0:64, 0:T]),
                                  reads=[skey], writes=[("swt", ri, 1)])
                            R.dve(lambda e, srcap=srcap, ri=ri, wi=wi, T=T: e.tensor_tensor(
                                out=t1[ri][:, 0:T], in0=srcap[:, 0:T], in1=rCt[wi][:, 0:T], op=ALU.mult),
                                reads=[skey, ("rCt", wi)], writes=[("t1", ri)])
                            R.pool(lambda e, ri=ri, wi=wi, T=T: e.tensor_tensor(
                                out=t2[ri][:, 0:T], in0=swt[ri][:, 0:T], in1=rSt[wi][:, 0:T], op=ALU.mult),
                                reads=[("swt", ri, 0), ("swt", ri, 1), ("rSt", wi)], writes=[("t2", ri)])
                            R.pool(lambda e, ri=ri, oi=oi, T=T: e.tensor_tensor(
                                out=oc[oi][:, 0:T], in0=t1[ri][:, 0:T], in1=t2[ri][:, 0:T], op=ALU.add),
                                reads=[("t1", ri), ("t2", ri)], writes=[("oc", oi)])
                            dst = qT[l] if kind == "q" else kTb[l]
                            dkey = ("qT" if kind == "q" else "kTb", c, w)
                        R.dma("sp", lambda e, dst=dst, c=c, oi=oi, tok0=tok0, T=T: e.dma_start(
                            out=dst[c * 128:(c + 1) * 128, tok0:tok0 + T], in_=oc[oi][:, 0:T]),
                            reads=[("oc", oi)], writes=[dkey])

                    prev = None
                    for ci, ch in enumerate(chunks):
                        bi = stage1(ch)
                        if prev is not None:
                            stage2(*prev)
                        prev = (ch, bi)
                        if inject is not None and ci == len(chunks) // 3:
                            inject()
                    vprev = None
                    for jj in range(nsub if PS >= 6 else 0):
                        vi = cnt["vo"] % 2
                        cnt["vo"] += 1
                        for c0 in range(0, FV, 512):
                            cw = min(512, FV - c0)
                            bi = cnt["pm"] % 4
                            cnt["pm"] += 1

                            def mmv(e, jj=jj, c0=c0, cw=cw, bi=bi, wi=wi):
                                ins = None
                                for k in range(8):
                                    ins = e.matmul(pm[bi][:, 0:cw], hT[wi][:, k, jj * 128:(jj + 1) * 128],
                                                   wbf[:, k, colv + c0: colv + c0 + cw], start=(k == 0), stop=(k == 7))
                                return ins
                            R.pe(mmv, reads=hkeys + wkeys, writes=[("pm", bi)])
                            if prev is not None:
                                stage2(*prev)
                                prev = None
                            if os.environ.get('KDEBUG_VCOPY', 'act') == 'dve':
                                R.dve(lambda e, vi=vi, c0=c0, cw=cw, bi=bi: e.tensor_copy(out=vo[vi][:, c0:c0 + cw], in_=pm[bi][:, 0:cw]),
                                      reads=[("pm", bi)], writes=[("vo", vi, c0)])
                            else:
                                R.act(lambda e, vi=vi, c0=c0, cw=cw, bi=bi: e.copy(out=vo[vi][:, c0:c0 + cw], in_=pm[bi][:, 0:cw]),
                                      reads=[("pm", bi)], writes=[("vo", vi, c0)])
                        R.dma("sp", lambda e, vi=vi, jj=jj, tok0=tok0: e.dma_start(
                            out=vb[l].rearrange("(h t) d -> t h d", h=NVH)[tok0 + jj * 128: tok0 + (jj + 1) * 128, :, :],
                            in_=vo[vi][:].rearrange("p (h d) -> p h d", h=NVH)),
                            reads=[("vo", vi, c0) for c0 in range(0, FV, 512)], writes=[("vb", w, jj)])
                    if prev is not None:
                        stage2(*prev)
                        prev = None
                if PS >= 2:
                    issue_loads(0)
                    tile(0, "front")
                    for w in range(NW + 1):
                        if w + 1 <= NW:
                            issue_loads(w + 1)
                        tile(w, "back", inject=(lambda w=w: tile(w + 1, "front")) if w + 1 <= NW else None)
                R.flush()

            if not run_ph('X'):
                continue
            RG = [[2 * p, 2 * p + 1] for p in range(NCORES // 2)]
            for c in range(max(NKC, NVH)):
                if c < NKC:
                    R.op("pool", lambda e, c=c: e.collective_compute(
                        "AllGather", ALU.bypass, replica_groups=RG,
                        ins=[kTb[l][c * 128:(c + 1) * 128, :].opt()], outs=[kTg[l][c * 256:(c + 1) * 256, :].opt()]),
                        writes=[("kTg", c)], kind="cc")
                if c < NVH:
                    R.op("pool", lambda e, c=c: e.collective_compute(
                        "AllGather", ALU.bypass, replica_groups=RG,
                        ins=[vb[l][c * NTOK:(c + 1) * NTOK, :].opt()], outs=[vg[l][c * 2 * NTOK:(c + 1) * 2 * NTOK, :].opt()]),
                        writes=[("vg", c)], kind="cc")
            if cfg.stop == 'X':
                R.flush()
                R.op("pool", lambda e: e.wait_ge(cc_sem, R.cc_n), kind="c")
                R.flush()

            if not run_ph('A'):
                continue
            with contextlib.ExitStack() as st:
                NU = 2 if is_a else 1
                kTs = [sb(st, f"kTs{i}", [128, 2 * NTOK], BF16) for i in range(2)]
                vs = [sb(st, f"vs{i}", [128, NKT, 128], BF16) for i in range(2)]
                qs = [[sb(st, f"qs{i}_{s}", [128, NTOK], BF16) for s in range(NU)] for i in range(2)]
                zs = [sb(st, f"zs{i}", [128, NTOK], BF16) for i in range(2)]
                NPT = 8
                pt = [sb(st, f"pt{i}", [128, 1024], BF16) for i in range(NPT)]
                accS = [sb(st, f"accS{i}", [128, 512], BF16) for i in range(2)]
                accP = [sb(st, f"accP{i}", [128, 1024], BF16) for i in range(2)]
                rr = [sb(st, f"rr{i}", [128, 512], F32) for i in range(2)]
                o_s = [sb(st, f"o_s{i}", [128, 512], F32) for i in range(2)]
                ocmb = sb(st, "ocmb", [128, 512], F32)
                sqe = sb(st, "sqe", [128, 512], BF16)
                sqo = [sb(st, f"sqo{i}", [128, 512], BF16) for i in range(2)]
                lno = sb(st, "lno", [128, 512], F32)
                rso = sb(st, "rso", [128, 512], F32)
                ono = sb(st, "ono", [128, 512], F32)
                ogs = [sb(st, f"ogs{i}", [128, 512], BF16) for i in range(2)]
                negM = [sb(st, f"negM{i}", [128, 1], F32) for i in range(4)]
                stt = sb(st, "stt", [128, 8], F32)
                stg = sb(st, "stg", [128, 64], F32)
                kmx = [sb(st, f"kmx{i}", [128, 2], F32) for i in range(2)]
                qmx = sb(st, "qmx", [128, 2], F32)
                if is_a:
                    lamt = sb(st, "lamt", [128, 256], F32)
                    lamj = sb(st, "lamj", [128, 64], F32)
                    lams = sb(st, "lams", [128, 4], F32)
                    neglam = sb(st, "neglam", [128, 1], F32)
                    gsub = sb(st, "gsub", [128, 1], F32)
                Sg = [ps(st, f"Sg{i}", [128, 1024]) for i in range(2)]
                Ob = [ps(st, f"Ob{i}", [128, 512]) for i in range(2)]
                Lb = ps(st, "Lb", [128, 512])
                accPS = ps(st, "accPS", [128, 512])
                pstat = [(Sg[0][:, 0:512], ("S", 0, 0)), (Sg[0][:, 512:1024], ("S", 0, 1)),
                         (Sg[1][:, 0:512], ("S", 1, 0)), (Sg[1][:, 512:1024], ("S", 1, 1))]

                scale = (64 ** -0.5) if is_a else (128 ** -0.5)
                if is_a:
                    lam_init = lambda_init_fn(l)
                    for i in range(2):
                        for s in range(2):
                            R.dve(lambda e, i=i, s=s: e.memset(qs[i][s][:], 0.0), writes=[("qs", i, s), ("qs2", i, s)])
                    R.dma("sp", lambda e: e.dma_start(out=lamt[:], in_=a_lambda[j, :, :]), writes=["lamt"])
                    R.dma("sp", lambda e: e.dma_start(out=gsub[:], in_=a_subln[j, :, :]), writes=["gsub0"])
                    for i in range(2):
                        R.dve(lambda e, i=i: e.scalar_tensor_tensor(
                            out=lamj[:], in0=lamt[:, (2 * i) * 64:(2 * i + 1) * 64], scalar=1.0,
                            in1=lamt[:, (2 * i + 1) * 64:(2 * i + 2) * 64], op0=ALU.mult, op1=ALU.mult,
                            accum_out=lams[:, i:i + 1]), reads=["lamt"], writes=["lamj", ("lams", i)])
                    R.act(lambda e: e.activation(out=lams[:, 2:4], in_=lams[:, 0:2], func=AF.Exp),
                          reads=[("lams", 0), ("lams", 1)], writes=["lame"])
                    R.dve(lambda e: e.scalar_tensor_tensor(out=neglam[:], in0=lams[:, 3:4], scalar=-lam_init,
                                                           in1=lams[:, 2:3], op0=ALU.add, op1=ALU.subtract),
                          reads=["lame"], writes=["neglam"])
                    R.dve(lambda e: e.tensor_scalar(out=gsub[:], in0=gsub[:], scalar1=(1.0 - lam_init), scalar2=None,
                                                    op0=ALU.mult), reads=["gsub0"], writes=["gsub"])

                FAST_RECIP = os.environ.get("KDEBUG_FASTRECIP", "0") == "1"
                pending = []
                state = {"unit": 0, "og": 0, "kvslot": -1, "kvhead": -1, "nm": 0, "sq": 0, "g": 0, "pb": 0}

                def drain(n=None, upto=None):
                    k = len(pending) if n is None else min(n, len(pending))
                    for _ in range(k):
                        if upto is not None and pending[0][0] > upto:
                            break
                        pending.pop(0)[1]()

                def defer(fn):
                    pending.append((state["unit"] - 1, fn))

                def flush_tail():
                    if state.get("tail") is not None:
                        t_ = state["tail"]
                        state["tail"] = None
                        t_()

                def maxsq(srct, ncols, rkeys, outs):
                    nch = 0
                    for c0 in range(0, ncols, 512):
                        cw = min(512, ncols - c0)
                        b = state["sq"] % 2
                        state["sq"] += 1
                        R.act(lambda e, b=b, c0=c0, cw=cw: e.activation(out=sqo[b][:, 0:cw], in_=srct[:, c0:c0 + cw], func=AF.Square),
                              reads=rkeys, writes=[("sqo", b)])
                        for oi_, (ind, dst, dkey) in enumerate(outs):
                            pb, pkey = pstat[state["pb"] % 4]
                            state["pb"] += 1
                            R.pe(lambda e, b=b, cw=cw, pb=pb, ind=ind: e.matmul(pb[:, 0:cw], ind, sqo[b][:, 0:cw], start=True, stop=True),
                                 reads=[("sqo", b)], writes=[pkey])
                            col = oi_ * 20 + nch
                            R.dve(lambda e, cw=cw, pb=pb, col=col: e.tensor_reduce(out=stg[:, col:col + 1], in_=pb[:, 0:cw], axis=AX.X, op=ALU.max),
                                  reads=[pkey], writes=[("stg", col)])
                        nch += 1
                    for oi_, (ind, dst, dkey) in enumerate(outs):
                        R.dve(lambda e, nch=nch, oi_=oi_, dst=dst: e.tensor_reduce(out=dst, in_=stg[:, oi_ * 20: oi_ * 20 + nch], axis=AX.X, op=ALU.max),
                              reads=[("stg", oi_ * 20 + i) for i in range(nch)], writes=[dkey])

                NH = 8
                for h in range(NH):
                    hs = h % 2
                    hk = h if is_a else h // 4
                    if hk != state["kvhead"]:
                        state["kvhead"] = hk
                        state["kvslot"] = (state["kvslot"] + 1) % 2
                        ks = state["kvslot"]
                        for rk in range(2):
                            R.dma("sp", lambda e, ks=ks, rk=rk, hk=hk: e.dma_start(
                                out=kTs[ks][:, rk * NTOK:(rk + 1) * NTOK],
                                in_=kTg[l][hk * 256 + rk * 128: hk * 256 + (rk + 1) * 128, :]),
                                reads=[("kTg", hk)], writes=[("kTs", ks, rk)])
                            vgv = vg[l][hk * 2 * NTOK:(hk + 1) * 2 * NTOK, :].rearrange("(kt p) f -> p kt f", p=128)
                            for part in range(4):
                                k0 = rk * KT_R + (KT_R * part) // 4
                                k1 = rk * KT_R + (KT_R * (part + 1)) // 4
                                if k1 > k0:
                                    R.dma("sp", lambda e, ks=ks, k0=k0, k1=k1, vgv=vgv: e.dma_start(
                                        out=vs[ks][:, k0:k1, :], in_=vgv[:, k0:k1, :]),
                                        reads=[("vg", hk)], writes=[("vs", ks, rk, part)])
                        if cfg.stab:
                            maxsq(kTs[ks], 2 * NTOK, [("kTs", ks, 0), ("kTs", ks, 1)],
                                  [((indm[:, s, :] if is_a else ones_bf[:]), kmx[ks][:, s:s + 1], ("kmx", ks, s)) for s in range(NU)])
                    ks = state["kvslot"]
                    if is_a:
                        for s in range(2):
                            for half in range(2):
                                p0 = 64 * half + 32 * s
                                R.dma("sp", lambda e, hs=hs, s=s, p0=p0, h=h: e.dma_start(
                                    out=qs[hs][s][p0:p0 + 32, :], in_=qT[l][h * 128 + p0: h * 128 + p0 + 32, :]),
                                    writes=[("qs", hs, s)] if half == 0 else [("qs2", hs, s)])
                    else:
                        R.dma("sp", lambda e, hs=hs, h=h: e.dma_start(out=qs[hs][0][:], in_=qT[l][h * 128:(h + 1) * 128, :]),
                              writes=[("qs", hs, 0)])
                    R.dma("sp", lambda e, hs=hs, h=h: e.dma_start(out=zs[hs][:], in_=zT[l][h * 128:(h + 1) * 128, :]),
                          writes=[("zs", hs)])

                    nm = []
                    for s in range(NU):
                        mi = state["nm"] % 4
                        state["nm"] += 1
                        nm.append(mi)
                        if cfg.stab:
                            maxsq(qs[hs][s], NTOK, [("qs", hs, s), ("qs2", hs, s)], [(ones_bf[:], qmx[:, s:s + 1], ("qmx", s))])
                            R.dve(lambda e, ks=ks, s=s: e.tensor_tensor(out=stt[:, 3:4], in0=kmx[ks][:, s:s + 1], in1=qmx[:, s:s + 1], op=ALU.mult),
                                  reads=[("kmx", ks, s), ("qmx", s)], writes=[("stt", 3)])
                            R.act(lambda e: e.activation(out=stt[:, 4:5], in_=stt[:, 3:4], func=AF.Ln, bias=epsb[:], scale=1.0),
                                  reads=[("stt", 3), "eps"], writes=[("stt", 4)])
                            R.act(lambda e: e.activation(out=stt[:, 5:6], in_=stt[:, 4:5], func=AF.Exp, scale=0.5),
                                  reads=[("stt", 4)], writes=[("stt", 5)])
                            R.dve(lambda e, mi=mi: e.tensor_scalar(out=negM[mi][:], in0=stt[:, 5:6], scalar1=-scale, scalar2=None, op0=ALU.mult),
                                  reads=[("stt", 5)], writes=[("negM", mi)])
                        else:
                            R.dve(lambda e, mi=mi: e.memset(negM[mi][:], 0.0), writes=[("negM", mi)])

                    def vpart(kt):
                        rk = kt // KT_R
                        for pp in range(4):
                            k0 = rk * KT_R + (KT_R * pp) // 4
                            k1 = rk * KT_R + (KT_R * (pp + 1)) // 4
                            if k0 <= kt < k1:
                                return rk, pp
                        return rk, 3

                    nqt = NW + (0 if last else 1)
                    for w in range(nqt):
                        T = 512 if w < NW else 128
                        tok0 = w * 512 if w < NW else LAT
                        ktl = list(range(NKT)) if w < NW else [KT_R - 1, 2 * KT_R - 1]
                        groups = [ktl[i:i + 2] for i in range(0, len(ktl), 2)]
                        if w >= NW:
                            flush_tail()
                            drain()
                        for s in range(NU):
                            ob = state["unit"] % 2
                            ab = ob
                            state["unit"] += 1
                            qsk = [("qs", hs, s), ("qs2", hs, s)] if is_a else [("qs", hs, s)]
                            mi = nm[s]

                            def QK(gi, grp, s=s, T=T, tok0=tok0, ks=ks, hs=hs, qsk=qsk):
                                for a, kt in enumerate(grp):
                                    R.pe(lambda e, gi=gi, a=a, kt=kt: e.matmul(
                                        Sg[gi][:, a * 512: a * 512 + T], kTs[ks][:, kt * 128:(kt + 1) * 128],
                                        qs[hs][s][:, tok0:tok0 + T], start=True, stop=True),
                                        reads=[("kTs", ks, kt // KT_R)] + qsk, writes=[("S", gi, a)])
                            ng = len(groups)
                            drain(upto=state["unit"] - 3)
                            step = max(1, (ng - 1) // (len(pending) + 1))
                            used = {"d": False, "p": False}
                            QK(state["g"] % 2, groups[0])
                            flush_tail()
                            prevPV = None
                            for gidx, grp in enumerate(groups):
                                g = state["g"]
                                state["g"] += 1
                                gi = g % 2
                                pi = g % NPT
                                na = len(grp)
                                if gidx + 1 < ng:
                                    QK((g + 1) % 2, groups[gidx + 1])
                                S3 = Sg[gi][:].rearrange("p (a t) -> p a t", a=2)[:, 0:na, 0:T]
                                P3 = pt[pi][:].rearrange("p (a t) -> p a t", a=2)[:, 0:na, 0:T]
                                R.act(lambda e, S3=S3, P3=P3, mi=mi: e.activation(out=P3, in_=S3, func=AF.Exp, bias=negM[mi][:], scale=scale),
                                      reads=[("S", gi, a) for a in range(na)] + [("negM", mi)], writes=[("pt", pi)])
                                if prevPV is not None:
                                    prevPV()

                                def PV(grp=grp, gidx=gidx, na=na, pi=pi, ob=ob, T=T, ks=ks, ng=ng):
                                    for a, kt in enumerate(grp):
                                        rk, part = vpart(kt)
                                        first = (gidx == 0 and a == 0)
                                        lastmm = (gidx == ng - 1 and a == na - 1)
                                        R.pe(lambda e, a=a, kt=kt, first=first, lastmm=lastmm: e.matmul(
                                            Ob[ob][:, 0:T], vs[ks][:, kt, :], pt[pi][:, a * 512: a * 512 + T], start=first, stop=lastmm),
                                            reads=[("vs", ks, rk, part), ("pt", pi)], writes=[("O", ob)])
                                prevPV = PV
                                use_d = (gidx % 2 == 0)
                                if use_d:
                                    for a in range(na):
                                        src_ = pt[pi][:, a * 512: a * 512 + T]
                                        if not used["d"]:
                                            used["d"] = True
                                            R.dve(lambda e, src_=src_, T=T: e.tensor_copy(out=accPS[:, 0:T], in_=src_),
                                                  reads=[("pt", pi)], writes=["accPS"])
                                        else:
                                            R.dve(lambda e, src_=src_, T=T: e.tensor_tensor(out=accPS[:, 0:T], in0=accPS[:, 0:T], in1=src_, op=ALU.add),
                                                  reads=[("pt", pi), "accPS"], writes=["accPS"])
                                else:
                                    A3 = accP[ab][:].rearrange("p (a t) -> p a t", a=2)[:, 0:na, 0:T]
                                    akey = ("accP", ab)
                                    if not used["p"]:
                                        used["p"] = True
                                        R.pool(lambda e, A3=A3, P3=P3: e.tensor_copy(out=A3, in_=P3), reads=[("pt", pi)], writes=[akey])
                                    else:
                                        R.pool(lambda e, A3=A3, P3=P3: e.tensor_tensor(out=A3, in0=A3, in1=P3, op=ALU.add),
                                               reads=[("pt", pi), akey], writes=[akey])
                                if gidx >= 1 and gidx % step == 0:
                                    drain(1)
                            state["tail"] = prevPV
                            na0 = len(groups[0])
                            srcs = []
                            if used["d"]:
                                R.dve(lambda e, ab=ab, T=T: e.tensor_copy(out=accS[ab][:, 0:T], in_=accPS[:, 0:T]),
                                      reads=["accPS"], writes=[("accS", ab)])
                                srcs += [(accS[ab], ("accS", ab), 0)]
                            if used["p"]:
                                srcs += [(accP[ab], ("accP", ab), a) for a in range(na0)]

                            def Lstage(srcs=srcs, T=T):
                                def mmL(e):
                                    ins = None
                                    for i, (t_, k_, a) in enumerate(srcs):
                                        ins = e.matmul(Lb[:, 0:T], ones_bf[:], t_[:, a * 512: a * 512 + T],
                                                       start=(i == 0), stop=(i == len(srcs) - 1))
                                    return ins
                                R.pe(mmL, reads=list({k_ for (_, k_, _) in srcs}), writes=["Lb"])
                            defer(Lstage)
                            defer(lambda s=s, T=T: R.dve(
                                lambda e: (e.reciprocal_approx_fast(out=rr[s][:, 0:T], in_=Lb[:, 0:T]) if FAST_RECIP
                                           else e.reciprocal(out=rr[s][:, 0:T], in_=Lb[:, 0:T])),
                                reads=["Lb"], writes=[("rr", s)]))
                            defer(lambda ob=ob, s=s, T=T: R.dve(
                                lambda e: e.tensor_tensor(out=o_s[s][:, 0:T], in0=Ob[ob][:, 0:T], in1=rr[s][:, 0:T], op=ALU.mult),
                                reads=[("O", ob), ("rr", s)], writes=[("o_s", s)]))
                        oi = state["og"] % 2
                        state["og"] += 1
                        if is_a:
                            defer(lambda T=T: R.dve(
                                lambda e: e.scalar_tensor_tensor(out=ocmb[:, 0:T], in0=o_s[1][:, 0:T], scalar=neglam[:],
                                                                 in1=o_s[0][:, 0:T], op0=ALU.mult, op1=ALU.add),
                                reads=[("o_s", 0), ("o_s", 1), "neglam"], writes=["ocmb"]))
                            defer(lambda T=T: R.act(
                                lambda e: e.activation(out=sqe[:, 0:T], in_=ocmb[:, 0:T], func=AF.Square),
                                reads=["ocmb"], writes=["sqe"]))
                            defer(lambda T=T: R.pe(
                                lambda e: e.matmul(Lb[:, 0:T], ones_bf[:], sqe[:, 0:T], start=True, stop=True),
                                reads=["sqe"], writes=["Lb"]))
                            defer(lambda T=T: R.act(
                                lambda e: e.activation(out=lno[:, 0:T], in_=Lb[:, 0:T], func=AF.Ln, bias=epsb[:], scale=1.0 / 128),
                                reads=["Lb"], writes=["lno"]))
                            defer(lambda T=T: R.act(
                                lambda e: e.activation(out=rso[:, 0:T], in_=lno[:, 0:T], func=AF.Exp, scale=-0.5),
                                reads=["lno"], writes=["rso"]))
                            defer(lambda T=T: R.dve(
                                lambda e: e.scalar_tensor_tensor(out=ono[:, 0:T], in0=ocmb[:, 0:T], scalar=gsub[:],
                                                                 in1=rso[:, 0:T], op0=ALU.mult, op1=ALU.mult),
                                reads=["ocmb", "rso", "gsub"], writes=["ono"]))
                            fin_src, fin_key = ono, "ono"
                        else:
                            fin_src, fin_key = o_s[0], ("o_s", 0)
                        defer(lambda T=T, oi=oi, tok0=tok0, fin_src=fin_src, fin_key=fin_key, hs=hs: R.dve(
                            lambda e: e.tensor_tensor(out=ogs[oi][:, 0:T], in0=fin_src[:, 0:T], in1=zs[hs][:, tok0:tok0 + T], op=ALU.mult),
                            reads=[fin_key, ("zs", hs)], writes=[("ogs", oi)]))
                        defer(lambda T=T, oi=oi, tok0=tok0, h=h, w=w: R.dma(
                            "pool", lambda e: e.dma_start(out=ogT[l][h * 128:(h + 1) * 128, tok0:tok0 + T], in_=ogs[oi][:, 0:T]),
                            reads=[("ogs", oi)], writes=[("ogT", h, w)]))
                flush_tail()
                drain()
                R.flush()

            if not run_ph('O'):
                continue
            with contextlib.ExitStack() as st:
                wo = sb(st, "wo", [128, 8, D], BF16)
                og = [sb(st, f"og{i}", [128, 8, 512], BF16) for i in range(2)]
                NXO = 6
                xo = [sb(st, f"xo{i}", [128, D], F32) for i in range(NXO)]
                yt = [sb(st, f"yt{i}", [128, D], F32) for i in range(2)]
                xw = [sb(st, f"xw{i}", [128, D], F32) for i in range(3)]
                sqy = sb(st, "sqy", [128, 512], BF16)
                ssy = [sb(st, f"ssy{i}", [128, 4], F32) for i in range(2)]
                py = [ps(st, f"py{i}", [128, 512]) for i in range(4)]
                wov = w_out[l].rearrange("(k p) n -> p k n", p=128)
                for k in range(8):
                    R.dma("pool", lambda e, k=k: e.dma_start(out=wo[:, k, :], in_=wov[:, k, :]), writes=[("wo", k)])
                wokeys = [("wo", k) for k in range(8)]
                ogv = ogT[l].rearrange("(k p) t -> p k t", p=128)
                tcount = 0
                nwt = NW + (0 if last else 1)
                for w in range(nwt):
                    T = 512 if w < NW else 128
                    nsub = T // 128
                    tok0 = w * 512 if w < NW else LAT
                    r = 0 if w < NW else 1
                    wi = w % 2
                    R.dma("sp", lambda e, wi=wi, tok0=tok0, T=T: e.dma_start(out=og[wi][:, :, 0:T], in_=ogv[:, :, tok0:tok0 + T]),
                          writes=[("og", wi)])
                    for jj in range(nsub):
                        xi = tcount % NXO
                        yi = tcount % 2
                        xwi = tcount % 3
                        tcount += 1
                        t0 = tok0 + jj * 128
                        R.dma("sp", lambda e, xi=xi, t0=t0: e.dma_start(out=xo[xi][:], in_=src[t0:t0 + 128, :]), writes=[("xo", xi)])
                        for nh in range(2):
                            bi = (2 * yi + nh)

                            def mmo(e, nh=nh, jj=jj, wi=wi, bi=bi):
                                ins = None
                                for k in range(8):
                                    ins = e.matmul(py[bi][:], og[wi][:, k, jj * 128:(jj + 1) * 128], wo[:, k, nh * 512:(nh + 1) * 512],
                                                   start=(k == 0), stop=(k == 7))
                                return ins
                            R.pe(mmo, reads=[("og", wi)] + wokeys, writes=[("py", bi)])
                            R.act(lambda e, bi=bi, yi=yi, nh=nh: e.activation(out=sqy[:], in_=py[bi][:], func=AF.Square,
                                                                              accum_out=ssy[yi][:, nh:nh + 1]),
                                  reads=[("py", bi)], writes=["sqy", ("ssy", yi, nh)])
                        R.dve(lambda e, yi=yi: e.tensor_tensor(out=ssy[yi][:, 2:3], in0=ssy[yi][:, 0:1], in1=ssy[yi][:, 1:2], op=ALU.add),
                              reads=[("ssy", yi, 0), ("ssy", yi, 1)], writes=[("ssy", yi, 2)])
                        R.act(lambda e, yi=yi: e.activation(out=ssy[yi][:, 3:4], in_=ssy[yi][:, 2:3], func=AF.Ln, bias=epsb[:], scale=1.0 / D),
                              reads=[("ssy", yi, 2), "eps"], writes=[("ssy", yi, 3)])
                        R.act(lambda e, yi=yi: e.activation(out=ssy[yi][:, 2:3], in_=ssy[yi][:, 3:4], func=AF.Exp, scale=-0.5),
                              reads=[("ssy", yi, 3)], writes=[("ssy", yi, 4)])
                        for nh in range(2):
                            bi = (2 * yi + nh)
                            R.dve(lambda e, bi=bi, yi=yi, nh=nh, r=r: e.scalar_tensor_tensor(
                                out=yt[yi][:, nh * 512:(nh + 1) * 512], in0=py[bi][:], scalar=ssy[yi][:, 2:3],
                                in1=Gb[r][:, nh * 512:(nh + 1) * 512], op0=ALU.mult, op1=ALU.mult),
                                reads=[("py", bi), ("ssy", yi, 4), ("Gb", r)], writes=[("yt", yi, nh)])
                        R.dve(lambda e, yi=yi, xi=xi, xwi=xwi: e.tensor_tensor(out=xw[xwi][:], in0=yt[yi][:], in1=xo[xi][:], op=ALU.add),
                              reads=[("yt", yi, 0), ("yt", yi, 1), ("xo", xi)], writes=[("xw", xwi)])
                        if last:
                            R.dma("pool", lambda e, xwi=xwi, t0=t0: e.dma_start(out=out[t0:t0 + 128, :], in_=xw[xwi][:]),
                                  reads=[("xw", xwi)], writes=[("out", t0)])
                        else:
                            R.dma("pool", lambda e, xwi=xwi, t0=t0: e.dma_start(out=xs[t0:t0 + 128, :], in_=xw[xwi][:]),
                                  reads=[("xw", xwi)], writes=[("xs", t0)])
                R.flush()
    return nc


def _rope_tables(cfg, hf, head_dim, dup):
    LAT, NTOK = cfg.LAT, cfg.NTOK
    t = np.arange(hf * LAT, (hf + 1) * LAT)
    rows = (t // GRID_W).astype(np.float32)
    cols = (t % GRID_W).astype(np.float32)
    axis_dim = head_dim // 2
    freqs = (ROPE_THETA ** (-np.arange(0, axis_dim, 2, dtype=np.float32) / np.float32(axis_dim))).astype(np.float32)
    ang = np.concatenate([rows[:, None] * freqs, cols[:, None] * freqs], axis=-1).astype(np.float32)
    cos = np.cos(ang).astype(np.float32).T
    sin = np.sin(ang).astype(np.float32).T
    half = head_dim // 2
    C = np.ones((128, NTOK), np.float32)
    S = np.zeros((128, NTOK), np.float32)
    if dup:
        for blk in range(4):
            C[blk * 32:(blk + 1) * 32, :LAT] = cos
            S[blk * 32:(blk + 1) * 32, :LAT] = -sin if blk < 2 else sin
    else:
        C[0:64, :LAT] = cos
        C[64:128, :LAT] = cos
        S[0:64, :LAT] = -sin
        S[64:128, :LAT] = sin
    return C, S


def _perm_a_cols():
    perm = np.arange(4096)
    p128 = np.zeros(128, np.int64)
    for n in range(128):
        blk = n // 32
        s = blk % 2
        d = (n % 32) + (32 if blk >= 2 else 0)
        p128[n] = s * 64 + d
    for base in (0, 1024):
        for h in range(8):
            perm[base + h * 128: base + (h + 1) * 128] = base + h * 128 + p128
    return perm


def make_in_maps(cfg, x, c, ctx, c_ctx, ada_w, ada_b, pre_g, post_g, w_out, a_w_in, a_lambda, a_subln_g, b_w_in, b_qk_g):
    DEPTH = cfg.DEPTH
    f = lambda a: np.ascontiguousarray(np.asarray(a, dtype=np.float32))
    x, c, ctx, c_ctx = f(x), f(c), f(ctx), f(c_ctx)
    NA = (DEPTH + 1) // 2
    NB = max(DEPTH // 2, 1)
    shared = {
        "ada_w": f(ada_w)[:DEPTH],
        "ada_b": np.ascontiguousarray(np.broadcast_to(f(ada_b)[:DEPTH, None, :], (DEPTH, 128, 3 * D))),
        "pre_g": np.ascontiguousarray(np.broadcast_to(f(pre_g)[:DEPTH, None, :], (DEPTH, 128, D))),
        "post_g": np.ascontiguousarray(np.broadcast_to(f(post_g)[:DEPTH, None, :], (DEPTH, 128, D))),
        "w_out": f(w_out)[:DEPTH],
        "a_w_in": np.ascontiguousarray(f(a_w_in)[:NA][:, :, _perm_a_cols()]),
        "a_lambda": np.ascontiguousarray(np.broadcast_to(f(a_lambda)[:NA].reshape(NA, 1, 256), (NA, 128, 256))),
        "a_subln": np.ascontiguousarray(f(a_subln_g)[:NA].reshape(NA, 128, 1)),
        "b_w_in": f(b_w_in)[:NB],
        "b_qk_g": np.ascontiguousarray(f(b_qk_g)[:NB].reshape(NB, 2, 128, 1)),
        "ident": np.eye(128, dtype=np.float32).astype(ml_dtypes.bfloat16),
    }
    ind = np.zeros((2, 128, 128), np.float32)
    for s in range(2):
        for p in range(128):
            if (p // 32) % 2 == s:
                ind[s, p, :] = 1.0
    shared["indmat"] = ind.astype(ml_dtypes.bfloat16)
    maps = []
    for i in range(NCORES):
        b, hf = i // 2, i % 2
        LAT = cfg.LAT
        xin = np.concatenate([x[b, hf * LAT:(hf + 1) * LAT], ctx[b, hf * CTXH:(hf + 1) * CTXH]], axis=0)
        cin = np.stack([c[b].reshape(8, 128).T, c_ctx.reshape(8, 128).T], axis=1)
        ac, as_ = _rope_tables(cfg, hf, 64, True)
        bc, bs = _rope_tables(cfg, hf, 128, False)
        m = dict(shared)
        m.update({"xin": np.ascontiguousarray(xin), "cin": np.ascontiguousarray(cin),
                  "ropeA_C": ac, "ropeA_S": as_, "ropeB_C": bc, "ropeB_S": bs})
        maps.append(m)
    return maps


_CACHE = {}


def run(cfg, inputs, trace=False):
    key = (cfg.SEQ, cfg.DEPTH, cfg.debug, cfg.stab, cfg.stop)
    if key not in _CACHE:
        _CACHE[key] = build_program(cfg)
    nc = _CACHE[key]
    maps = make_in_maps(cfg, **inputs)
    res = run_bass_kernel_spmd(nc, maps, core_ids=list(range(NCORES)))
    return res


def kernel(x, c, ctx, c_ctx, ada_w, ada_b, pre_g, post_g, w_out, a_w_in, a_lambda, a_subln_g, b_w_in, b_qk_g):
    x = np.asarray(x)
    B, S, _ = x.shape
    cfg = Cfg(S, 4)
    res = run(cfg, dict(x=x, c=c, ctx=ctx, c_ctx=c_ctx, ada_w=ada_w, ada_b=ada_b, pre_g=pre_g, post_g=post_g,
                        w_out=w_out, a_w_in=a_w_in, a_lambda=a_lambda, a_subln_g=a_subln_g, b_w_in=b_w_in, b_qk_g=b_qk_g))
    outp = np.empty((B, S, D), np.float32)
    for i in range(NCORES):
        b, hf = i // 2, i % 2
        outp[b, hf * cfg.LAT:(hf + 1) * cfg.LAT] = np.asarray(res.results[i]["out"], dtype=np.float32)
    return outp
```

```python
import math
import numpy as np
import ml_dtypes
import concourse.bass as bass
import concourse.mybir as mybir
from concourse.bass_utils import run_bass_kernel_spmd

F32 = mybir.dt.float32
BF16 = mybir.dt.bfloat16
AF = mybir.ActivationFunctionType
ALU = mybir.AluOpType
AX = mybir.AxisListType

D = 1024
NCORES = 8
CTXH = 128
EPS = 1e-6
ROPE_THETA = 10000.0
GRID_W = 64


def lambda_init_fn(i):
    return 0.8 - 0.6 * math.exp(-0.3 * i)


class Op:
    __slots__ = ("eng", "fn", "deps", "kind", "signal", "val", "sem", "prewait", "embed")

    def __init__(self, eng, fn, deps, kind):
        self.eng = eng
        self.fn = fn
        self.deps = deps
        self.kind = kind
        self.signal = kind != "c"
        self.val = None
        self.sem = None
        self.prewait = None
        self.embed = False


import os as _os
EMBED_WAITS = _os.environ.get("KDEBUG_EMBED", "1") == "1"


class Rec:
    ENGS = ("pe", "act", "dve", "pool", "sp")
    NPOOL = 8

    def __init__(self, nc, sems, dma_sems, cc_sem):
        self.nc = nc
        self.sems = sems
        self.dma_sems = dma_sems
        self.cc_sem = cc_sem
        self.cnt = {e: 0 for e in self.ENGS}
        self.dma_n = {e: 0 for e in self.ENGS}
        self.cc_n = 0
        self.ops = []
        self.last_w = {}
        self.readers = {}
        self.nops = 0

    def op(self, eng, fn, reads=(), writes=(), kind="c"):
        deps = set()
        for k in reads:
            w = self.last_w.get(k)
            if w is not None:
                deps.add(w)
        for k in writes:
            w = self.last_w.get(k)
            if w is not None:
                deps.add(w)
            for r in self.readers.get(k, ()):
                deps.add(r)
        o = Op(eng, fn, deps, kind)
        for k in reads:
            self.readers.setdefault(k, []).append(o)
        for k in writes:
            self.last_w[k] = o
            self.readers[k] = []
        self.ops.append(o)
        return o

    def pe(self, fn, reads=(), writes=()):
        return self.op("pe", fn, reads, writes)

    def act(self, fn, reads=(), writes=()):
        return self.op("act", fn, reads, writes)

    def dve(self, fn, reads=(), writes=()):
        return self.op("dve", fn, reads, writes)

    def pool(self, fn, reads=(), writes=()):
        return self.op("pool", fn, reads, writes)

    def dma(self, eng, fn, reads=(), writes=()):
        return self.op(eng, fn, reads, writes, kind="d")

    def flush(self):
        nc = self.nc
        ops = self.ops
        live = set(id(o) for o in ops)
        for o in ops:
            nd = set()
            for d in o.deps:
                if id(d) not in live:
                    continue
                if d.eng == "pe" and o.eng == "pe" and d.kind == "c":
                    continue
                nd.add(d)
                d.signal = True
            o.deps = nd
        per = {e: [] for e in self.ENGS}
        for o in ops:
            per[o.eng].append(o)
            if o.kind == "d":
                n = self.dma_n[o.eng]
                self.dma_n[o.eng] = n + 1
                o.sem = self.dma_sems[o.eng][n % self.NPOOL]
                o.val = 16 * (n // self.NPOOL + 1)
                if n >= self.NPOOL:
                    o.prewait = (o.sem, 16 * (n // self.NPOOL))
            elif o.kind == "cc":
                self.cc_n += 1
                o.sem = self.cc_sem
                o.val = self.cc_n
            elif o.signal:
                self.cnt[o.eng] += 1
                o.sem = self.sems[o.eng]
                o.val = self.cnt[o.eng]
        dma_final = {}
        for e in self.ENGS:
            n = self.dma_n[e]
            fin = []
            for i in range(min(n, self.NPOOL)):
                last = ((n - 1 - i) // self.NPOOL) * self.NPOOL + i
                fin.append((self.dma_sems[e][i], 16 * (last // self.NPOOL + 1)))
            dma_final[e] = fin
        self.nops += len(ops)

        def emit(e_ops, eng_name):
            def body(e):
                waited = {}
                for o in e_ops:
                    ws = []
                    if o.prewait is not None:
                        ws.append(o.prewait)
                    for d in o.deps:
                        ws.append((d.sem, d.val))
                    need = []
                    for (s, v) in ws:
                        key = id(s)
                        if waited.get(key, 0) >= v:
                            continue
                        waited[key] = v
                        need.append((s, v))
                    emb = None
                    if o.embed and need and EMBED_WAITS:
                        emb = need.pop()
                    for (s, v) in need:
                        e.wait_ge(s, v)
                    ins = o.fn(e)
                    if emb is not None:
                        ins._wait_ge(emb[0], emb[1])
                    if o.kind == "d":
                        ins.then_inc(o.sem, 16)
                    elif o.kind == "cc":
                        ins.then_inc(o.sem)
                    elif o.signal:
                        ins.then_inc(o.sem, 1)
                for (s, v) in dma_final[eng_name]:
                    if waited.get(id(s), 0) < v:
                        e.wait_ge(s, v)
            return body

        with nc.Block() as block:
            reg = {"pe": block.tensor, "act": block.scalar, "dve": block.vector,
                   "pool": block.gpsimd, "sp": block.sync}
            for e in self.ENGS:
                if per[e] or dma_final[e]:
                    reg[e](emit(per[e], e))
        self.ops = []
        self.last_w = {}
        self.readers = {}


class Cfg:
    def __init__(self, seq, depth, debug=False, stab=True, stop=None):
        self.stop = stop
        self.SEQ = seq
        self.DEPTH = depth
        self.LAT = seq // 2
        self.NTOK = self.LAT + CTXH
        self.NW = self.LAT // 512
        self.KT_R = self.NTOK // 128
        self.NKT = 2 * self.KT_R
        self.debug = debug
        self.stab = stab


def layer_dims(l):
    if l % 2 == 0:
        return True, 1024, 1024, 1024, 4096, 0, 1024, 2048, 3072
    return False, 1024, 256, 256, 2560, 0, 1024, 1280, 1536


def build_program(cfg):
    nc = bass.Bass("TRN2", target_bir_lowering=False)
    NTOK, LAT, NW, KT_R, NKT, DEPTH = cfg.NTOK, cfg.LAT, cfg.NW, cfg.KT_R, cfg.NKT, cfg.DEPTH
    dbg_kind = "ExternalOutput" if cfg.debug else "Internal"

    def din(name, shape, dt=F32):
        return nc.dram_tensor(name, list(shape), dt, kind="ExternalInput")

    xin = din("xin", [NTOK, D])
    cin = din("cin", [128, 2, 8])
    ada_w = din("ada_w", [DEPTH, D, 3 * D])
    ada_b = din("ada_b", [DEPTH, 128, 3 * D])
    pre_g = din("pre_g", [DEPTH, 128, D])
    post_g = din("post_g", [DEPTH, 128, D])
    w_out = din("w_out", [DEPTH, D, D])
    NA = (DEPTH + 1) // 2
    NB = max(DEPTH // 2, 1)
    a_w_in = din("a_w_in", [NA, D, 4096])
    a_lambda = din("a_lambda", [NA, 128, 256])
    a_subln = din("a_subln", [NA, 128, 1])
    b_w_in = din("b_w_in", [NB, D, 2560])
    b_qk_g = din("b_qk_g", [NB, 2, 128, 1])
    ropeC = [din("ropeA_C", [128, NTOK]), din("ropeB_C", [128, NTOK])]
    ropeS = [din("ropeA_S", [128, NTOK]), din("ropeB_S", [128, NTOK])]
    ident_in = din("ident", [128, 128], BF16)
    ind_in = din("indmat", [2, 128, 128], BF16)
    out = nc.dram_tensor("out", [LAT, D], F32, kind="ExternalOutput")

    xs = nc.dram_tensor("xs", [NTOK, D], F32, kind=dbg_kind)
    qT, zT, ogT, kTb, kTg, vb, vg = [], [], [], [], [], [], []
    for l in range(DEPTH):
        is_a, FQ, FK, FV, *_ = layer_dims(l)
        qT.append(nc.dram_tensor(f"qT{l}", [FQ, NTOK], BF16, kind=dbg_kind))
        zT.append(nc.dram_tensor(f"zT{l}", [D, NTOK], BF16, kind=dbg_kind))
        ogT.append(nc.dram_tensor(f"ogT{l}", [D, NTOK], BF16, kind=dbg_kind))
        kTb.append(nc.dram_tensor(f"kTb{l}", [FK, NTOK], BF16))
        kTg.append(nc.dram_tensor(f"kTg{l}", [2 * FK, NTOK], BF16))
        vb.append(nc.dram_tensor(f"vb{l}", [(FV // 128) * NTOK, 128], BF16))
        vg.append(nc.dram_tensor(f"vg{l}", [(FV // 128) * 2 * NTOK, 128], BF16))

    import contextlib
    es = contextlib.ExitStack()
    with es:
        def sem(name):
            return es.enter_context(nc.semaphore(name))

        sems = {e: sem(f"s_{e}") for e in Rec.ENGS}
        dma_sems = {e: [sem(f"d_{e}{i}") for i in range(Rec.NPOOL)] for e in ("sp", "pool", "act")}
        dma_sems["pe"] = dma_sems["dve"] = []
        cc_sem = sem("cc")
        R = Rec(nc, sems, dma_sems, cc_sem)

        l_tag = ["g"]

        def sb(st, name, shape, dt):
            return st.enter_context(nc.sbuf_tensor(f"sb{l_tag[0]}_{name}", list(shape), dt))

        def ps(st, name, shape, dt=F32):
            return st.enter_context(nc.psum_tensor(f"ps{l_tag[0]}_{name}", list(shape), dt))

        ident = sb(es, "ident", [128, 128], BF16)
        identf = sb(es, "identf", [128, 128], F32)
        ones_bf = sb(es, "ones_bf", [128, 128], BF16)
        onesf = sb(es, "onesf", [128, 128], F32)
        indm = sb(es, "indm", [128, 2, 128], BF16)
        epsb = sb(es, "epsb", [128, 1], F32)
        cint = sb(es, "cint", [128, 2, 8], F32)
        scs = sb(es, "scs", [128, 2, 8], F32)
        scb = sb(es, "scb", [128, 2, 8, 128], F32)
        Gb = [sb(es, f"Gb{r}", [128, D], F32) for r in range(2)]
        Acol = [sb(es, f"Acol{r}", [128, 8], F32) for r in range(2)]
        Scol = [sb(es, f"Scol{r}", [128, 8], F32) for r in range(2)]

        R.dma("sp", lambda e: e.dma_start(out=ident[:], in_=ident_in[:, :]), writes=["ident"])
        R.dma("sp", lambda e: e.dma_start(out=indm[:], in_=ind_in.ap().rearrange("s p m -> p s m")),
              writes=["indm"])
        R.dma("sp", lambda e: e.dma_start(out=cint[:], in_=cin[:, :, :]), writes=["cint"])
        R.dve(lambda e: e.tensor_copy(out=identf[:], in_=ident[:]), reads=["ident"], writes=["identf"])
        R.dve(lambda e: e.memset(ones_bf[:], 1.0), writes=["ones"])
        R.dve(lambda e: e.memset(onesf[:], 1.0), writes=["onesf"])
        R.dve(lambda e: e.memset(epsb[:], EPS), writes=["eps"])
        import os
        ZS = int(os.environ.get('KDEBUG_ZSTEP', '9'))
        if ZS >= 2:
            R.act(lambda e: e.activation(out=scs[:], in_=cint[:], func=AF.Silu), reads=["cint"], writes=["scs"])
        for r in range(2 if ZS >= 3 else 0):
            for k in range(8):
                R.dve(lambda e, r=r, k=k: e.tensor_scalar(
                    out=scb[:, r, k, :], in0=onesf[:], scalar1=scs[:, r, k:k + 1], scalar2=None,
                    op0=ALU.mult), reads=["onesf", "scs"], writes=[("scb", r, k)])
        R.flush()

        for l in range(DEPTH):
            if cfg.stop == 'Z':
                break
            is_a, FQ, FK, FV, NCOL, colq, colk, colv, colz = layer_dims(l)
            l_tag[0] = str(l)
            j = l // 2
            last = l == DEPTH - 1
            w_in = a_w_in if is_a else b_w_in
            rC, rS = (ropeC[0], ropeS[0]) if is_a else (ropeC[1], ropeS[1])
            src = xin if l == 0 else xs
            NQC = FQ // 128
            NKC = FK // 128
            NVH = FV // 128

            PH = ['M', 'P', 'X', 'A', 'O']
            run_ph = lambda t: cfg.stop is None or PH.index(t) <= PH.index(cfg.stop)
            with contextlib.ExitStack() as st:
                awt = [sb(st, f"awt{i}", [128, 8, 512], F32) for i in range(2)]
                modt = [sb(st, f"modt{r}", [128, 3 * D], F32) for r in range(2)]
                adab = sb(st, "adab", [128, 3 * D], F32)
                pgb = sb(st, "pgb", [128, D], F32)
                qgb = sb(st, "qgb", [128, D], F32)
                tmpA = sb(st, "tmpA", [128, D], F32)
                junk = sb(st, "junkM", [128, 128], F32)
                pm = [ps(st, f"pmM{i}", [128, 512]) for i in range(4)]
                R.dma("sp", lambda e: e.dma_start(out=adab[:], in_=ada_b[l, :, :]), writes=["adab"])
                R.dma("sp", lambda e: e.dma_start(out=pgb[:], in_=pre_g[l, :, :]), writes=["pgb"])
                R.dma("sp", lambda e: e.dma_start(out=qgb[:], in_=post_g[l, :, :]), writes=["qgb"])
                awv = ada_w[l].rearrange("(k p) n -> p k n", p=128)
                for c in range(6):
                    R.dma("sp", lambda e, c=c: e.dma_start(out=awt[c % 2][:], in_=awv[:, :, c * 512:(c + 1) * 512]),
                          writes=[("awt", c % 2)])
                    for r in range(2):
                        bank = pm[(2 * c + r) % 4]

                        def mm(e, c=c, r=r, bank=bank):
                            ins = None
                            for k in range(8):
                                ins = e.matmul(bank[:], scb[:, r, k, :], awt[c % 2][:, k, :],
                                               start=(k == 0), stop=(k == 7))
                            return ins
                        R.pe(mm, reads=[("awt", c % 2)], writes=[("pmM", (2 * c + r) % 4)])
                        R.dve(lambda e, c=c, r=r, bank=bank: e.tensor_tensor(
                            out=modt[r][:, c * 512:(c + 1) * 512], in0=bank[:],
                            in1=adab[:, c * 512:(c + 1) * 512], op=ALU.add),
                            reads=[("pmM", (2 * c + r) % 4), "adab"], writes=[("modt", r, c)])
                import os
                MS = int(os.environ.get('KDEBUG_MSTEP', '9'))
                for r in range(2 if MS >= 2 else 0):
                    allmod = [("modt", r, c) for c in range(6)]
                    R.dve(lambda e, r=r: e.scalar_tensor_tensor(
                        out=tmpA[:], in0=modt[r][:, D:2 * D], scalar=1.0, in1=pgb[:],
                        op0=ALU.add, op1=ALU.mult), reads=allmod + ["pgb"], writes=["tmpA"])
                    for k in range(8):
                        R.dve(lambda e, r=r, k=k: e.scalar_tensor_tensor(
                            out=junk[:], in0=tmpA[:, k * 128:(k + 1) * 128], scalar=1.0, in1=identf[:],
                            op0=ALU.mult, op1=ALU.mult, accum_out=Acol[r][:, k:k + 1]),
                            reads=["tmpA"], writes=["junkM", ("Acol", r)])
                    for k in range(8):
                        R.dve(lambda e, r=r, k=k: e.scalar_tensor_tensor(
                            out=junk[:], in0=modt[r][:, k * 128:(k + 1) * 128], scalar=1.0, in1=identf[:],
                            op0=ALU.mult, op1=ALU.mult, accum_out=Scol[r][:, k:k + 1]),
                            reads=allmod, writes=["junkM", ("Scol", r)])
                    R.dve(lambda e, r=r: e.tensor_tensor(out=Gb[r][:], in0=modt[r][:, 2 * D:3 * D], in1=qgb[:],
                                                         op=ALU.mult), reads=allmod + ["qgb"], writes=[("Gb", r)])
                R.flush()

            if not run_ph('P'):
                continue
            with contextlib.ExitStack() as st:
                wbf = sb(st, "wbf", [128, 8, NCOL], BF16)
                NXT = 8
                xt = [sb(st, f"xt{i}", [128, D], F32) for i in range(NXT)]
                xn = [sb(st, f"xn{i}", [128, D], BF16) for i in range(4)]
                sqj = sb(st, "sqj", [128, D], BF16)
                ssq = [sb(st, f"ssq{i}", [128, 4], F32) for i in range(2)]
                lnv = [sb(st, f"lnv{i}", [128, 4], F32) for i in range(2)]
                rst = [sb(st, f"rst{i}", [128, 4], F32) for i in range(2)]
                hT = [sb(st, f"hT{i}", [128, 8, 512], BF16) for i in range(2)]
                rCt = [sb(st, f"rCt{i}", [128, 512], F32) for i in range(2)]
                rSt = [sb(st, f"rSt{i}", [128, 512], F32) for i in range(2)]
                swt = [sb(st, f"swt{i}", [128, 512], F32) for i in range(2)]
                t1 = [sb(st, f"t1_{i}", [128, 512], F32) for i in range(2)]
                t2 = [sb(st, f"t2_{i}", [128, 512], F32) for i in range(2)]
                NOC = 4
                oc = [sb(st, f"oc{i}", [128, 512], BF16) for i in range(NOC)]
                vo = [sb(st, f"vo{i}", [128, FV], BF16) for i in range(2)]
                if not is_a:
                    sqb = [sb(st, f"sqb{i}", [128, 512], BF16) for i in range(2)]
                    lnt = [sb(st, f"lnt{i}", [128, 512], F32) for i in range(2)]
                    rsb = [sb(st, f"rsb{i}", [128, 512], F32) for i in range(2)]
                    qn = [sb(st, f"qn{i}", [128, 512], F32) for i in range(2)]
                    gqk = sb(st, "gqk", [128, 2], F32)
                tp = [ps(st, f"tp{i}", [128, 1024], BF16) for i in range(2)]
                pm = [ps(st, f"pmP{i}", [128, 512]) for i in range(4)]
                if not is_a:
                    pss = [ps(st, f"pss{i}", [128, 512]) for i in range(2)]

                wv = w_in[j].rearrange("(k p) n -> p k n", p=128)
                CW = 1024
                for k in range(8):
                    for c0 in range(0, NCOL, CW):
                        c1 = min(NCOL, c0 + CW)
                        R.dma("pool", lambda e, k=k, c0=c0, c1=c1: e.dma_start(out=wbf[:, k, c0:c1], in_=wv[:, k, c0:c1]),
                              writes=[("wbf", k, c0)])
                wkeys = [("wbf", k, c0) for k in range(8) for c0 in range(0, NCOL, CW)]
                if not is_a:
                    for i in range(2):
                        R.dma("sp", lambda e, i=i: e.dma_start(out=gqk[:, i:i + 1], in_=b_qk_g[j, i, :, :]),
                              writes=[("gqk", i)])

                cnt = {"xt": 0, "oc": 0, "pm": 0, "vo": 0, "tp": 0, "rp": 0, "b": 0}
                PS = int(os.environ.get('KDEBUG_PSTEP', '9'))
                xts_by_w = {}

                def issue_loads(w_):
                    T_ = 512 if w_ < NW else 128
                    tok0_ = w_ * 512 if w_ < NW else LAT
                    wi_ = w_ % 2
                    lst = []
                    for jj in range(T_ // 128):
                        xi = cnt["xt"] % NXT
                        cnt["xt"] += 1
                        lst.append(xi)
                        R.dma("sp", lambda e, xi=xi, jj=jj, tok0_=tok0_: e.dma_start(
                            out=xt[xi][:], in_=src[tok0_ + jj * 128: tok0_ + (jj + 1) * 128, :]),
                            reads=[("xs", tok0_ + jj * 128)], writes=[("xt", xi)])
                    xts_by_w[w_] = lst
                    R.dma("sp", lambda e, wi_=wi_, tok0_=tok0_, T_=T_: e.dma_start(out=rCt[wi_][:, 0:T_], in_=rC[:, tok0_:tok0_ + T_]),
                          writes=[("rCt", wi_)])
                    R.dma("sp", lambda e, wi_=wi_, tok0_=tok0_, T_=T_: e.dma_start(out=rSt[wi_][:, 0:T_], in_=rS[:, tok0_:tok0_ + T_]),
                          writes=[("rSt", wi_)])
                def tile(w, part, inject=None):
                    T = 512 if w < NW else 128
                    nsub = T // 128
                    tok0 = w * 512 if w < NW else LAT
                    r = 0 if w < NW else 1
                    wi = w % 2
                    hkeys = [("hT", wi, k) for k in range(8)]
                    if part == "front":
                        xts = xts_by_w[w]
                        for jj in range(nsub):
                            xi = xts[jj]
                            R.act(lambda e, xi=xi, jj=jj, wi=wi: e.activation(
                                out=sqj[:], in_=xt[xi][:], func=AF.Square, accum_out=ssq[wi][:, jj:jj + 1]),
                                reads=[("xt", xi)], writes=["sqj", ("ssq", wi, jj)])
                        sskeys = [("ssq", wi, jj) for jj in range(nsub)]
                        R.act(lambda e, wi=wi, nsub=nsub: e.activation(
                            out=lnv[wi][:, 0:nsub], in_=ssq[wi][:, 0:nsub], func=AF.Ln, bias=epsb[:], scale=1.0 / D),
                            reads=sskeys + ["eps"], writes=[("lnv", wi)])
                        R.act(lambda e, wi=wi, nsub=nsub: e.activation(
                            out=rst[wi][:, 0:nsub], in_=lnv[wi][:, 0:nsub], func=AF.Exp, scale=-0.5),
                            reads=[("lnv", wi)], writes=[("rst", wi)])
                        for jj in range(nsub):
                            R.act(lambda e, xi=xts[jj], jj=jj, wi=wi: e.activation(
                                out=xn[jj][:], in_=xt[xi][:], func=AF.Copy, scale=rst[wi][:, jj:jj + 1]),
                                reads=[("xt", xts[jj]), ("rst", wi)], writes=[("xn", jj)])
                        if PS < 3:
                            return
                        for kk in range(4):
                            ti = cnt["tp"] % 2
                            cnt["tp"] += 1

                            def trs(e, kk=kk, ti=ti, nsub=nsub):
                                ins = None
                                for half in range(2):
                                    k = 2 * kk + half
                                    for jj in range(nsub):
                                        ins = e.transpose(tp[ti][:, half * 512 + jj * 128: half * 512 + (jj + 1) * 128],
                                                          xn[jj][:, k * 128:(k + 1) * 128], ident[:])
                                return ins
                            R.pe(trs, reads=[("xn", jj) for jj in range(nsub)] + ["ident"], writes=[("tp", ti)])
                            for half in range(2):
                                k = 2 * kk + half
                                R.dve(lambda e, k=k, ti=ti, half=half, wi=wi, T=T, r=r: e.tensor_scalar(
                                    out=hT[wi][:, k, 0:T], in0=tp[ti][:, half * 512: half * 512 + T],
                                    scalar1=Acol[r][:, k:k + 1], scalar2=Scol[r][:, k:k + 1],
                                    op0=ALU.mult, op1=ALU.add),
                                    reads=[("tp", ti), ("Acol", r), ("Scol", r)], writes=[("hT", wi, k)])
                        return

                    if PS < 4:
                        return
                    chunks = []
                    qk = [("q", c, colq + c * 128) for c in range(NQC)] + [("k", c, colk + c * 128) for c in range(NKC)]
                    zc = [("z", c, colz + c * 128) for c in range(8)]
                    per = max(1, len(qk) // len(zc)) if is_a else len(qk)
                    while qk or zc:
                        for _ in range(per):
                            if qk:
                                chunks.append(qk.pop(0))
                        if zc:
                            chunks.append(zc.pop(0))

                    def stage1(ch):
                        kind, c, col0 = ch
                        bi = cnt["pm"] % 4
                        cnt["pm"] += 1

                        def mm(e, col0=col0, bi=bi, wi=wi, T=T):
                            ins = None
                            for k in range(8):
                                ins = e.matmul(pm[bi][:, 0:T], wbf[:, k, col0:col0 + 128], hT[wi][:, k, 0:T],
                                               start=(k == 0), stop=(k == 7))
                            return ins
                        R.pe(mm, reads=hkeys + wkeys, writes=[("pm", bi)])
                        return bi

                    def stage2(ch, bi):
                        kind, c, col0 = ch
                        oi = cnt["oc"] % NOC
                        cnt["oc"] += 1
                        if kind == "z":
                            R.act(lambda e, bi=bi, oi=oi, T=T: e.activation(out=oc[oi][:, 0:T], in_=pm[bi][:, 0:T], func=AF.Silu),
                                  reads=[("pm", bi)], writes=[("oc", oi)])
                            dst = zT[l]
                            dkey = ("zT", c, w)
                        else:
                            ri = cnt["rp"] % 2
                            cnt["rp"] += 1
                            if is_a:
                                srcap = pm[bi]
                                skey = ("pm", bi)
                            else:
                                b = cnt["b"] % 2
                                cnt["b"] += 1
                                gi = 0 if kind == "q" else 1
                                R.act(lambda e, bi=bi, b=b, T=T: e.activation(out=sqb[b][:, 0:T], in_=pm[bi][:, 0:T], func=AF.Square),
                                      reads=[("pm", bi)], writes=[("sqb", b)])
                                R.pe(lambda e, b=b, T=T: e.matmul(pss[b][:, 0:T], ones_bf[:], sqb[b][:, 0:T], start=True, stop=True),
                                     reads=[("sqb", b), "ones"], writes=[("pss", b)])
                                R.act(lambda e, b=b, T=T: e.activation(out=lnt[b][:, 0:T], in_=pss[b][:, 0:T], func=AF.Ln,
                                                                      bias=epsb[:], scale=1.0 / 128),
                                      reads=[("pss", b), "eps"], writes=[("lnt", b)])
                                R.act(lambda e, b=b, T=T: e.activation(out=rsb[b][:, 0:T], in_=lnt[b][:, 0:T], func=AF.Exp, scale=-0.5),
                                      reads=[("lnt", b)], writes=[("rsb", b)])
                                R.dve(lambda e, b=b, bi=bi, gi=gi, T=T: e.scalar_tensor_tensor(
                                    out=qn[b][:, 0:T], in0=pm[bi][:, 0:T], scalar=gqk[:, gi:gi + 1], in1=rsb[b][:, 0:T],
                                    op0=ALU.mult, op1=ALU.mult),
                                    reads=[("pm", bi), ("rsb", b), ("gqk", gi)], writes=[("qn", b)])
                                srcap = qn[b]
                                skey = ("qn", b)
                            R.dve(lambda e, srcap=srcap, ri=ri, T=T: e.tensor_copy(out=swt[ri][0:64, 0:T], in_=srcap[64:128, 0:T]),
                                  reads=[skey], writes=[("swt", ri, 0)])
                            R.dve(lambda e, srcap=srcap, ri=ri, T=T: e.tensor_copy(out=swt[ri][64:128, 0:T], in_=srcap[0:64, 0:T]),
                                  reads=[skey], writes=[("swt", ri, 1)])
                            R.dve(lambda e, srcap=srcap, ri=ri, wi=wi, T=T: e.tensor_tensor(
                                out=t1[ri][:, 0:T], in0=srcap[:, 0:T], in1=rCt[wi][:, 0:T], op=ALU.mult),
                                reads=[skey, ("rCt", wi)], writes=[("t1", ri)])
                            R.pool(lambda e, ri=ri, wi=wi, T=T: e.tensor_tensor(
                                out=t2[ri][:, 0:T], in0=swt[ri][:, 0:T], in1=rSt[wi][:, 0:T], op=ALU.mult),
                                reads=[("swt", ri, 0), ("swt", ri, 1), ("rSt", wi)], writes=[("t2", ri)])
                            R.pool(lambda e, ri=ri, oi=oi, T=T: e.tensor_tensor(
                                out=oc[oi][:, 0:T], in0=t1[ri][:, 0:T], in1=t2[ri][:, 0:T], op=ALU.add),
                                reads=[("t1", ri), ("t2", ri)], writes=[("oc", oi)])
                            dst = qT[l] if kind == "q" else kTb[l]
                            dkey = ("qT" if kind == "q" else "kTb", c, w)
                        R.dma("sp", lambda e, dst=dst, c=c, oi=oi, tok0=tok0, T=T: e.dma_start(
                            out=dst[c * 128:(c + 1) * 128, tok0:tok0 + T], in_=oc[oi][:, 0:T]),
                            reads=[("oc", oi)], writes=[dkey])

                    prev = None
                    for ci, ch in enumerate(chunks):
                        bi = stage1(ch)
                        if prev is not None:
                            stage2(*prev)
                        prev = (ch, bi)
                        if inject is not None and ci == len(chunks) // 3:
                            inject()
                    vprev = None
                    for jj in range(nsub if PS >= 6 else 0):
                        vi = cnt["vo"] % 2
                        cnt["vo"] += 1
                        for c0 in range(0, FV, 512):
                            cw = min(512, FV - c0)
                            bi = cnt["pm"] % 4
                            cnt["pm"] += 1

                            def mmv(e, jj=jj, c0=c0, cw=cw, bi=bi, wi=wi):
                                ins = None
                                for k in range(8):
                                    ins = e.matmul(pm[bi][:, 0:cw], hT[wi][:, k, jj * 128:(jj + 1) * 128],
                                                   wbf[:, k, colv + c0: colv + c0 + cw], start=(k == 0), stop=(k == 7))
                                return ins
                            R.pe(mmv, reads=hkeys + wkeys, writes=[("pm", bi)])
                            if prev is not None:
                                stage2(*prev)
                                prev = None
                            if os.environ.get('KDEBUG_VCOPY', 'act') == 'dve':
                                R.dve(lambda e, vi=vi, c0=c0, cw=cw, bi=bi: e.tensor_copy(out=vo[vi][:, c0:c0 + cw], in_=pm[bi][:, 0:cw]),
                                      reads=[("pm", bi)], writes=[("vo", vi, c0)])
                            else:
                                R.act(lambda e, vi=vi, c0=c0, cw=cw, bi=bi: e.copy(out=vo[vi][:, c0:c0 + cw], in_=pm[bi][:, 0:cw]),
                                      reads=[("pm", bi)], writes=[("vo", vi, c0)])
                        R.dma("sp", lambda e, vi=vi, jj=jj, tok0=tok0: e.dma_start(
                            out=vb[l].rearrange("(h t) d -> t h d", h=NVH)[tok0 + jj * 128: tok0 + (jj + 1) * 128, :, :],
                            in_=vo[vi][:].rearrange("p (h d) -> p h d", h=NVH)),
                            reads=[("vo", vi, c0) for c0 in range(0, FV, 512)], writes=[("vb", w, jj)])
                    if prev is not None:
                        stage2(*prev)
                        prev = None
                if PS >= 2:
                    issue_loads(0)
                    tile(0, "front")
                    for w in range(NW + 1):
                        if w + 1 <= NW:
                            issue_loads(w + 1)
                        tile(w, "back", inject=(lambda w=w: tile(w + 1, "front")) if w + 1 <= NW else None)
                R.flush()

            if not run_ph('X'):
                continue
            RG = [[2 * p, 2 * p + 1] for p in range(NCORES // 2)]
            for c in range(max(NKC, NVH)):
                if c < NKC:
                    R.op("pool", lambda e, c=c: e.collective_compute(
                        "AllGather", ALU.bypass, replica_groups=RG,
                        ins=[kTb[l][c * 128:(c + 1) * 128, :].opt()], outs=[kTg[l][c * 256:(c + 1) * 256, :].opt()]),
                        writes=[("kTg", c)], kind="cc")
                if c < NVH:
                    R.op("pool", lambda e, c=c: e.collective_compute(
                        "AllGather", ALU.bypass, replica_groups=RG,
                        ins=[vb[l][c * NTOK:(c + 1) * NTOK, :].opt()], outs=[vg[l][c * 2 * NTOK:(c + 1) * 2 * NTOK, :].opt()]),
                        writes=[("vg", c)], kind="cc")
            if cfg.stop == 'X':
                R.flush()
                R.op("pool", lambda e: e.wait_ge(cc_sem, R.cc_n), kind="c")
                R.flush()

            if not run_ph('A'):
                continue
            with contextlib.ExitStack() as st:
                NU = 2 if is_a else 1
                kTs = [sb(st, f"kTs{i}", [128, 2 * NTOK], BF16) for i in range(2)]
                vs = [sb(st, f"vs{i}", [128, NKT, 128], BF16) for i in range(2)]
                qs = [[sb(st, f"qs{i}_{s}", [128, NTOK], BF16) for s in range(NU)] for i in range(2)]
                zs = [sb(st, f"zs{i}", [128, NTOK], BF16) for i in range(2)]
                NPT = 8
                pt = [sb(st, f"pt{i}", [128, 1024], BF16) for i in range(NPT)]
                accS = [sb(st, f"accS{i}", [128, 512], BF16) for i in range(2)]
                accP = [sb(st, f"accP{i}", [128, 1024], BF16) for i in range(2)]
                rr = [sb(st, f"rr{i}", [128, 512], F32) for i in range(2)]
                o_s = [sb(st, f"o_s{i}", [128, 512], F32) for i in range(2)]
                ocmb = sb(st, "ocmb", [128, 512], F32)
                sqe = sb(st, "sqe", [128, 512], BF16)
                sqo = [sb(st, f"sqo{i}", [128, 512], BF16) for i in range(2)]
                lno = sb(st, "lno", [128, 512], F32)
                rso = sb(st, "rso", [128, 512], F32)
                ono = sb(st, "ono", [128, 512], F32)
                ogs = [sb(st, f"ogs{i}", [128, 512], BF16) for i in range(2)]
                negM = [sb(st, f"negM{i}", [128, 1], F32) for i in range(4)]
                stt = sb(st, "stt", [128, 8], F32)
                stg = sb(st, "stg", [128, 64], F32)
                kmx = [sb(st, f"kmx{i}", [128, 2], F32) for i in range(2)]
                qmx = sb(st, "qmx", [128, 2], F32)
                if is_a:
                    lamt = sb(st, "lamt", [128, 256], F32)
                    lamj = sb(st, "lamj", [128, 64], F32)
                    lams = sb(st, "lams", [128, 4], F32)
                    neglam = sb(st, "neglam", [128, 1], F32)
                    gsub = sb(st, "gsub", [128, 1], F32)
                Sg = [ps(st, f"Sg{i}", [128, 1024]) for i in range(2)]
                Ob = [ps(st, f"Ob{i}", [128, 512]) for i in range(2)]
                Lb = ps(st, "Lb", [128, 512])
                accPS = ps(st, "accPS", [128, 512])
                pstat = [(Sg[0][:, 0:512], ("S", 0, 0)), (Sg[0][:, 512:1024], ("S", 0, 1)),
                         (Sg[1][:, 0:512], ("S", 1, 0)), (Sg[1][:, 512:1024], ("S", 1, 1))]

                scale = (64 ** -0.5) if is_a else (128 ** -0.5)
                if is_a:
                    lam_init = lambda_init_fn(l)
                    for i in range(2):
                        for s in range(2):
                            R.dve(lambda e, i=i, s=s: e.memset(qs[i][s][:], 0.0), writes=[("qs", i, s), ("qs2", i, s)])
                    R.dma("sp", lambda e: e.dma_start(out=lamt[:], in_=a_lambda[j, :, :]), writes=["lamt"])
                    R.dma("sp", lambda e: e.dma_start(out=gsub[:], in_=a_subln[j, :, :]), writes=["gsub0"])
                    for i in range(2):
                        R.dve(lambda e, i=i: e.scalar_tensor_tensor(
                            out=lamj[:], in0=lamt[:, (2 * i) * 64:(2 * i + 1) * 64], scalar=1.0,
                            in1=lamt[:, (2 * i + 1) * 64:(2 * i + 2) * 64], op0=ALU.mult, op1=ALU.mult,
                            accum_out=lams[:, i:i + 1]), reads=["lamt"], writes=["lamj", ("lams", i)])
                    R.act(lambda e: e.activation(out=lams[:, 2:4], in_=lams[:, 0:2], func=AF.Exp),
                          reads=[("lams", 0), ("lams", 1)], writes=["lame"])
                    R.dve(lambda e: e.scalar_tensor_tensor(out=neglam[:], in0=lams[:, 3:4], scalar=-lam_init,
                                                           in1=lams[:, 2:3], op0=ALU.add, op1=ALU.subtract),
                          reads=["lame"], writes=["neglam"])
                    R.dve(lambda e: e.tensor_scalar(out=gsub[:], in0=gsub[:], scalar1=(1.0 - lam_init), scalar2=None,
                                                    op0=ALU.mult), reads=["gsub0"], writes=["gsub"])

                FAST_RECIP = os.environ.get("KDEBUG_FASTRECIP", "0") == "1"
                pending = []
                state = {"unit": 0, "og": 0, "kvslot": -1, "kvhead": -1, "nm": 0, "sq": 0, "g": 0, "pb": 0}

                def drain(n=None, upto=None):
                    k = len(pending) if n is None else min(n, len(pending))
                    for _ in range(k):
                        if upto is not None and pending[0][0] > upto:
                            break
                        pending.pop(0)[1]()

                def defer(fn):
                    pending.append((state["unit"] - 1, fn))

                def flush_tail():
                    if state.get("tail") is not None:
                        t_ = state["tail"]
                        state["tail"] = None
                        t_()

                def maxsq(srct, ncols, rkeys, outs):
                    nch = 0
                    for c0 in range(0, ncols, 512):
                        cw = min(512, ncols - c0)
                        b = state["sq"] % 2
                        state["sq"] += 1
                        R.act(lambda e, b=b, c0=c0, cw=cw: e.activation(out=sqo[b][:, 0:cw], in_=srct[:, c0:c0 + cw], func=AF.Square),
                              reads=rkeys, writes=[("sqo", b)])
                        for oi_, (ind, dst, dkey) in enumerate(outs):
                            pb, pkey = pstat[state["pb"] % 4]
                            state["pb"] += 1
                            R.pe(lambda e, b=b, cw=cw, pb=pb, ind=ind: e.matmul(pb[:, 0:cw], ind, sqo[b][:, 0:cw], start=True, stop=True),
                                 reads=[("sqo", b)], writes=[pkey])
                            col = oi_ * 20 + nch
                            R.dve(lambda e, cw=cw, pb=pb, col=col: e.tensor_reduce(out=stg[:, col:col + 1], in_=pb[:, 0:cw], axis=AX.X, op=ALU.max),
                                  reads=[pkey], writes=[("stg", col)])
                        nch += 1
                    for oi_, (ind, dst, dkey) in enumerate(outs):
                        R.dve(lambda e, nch=nch, oi_=oi_, dst=dst: e.tensor_reduce(out=dst, in_=stg[:, oi_ * 20: oi_ * 20 + nch], axis=AX.X, op=ALU.max),
                              reads=[("stg", oi_ * 20 + i) for i in range(nch)], writes=[dkey])

                NH = 8
                for h in range(NH):
                    hs = h % 2
                    hk = h if is_a else h // 4
                    if hk != state["kvhead"]:
                        state["kvhead"] = hk
                        state["kvslot"] = (state["kvslot"] + 1) % 2
                        ks = state["kvslot"]
                        for rk in range(2):
                            R.dma("sp", lambda e, ks=ks, rk=rk, hk=hk: e.dma_start(
                                out=kTs[ks][:, rk * NTOK:(rk + 1) * NTOK],
                                in_=kTg[l][hk * 256 + rk * 128: hk * 256 + (rk + 1) * 128, :]),
                                reads=[("kTg", hk)], writes=[("kTs", ks, rk)])
                            vgv = vg[l][hk * 2 * NTOK:(hk + 1) * 2 * NTOK, :].rearrange("(kt p) f -> p kt f", p=128)
                            for part in range(4):
                                k0 = rk * KT_R + (KT_R * part) // 4
                                k1 = rk * KT_R + (KT_R * (part + 1)) // 4
                                if k1 > k0:
                                    R.dma("sp", lambda e, ks=ks, k0=k0, k1=k1, vgv=vgv: e.dma_start(
                                        out=vs[ks][:, k0:k1, :], in_=vgv[:, k0:k1, :]),
                                        reads=[("vg", hk)], writes=[("vs", ks, rk, part)])
                        if cfg.stab:
                            maxsq(kTs[ks], 2 * NTOK, [("kTs", ks, 0), ("kTs", ks, 1)],
                                  [((indm[:, s, :] if is_a else ones_bf[:]), kmx[ks][:, s:s + 1], ("kmx", ks, s)) for s in range(NU)])
                    ks = state["kvslot"]
                    if is_a:
                        for s in range(2):
                            for half in range(2):
                                p0 = 64 * half + 32 * s
                                R.dma("sp", lambda e, hs=hs, s=s, p0=p0, h=h: e.dma_start(
                                    out=qs[hs][s][p0:p0 + 32, :], in_=qT[l][h * 128 + p0: h * 128 + p0 + 32, :]),
                                    writes=[("qs", hs, s)] if half == 0 else [("qs2", hs, s)])
                    else:
                        R.dma("sp", lambda e, hs=hs, h=h: e.dma_start(out=qs[hs][0][:], in_=qT[l][h * 128:(h + 1) * 128, :]),
                              writes=[("qs", hs, 0)])
                    R.dma("sp", lambda e, hs=hs, h=h: e.dma_start(out=zs[hs][:], in_=zT[l][h * 128:(h + 1) * 128, :]),
                          writes=[("zs", hs)])

                    nm = []
                    for s in range(NU):
                        mi = state["nm"] % 4
                        state["nm"] += 1
                        nm.append(mi)
                        if cfg.stab:
                            maxsq(qs[hs][s], NTOK, [("qs", hs, s), ("qs2", hs, s)], [(ones_bf[:], qmx[:, s:s + 1], ("qmx", s))])
                            R.dve(lambda e, ks=ks, s=s: e.tensor_tensor(out=stt[:, 3:4], in0=kmx[ks][:, s:s + 1], in1=qmx[:, s:s + 1], op=ALU.mult),
                                  reads=[("kmx", ks, s), ("qmx", s)], writes=[("stt", 3)])
                            R.act(lambda e: e.activation(out=stt[:, 4:5], in_=stt[:, 3:4], func=AF.Ln, bias=epsb[:], scale=1.0),
                                  reads=[("stt", 3), "eps"], writes=[("stt", 4)])
                            R.act(lambda e: e.activation(out=stt[:, 5:6], in_=stt[:, 4:5], func=AF.Exp, scale=0.5),
                                  reads=[("stt", 4)], writes=[("stt", 5)])
                            R.dve(lambda e, mi=mi: e.tensor_scalar(out=negM[mi][:], in0=stt[:, 5:6], scalar1=-scale, scalar2=None, op0=ALU.mult),
                                  reads=[("stt", 5)], writes=[("negM", mi)])
                        else:
                            R.dve(lambda e, mi=mi: e.memset(negM[mi][:], 0.0), writes=[("negM", mi)])

                    def vpart(kt):
                        rk = kt // KT_R
                        for pp in range(4):
                            k0 = rk * KT_R + (KT_R * pp) // 4
                            k1 = rk * KT_R + (KT_R * (pp + 1)) // 4
                            if k0 <= kt < k1:
                                return rk, pp
                        return rk, 3

                    nqt = NW + (0 if last else 1)
                    for w in range(nqt):
                        T = 512 if w < NW else 128
                        tok0 = w * 512 if w < NW else LAT
                        ktl = list(range(NKT)) if w < NW else [KT_R - 1, 2 * KT_R - 1]
                        groups = [ktl[i:i + 2] for i in range(0, len(ktl), 2)]
                        if w >= NW:
                            flush_tail()
                            drain()
                        for s in range(NU):
                            ob = state["unit"] % 2
                            ab = ob
                            state["unit"] += 1
                            qsk = [("qs", hs, s), ("qs2", hs, s)] if is_a else [("qs", hs, s)]
                            mi = nm[s]

                            def QK(gi, grp, s=s, T=T, tok0=tok0, ks=ks, hs=hs, qsk=qsk):
                                for a, kt in enumerate(grp):
                                    R.pe(lambda e, gi=gi, a=a, kt=kt: e.matmul(
                                        Sg[gi][:, a * 512: a * 512 + T], kTs[ks][:, kt * 128:(kt + 1) * 128],
                                        qs[hs][s][:, tok0:tok0 + T], start=True, stop=True),
                                        reads=[("kTs", ks, kt // KT_R)] + qsk, writes=[("S", gi, a)]).embed = True
                            ng = len(groups)
                            drain(upto=state["unit"] - 3)
                            step = max(1, (ng - 1) // (len(pending) + 1))
                            used = {"d": False, "p": False}
                            QK(state["g"] % 2, groups[0])
                            flush_tail()
                            prevPV = None
                            for gidx, grp in enumerate(groups):
                                g = state["g"]
                                state["g"] += 1
                                gi = g % 2
                                pi = g % NPT
                                na = len(grp)
                                if gidx + 1 < ng:
                                    QK((g + 1) % 2, groups[gidx + 1])
                                S3 = Sg[gi][:].rearrange("p (a t) -> p a t", a=2)[:, 0:na, 0:T]
                                P3 = pt[pi][:].rearrange("p (a t) -> p a t", a=2)[:, 0:na, 0:T]
                                R.act(lambda e, S3=S3, P3=P3, mi=mi: e.activation(out=P3, in_=S3, func=AF.Exp, bias=negM[mi][:], scale=scale),
                                      reads=[("S", gi, a) for a in range(na)] + [("negM", mi)], writes=[("pt", pi)]).embed = True
                                if prevPV is not None:
                                    prevPV()

                                def PV(grp=grp, gidx=gidx, na=na, pi=pi, ob=ob, T=T, ks=ks, ng=ng):
                                    for a, kt in enumerate(grp):
                                        rk, part = vpart(kt)
                                        first = (gidx == 0 and a == 0)
                                        lastmm = (gidx == ng - 1 and a == na - 1)
                                        R.pe(lambda e, a=a, kt=kt, first=first, lastmm=lastmm: e.matmul(
                                            Ob[ob][:, 0:T], vs[ks][:, kt, :], pt[pi][:, a * 512: a * 512 + T], start=first, stop=lastmm),
                                            reads=[("vs", ks, rk, part), ("pt", pi)], writes=[("O", ob)]).embed = True
                                prevPV = PV
                                use_d = (gidx % 2 == 0)
                                if use_d:
                                    for a in range(na):
                                        src_ = pt[pi][:, a * 512: a * 512 + T]
                                        if not used["d"]:
                                            used["d"] = True
                                            R.dve(lambda e, src_=src_, T=T: e.tensor_copy(out=accPS[:, 0:T], in_=src_),
                                                  reads=[("pt", pi)], writes=["accPS"]).embed = True
                                        else:
                                            R.dve(lambda e, src_=src_, T=T: e.tensor_tensor(out=accPS[:, 0:T], in0=accPS[:, 0:T], in1=src_, op=ALU.add),
                                                  reads=[("pt", pi), "accPS"], writes=["accPS"]).embed = True
                                else:
                                    A3 = accP[ab][:].rearrange("p (a t) -> p a t", a=2)[:, 0:na, 0:T]
                                    akey = ("accP", ab)
                                    if not used["p"]:
                                        used["p"] = True
                                        R.pool(lambda e, A3=A3, P3=P3: e.tensor_copy(out=A3, in_=P3), reads=[("pt", pi)], writes=[akey]).embed = True
                                    else:
                                        R.pool(lambda e, A3=A3, P3=P3: e.tensor_tensor(out=A3, in0=A3, in1=P3, op=ALU.add),
                                               reads=[("pt", pi), akey], writes=[akey]).embed = True
                                if gidx >= 1 and gidx % step == 0:
                                    drain(1)
                            state["tail"] = prevPV
                            na0 = len(groups[0])
                            srcs = []
                            if used["d"]:
                                R.dve(lambda e, ab=ab, T=T: e.tensor_copy(out=accS[ab][:, 0:T], in_=accPS[:, 0:T]),
                                      reads=["accPS"], writes=[("accS", ab)])
                                srcs += [(accS[ab], ("accS", ab), 0)]
                            if used["p"]:
                                srcs += [(accP[ab], ("accP", ab), a) for a in range(na0)]

                            def Lstage(srcs=srcs, T=T):
                                def mmL(e):
                                    ins = None
                                    for i, (t_, k_, a) in enumerate(srcs):
                                        ins = e.matmul(Lb[:, 0:T], ones_bf[:], t_[:, a * 512: a * 512 + T],
                                                       start=(i == 0), stop=(i == len(srcs) - 1))
                                    return ins
                                R.pe(mmL, reads=list({k_ for (_, k_, _) in srcs}), writes=["Lb"])
                            defer(Lstage)
                            defer(lambda s=s, T=T: R.dve(
                                lambda e: (e.reciprocal_approx_fast(out=rr[s][:, 0:T], in_=Lb[:, 0:T]) if FAST_RECIP
                                           else e.reciprocal(out=rr[s][:, 0:T], in_=Lb[:, 0:T])),
                                reads=["Lb"], writes=[("rr", s)]))
                            defer(lambda ob=ob, s=s, T=T: R.dve(
                                lambda e: e.tensor_tensor(out=o_s[s][:, 0:T], in0=Ob[ob][:, 0:T], in1=rr[s][:, 0:T], op=ALU.mult),
                                reads=[("O", ob), ("rr", s)], writes=[("o_s", s)]))
                        oi = state["og"] % 2
                        state["og"] += 1
                        if is_a:
                            defer(lambda T=T: R.dve(
                                lambda e: e.scalar_tensor_tensor(out=ocmb[:, 0:T], in0=o_s[1][:, 0:T], scalar=neglam[:],
                                                                 in1=o_s[0][:, 0:T], op0=ALU.mult, op1=ALU.add),
                                reads=[("o_s", 0), ("o_s", 1), "neglam"], writes=["ocmb"]))
                            defer(lambda T=T: R.act(
                                lambda e: e.activation(out=sqe[:, 0:T], in_=ocmb[:, 0:T], func=AF.Square),
                                reads=["ocmb"], writes=["sqe"]))
                            defer(lambda T=T: R.pe(
                                lambda e: e.matmul(Lb[:, 0:T], ones_bf[:], sqe[:, 0:T], start=True, stop=True),
                                reads=["sqe"], writes=["Lb"]))
                            defer(lambda T=T: R.act(
                                lambda e: e.activation(out=lno[:, 0:T], in_=Lb[:, 0:T], func=AF.Ln, bias=epsb[:], scale=1.0 / 128),
                                reads=["Lb"], writes=["lno"]))
                            defer(lambda T=T: R.act(
                                lambda e: e.activation(out=rso[:, 0:T], in_=lno[:, 0:T], func=AF.Exp, scale=-0.5),
                                reads=["lno"], writes=["rso"]))
                            defer(lambda T=T: R.dve(
                                lambda e: e.scalar_tensor_tensor(out=ono[:, 0:T], in0=ocmb[:, 0:T], scalar=gsub[:],
                                                                 in1=rso[:, 0:T], op0=ALU.mult, op1=ALU.mult),
                                reads=["ocmb", "rso", "gsub"], writes=["ono"]))
                            fin_src, fin_key = ono, "ono"
                        else:
                            fin_src, fin_key = o_s[0], ("o_s", 0)
                        defer(lambda T=T, oi=oi, tok0=tok0, fin_src=fin_src, fin_key=fin_key, hs=hs: R.dve(
                            lambda e: e.tensor_tensor(out=ogs[oi][:, 0:T], in0=fin_src[:, 0:T], in1=zs[hs][:, tok0:tok0 + T], op=ALU.mult),
                            reads=[fin_key, ("zs", hs)], writes=[("ogs", oi)]))
                        defer(lambda T=T, oi=oi, tok0=tok0, h=h, w=w: R.dma(
                            "pool", lambda e: e.dma_start(out=ogT[l][h * 128:(h + 1) * 128, tok0:tok0 + T], in_=ogs[oi][:, 0:T]),
                            reads=[("ogs", oi)], writes=[("ogT", h, w)]))
                flush_tail()
                drain()
                R.flush()

            if not run_ph('O'):
                continue
            with contextlib.ExitStack() as st:
                wo = sb(st, "wo", [128, 8, D], BF16)
                og = [sb(st, f"og{i}", [128, 8, 512], BF16) for i in range(2)]
                NXO = 6
                xo = [sb(st, f"xo{i}", [128, D], F32) for i in range(NXO)]
                yt = [sb(st, f"yt{i}", [128, D], F32) for i in range(2)]
                xw = [sb(st, f"xw{i}", [128, D], F32) for i in range(3)]
                sqy = sb(st, "sqy", [128, 512], BF16)
                ssy = [sb(st, f"ssy{i}", [128, 4], F32) for i in range(2)]
                py = [ps(st, f"py{i}", [128, 512]) for i in range(4)]
                wov = w_out[l].rearrange("(k p) n -> p k n", p=128)
                for k in range(8):
                    R.dma("pool", lambda e, k=k: e.dma_start(out=wo[:, k, :], in_=wov[:, k, :]), writes=[("wo", k)])
                wokeys = [("wo", k) for k in range(8)]
                ogv = ogT[l].rearrange("(k p) t -> p k t", p=128)
                tcount = 0
                nwt = NW + (0 if last else 1)
                for w in range(nwt):
                    T = 512 if w < NW else 128
                    nsub = T // 128
                    tok0 = w * 512 if w < NW else LAT
                    r = 0 if w < NW else 1
                    wi = w % 2
                    R.dma("sp", lambda e, wi=wi, tok0=tok0, T=T: e.dma_start(out=og[wi][:, :, 0:T], in_=ogv[:, :, tok0:tok0 + T]),
                          writes=[("og", wi)])
                    for jj in range(nsub):
                        xi = tcount % NXO
                        yi = tcount % 2
                        xwi = tcount % 3
                        tcount += 1
                        t0 = tok0 + jj * 128
                        R.dma("sp", lambda e, xi=xi, t0=t0: e.dma_start(out=xo[xi][:], in_=src[t0:t0 + 128, :]), writes=[("xo", xi)])
                        for nh in range(2):
                            bi = (2 * yi + nh)

                            def mmo(e, nh=nh, jj=jj, wi=wi, bi=bi):
                                ins = None
                                for k in range(8):
                                    ins = e.matmul(py[bi][:], og[wi][:, k, jj * 128:(jj + 1) * 128], wo[:, k, nh * 512:(nh + 1) * 512],
                                                   start=(k == 0), stop=(k == 7))
                                return ins
                            R.pe(mmo, reads=[("og", wi)] + wokeys, writes=[("py", bi)])
                            R.act(lambda e, bi=bi, yi=yi, nh=nh: e.activation(out=sqy[:], in_=py[bi][:], func=AF.Square,
                                                                              accum_out=ssy[yi][:, nh:nh + 1]),
                                  reads=[("py", bi)], writes=["sqy", ("ssy", yi, nh)])
                        R.dve(lambda e, yi=yi: e.tensor_tensor(out=ssy[yi][:, 2:3], in0=ssy[yi][:, 0:1], in1=ssy[yi][:, 1:2], op=ALU.add),
                              reads=[("ssy", yi, 0), ("ssy", yi, 1)], writes=[("ssy", yi, 2)])
                        R.act(lambda e, yi=yi: e.activation(out=ssy[yi][:, 3:4], in_=ssy[yi][:, 2:3], func=AF.Ln, bias=epsb[:], scale=1.0 / D),
                              reads=[("ssy", yi, 2), "eps"], writes=[("ssy", yi, 3)])
                        R.act(lambda e, yi=yi: e.activation(out=ssy[yi][:, 2:3], in_=ssy[yi][:, 3:4], func=AF.Exp, scale=-0.5),
                              reads=[("ssy", yi, 3)], writes=[("ssy", yi, 4)])
                        for nh in range(2):
                            bi = (2 * yi + nh)
                            R.dve(lambda e, bi=bi, yi=yi, nh=nh, r=r: e.scalar_tensor_tensor(
                                out=yt[yi][:, nh * 512:(nh + 1) * 512], in0=py[bi][:], scalar=ssy[yi][:, 2:3],
                                in1=Gb[r][:, nh * 512:(nh + 1) * 512], op0=ALU.mult, op1=ALU.mult),
                                reads=[("py", bi), ("ssy", yi, 4), ("Gb", r)], writes=[("yt", yi, nh)])
                        R.dve(lambda e, yi=yi, xi=xi, xwi=xwi: e.tensor_tensor(out=xw[xwi][:], in0=yt[yi][:], in1=xo[xi][:], op=ALU.add),
                              reads=[("yt", yi, 0), ("yt", yi, 1), ("xo", xi)], writes=[("xw", xwi)])
                        if last:
                            R.dma("pool", lambda e, xwi=xwi, t0=t0: e.dma_start(out=out[t0:t0 + 128, :], in_=xw[xwi][:]),
                                  reads=[("xw", xwi)], writes=[("out", t0)])
                        else:
                            R.dma("pool", lambda e, xwi=xwi, t0=t0: e.dma_start(out=xs[t0:t0 + 128, :], in_=xw[xwi][:]),
                                  reads=[("xw", xwi)], writes=[("xs", t0)])
                R.flush()
    return nc


def _rope_tables(cfg, hf, head_dim, dup):
    LAT, NTOK = cfg.LAT, cfg.NTOK
    t = np.arange(hf * LAT, (hf + 1) * LAT)
    rows = (t // GRID_W).astype(np.float32)
    cols = (t % GRID_W).astype(np.float32)
    axis_dim = head_dim // 2
    freqs = (ROPE_THETA ** (-np.arange(0, axis_dim, 2, dtype=np.float32) / np.float32(axis_dim))).astype(np.float32)
    ang = np.concatenate([rows[:, None] * freqs, cols[:, None] * freqs], axis=-1).astype(np.float32)
    cos = np.cos(ang).astype(np.float32).T
    sin = np.sin(ang).astype(np.float32).T
    half = head_dim // 2
    C = np.ones((128, NTOK), np.float32)
    S = np.zeros((128, NTOK), np.float32)
    if dup:
        for blk in range(4):
            C[blk * 32:(blk + 1) * 32, :LAT] = cos
            S[blk * 32:(blk + 1) * 32, :LAT] = -sin if blk < 2 else sin
    else:
        C[0:64, :LAT] = cos
        C[64:128, :LAT] = cos
        S[0:64, :LAT] = -sin
        S[64:128, :LAT] = sin
    return C, S


def _perm_a_cols():
    perm = np.arange(4096)
    p128 = np.zeros(128, np.int64)
    for n in range(128):
        blk = n // 32
        s = blk % 2
        d = (n % 32) + (32 if blk >= 2 else 0)
        p128[n] = s * 64 + d
    for base in (0, 1024):
        for h in range(8):
            perm[base + h * 128: base + (h + 1) * 128] = base + h * 128 + p128
    return perm


def make_in_maps(cfg, x, c, ctx, c_ctx, ada_w, ada_b, pre_g, post_g, w_out, a_w_in, a_lambda, a_subln_g, b_w_in, b_qk_g):
    DEPTH = cfg.DEPTH
    f = lambda a: np.ascontiguousarray(np.asarray(a, dtype=np.float32))
    x, c, ctx, c_ctx = f(x), f(c), f(ctx), f(c_ctx)
    NA = (DEPTH + 1) // 2
    NB = max(DEPTH // 2, 1)
    shared = {
        "ada_w": f(ada_w)[:DEPTH],
        "ada_b": np.ascontiguousarray(np.broadcast_to(f(ada_b)[:DEPTH, None, :], (DEPTH, 128, 3 * D))),
        "pre_g": np.ascontiguousarray(np.broadcast_to(f(pre_g)[:DEPTH, None, :], (DEPTH, 128, D))),
        "post_g": np.ascontiguousarray(np.broadcast_to(f(post_g)[:DEPTH, None, :], (DEPTH, 128, D))),
        "w_out": f(w_out)[:DEPTH],
        "a_w_in": np.ascontiguousarray(f(a_w_in)[:NA][:, :, _perm_a_cols()]),
        "a_lambda": np.ascontiguousarray(np.broadcast_to(f(a_lambda)[:NA].reshape(NA, 1, 256), (NA, 128, 256))),
        "a_subln": np.ascontiguousarray(f(a_subln_g)[:NA].reshape(NA, 128, 1)),
        "b_w_in": f(b_w_in)[:NB],
        "b_qk_g": np.ascontiguousarray(f(b_qk_g)[:NB].reshape(NB, 2, 128, 1)),
        "ident": np.eye(128, dtype=np.float32).astype(ml_dtypes.bfloat16),
    }
    ind = np.zeros((2, 128, 128), np.float32)
    for s in range(2):
        for p in range(128):
            if (p // 32) % 2 == s:
                ind[s, p, :] = 1.0
    shared["indmat"] = ind.astype(ml_dtypes.bfloat16)
    maps = []
    for i in range(NCORES):
        b, hf = i // 2, i % 2
        LAT = cfg.LAT
        xin = np.concatenate([x[b, hf * LAT:(hf + 1) * LAT], ctx[b, hf * CTXH:(hf + 1) * CTXH]], axis=0)
        cin = np.stack([c[b].reshape(8, 128).T, c_ctx.reshape(8, 128).T], axis=1)
        ac, as_ = _rope_tables(cfg, hf, 64, True)
        bc, bs = _rope_tables(cfg, hf, 128, False)
        m = dict(shared)
        m.update({"xin": np.ascontiguousarray(xin), "cin": np.ascontiguousarray(cin),
                  "ropeA_C": ac, "ropeA_S": as_, "ropeB_C": bc, "ropeB_S": bs})
        maps.append(m)
    return maps


_CACHE = {}


def run(cfg, inputs, trace=False):
    key = (cfg.SEQ, cfg.DEPTH, cfg.debug, cfg.stab, cfg.stop)
    if key not in _CACHE:
        _CACHE[key] = build_program(cfg)
    nc = _CACHE[key]
    maps = make_in_maps(cfg, **inputs)
    res = run_bass_kernel_spmd(nc, maps, core_ids=list(range(NCORES)))
    return res


def kernel(x, c, ctx, c_ctx, ada_w, ada_b, pre_g, post_g, w_out, a_w_in, a_lambda, a_subln_g, b_w_in, b_qk_g):
    x = np.asarray(x)
    B, S, _ = x.shape
    cfg = Cfg(S, 4)
    res = run(cfg, dict(x=x, c=c, ctx=ctx, c_ctx=c_ctx, ada_w=ada_w, ada_b=ada_b, pre_g=pre_g, post_g=post_g,
                        w_out=w_out, a_w_in=a_w_in, a_lambda=a_lambda, a_subln_g=a_subln_g, b_w_in=b_w_in, b_qk_g=b_qk_g))
    outp = np.empty((B, S, D), np.float32)
    for i in range(NCORES):
        b, hf = i // 2, i % 2
        outp[b, hf * cfg.LAT:(hf + 1) * cfg.LAT] = np.asarray(res.results[i]["out"], dtype=np.float32)
    return outp
```

```python
import math
import numpy as np
import ml_dtypes
import concourse.bass as bass
import concourse.mybir as mybir
from concourse.bass_utils import run_bass_kernel_spmd

F32 = mybir.dt.float32
BF16 = mybir.dt.bfloat16
AF = mybir.ActivationFunctionType
ALU = mybir.AluOpType
AX = mybir.AxisListType

D = 1024
NCORES = 8
CTXH = 128
EPS = 1e-6
ROPE_THETA = 10000.0
GRID_W = 64


def lambda_init_fn(i):
    return 0.8 - 0.6 * math.exp(-0.3 * i)


class Op:
    __slots__ = ("eng", "fn", "deps", "kind", "signal", "val", "sem", "prewait", "embed")

    def __init__(self, eng, fn, deps, kind):
        self.eng = eng
        self.fn = fn
        self.deps = deps
        self.kind = kind
        self.signal = kind != "c"
        self.val = None
        self.sem = None
        self.prewait = None
        self.embed = False


import os as _os
EMBED_WAITS = _os.environ.get("KDEBUG_EMBED", "1") == "1"


class Rec:
    ENGS = ("pe", "act", "dve", "pool", "sp")
    NPOOL = 8

    def __init__(self, nc, sems, dma_sems, cc_sem):
        self.nc = nc
        self.sems = sems
        self.dma_sems = dma_sems
        self.cc_sem = cc_sem
        self.cnt = {e: 0 for e in self.ENGS}
        self.dma_n = {e: 0 for e in self.ENGS}
        self.cc_n = 0
        self.ops = []
        self.last_w = {}
        self.readers = {}
        self.nops = 0
        self.auto_embed = False

    def embedding(self):
        rec = self

        class _Ctx:
            def __enter__(self_):
                self_.old = rec.auto_embed
                rec.auto_embed = True

            def __exit__(self_, *a):
                rec.auto_embed = self_.old
                return False
        return _Ctx()

    def op(self, eng, fn, reads=(), writes=(), kind="c"):
        deps = set()
        for k in reads:
            w = self.last_w.get(k)
            if w is not None:
                deps.add(w)
        for k in writes:
            w = self.last_w.get(k)
            if w is not None:
                deps.add(w)
            for r in self.readers.get(k, ()):
                deps.add(r)
        o = Op(eng, fn, deps, kind)
        o.embed = self.auto_embed and kind == "c" and getattr(fn, "__name__", "") == "<lambda>"
        for k in reads:
            self.readers.setdefault(k, []).append(o)
        for k in writes:
            self.last_w[k] = o
            self.readers[k] = []
        self.ops.append(o)
        return o

    def pe(self, fn, reads=(), writes=()):
        return self.op("pe", fn, reads, writes)

    def act(self, fn, reads=(), writes=()):
        return self.op("act", fn, reads, writes)

    def dve(self, fn, reads=(), writes=()):
        return self.op("dve", fn, reads, writes)

    def pool(self, fn, reads=(), writes=()):
        return self.op("pool", fn, reads, writes)

    def dma(self, eng, fn, reads=(), writes=()):
        return self.op(eng, fn, reads, writes, kind="d")

    def flush(self):
        nc = self.nc
        ops = self.ops
        live = set(id(o) for o in ops)
        for o in ops:
            nd = set()
            for d in o.deps:
                if id(d) not in live:
                    continue
                if d.eng == "pe" and o.eng == "pe" and d.kind == "c":
                    continue
                nd.add(d)
                d.signal = True
            o.deps = nd
        per = {e: [] for e in self.ENGS}
        for o in ops:
            per[o.eng].append(o)
            if o.kind == "d":
                n = self.dma_n[o.eng]
                self.dma_n[o.eng] = n + 1
                o.sem = self.dma_sems[o.eng][n % self.NPOOL]
                o.val = 16 * (n // self.NPOOL + 1)
                if n >= self.NPOOL:
                    o.prewait = (o.sem, 16 * (n // self.NPOOL))
            elif o.kind == "cc":
                self.cc_n += 1
                o.sem = self.cc_sem
                o.val = self.cc_n
            elif o.signal:
                self.cnt[o.eng] += 1
                o.sem = self.sems[o.eng]
                o.val = self.cnt[o.eng]
        dma_final = {}
        for e in self.ENGS:
            n = self.dma_n[e]
            fin = []
            for i in range(min(n, self.NPOOL)):
                last = ((n - 1 - i) // self.NPOOL) * self.NPOOL + i
                fin.append((self.dma_sems[e][i], 16 * (last // self.NPOOL + 1)))
            dma_final[e] = fin
        self.nops += len(ops)

        def emit(e_ops, eng_name):
            def body(e):
                waited = {}
                for o in e_ops:
                    ws = []
                    if o.prewait is not None:
                        ws.append(o.prewait)
                    for d in o.deps:
                        ws.append((d.sem, d.val))
                    need = []
                    for (s, v) in ws:
                        key = id(s)
                        if waited.get(key, 0) >= v:
                            continue
                        waited[key] = v
                        need.append((s, v))
                    emb = None
                    if o.embed and need and EMBED_WAITS:
                        emb = need.pop()
                    for (s, v) in need:
                        e.wait_ge(s, v)
                    ins = o.fn(e)
                    if emb is not None:
                        ins._wait_ge(emb[0], emb[1])
                    if o.kind == "d":
                        ins.then_inc(o.sem, 16)
                    elif o.kind == "cc":
                        ins.then_inc(o.sem)
                    elif o.signal:
                        ins.then_inc(o.sem, 1)
                for (s, v) in dma_final[eng_name]:
                    if waited.get(id(s), 0) < v:
                        e.wait_ge(s, v)
            return body

        with nc.Block() as block:
            reg = {"pe": block.tensor, "act": block.scalar, "dve": block.vector,
                   "pool": block.gpsimd, "sp": block.sync}
            for e in self.ENGS:
                if per[e] or dma_final[e]:
                    reg[e](emit(per[e], e))
        self.ops = []
        self.last_w = {}
        self.readers = {}


class Cfg:
    def __init__(self, seq, depth, debug=False, stab=True, stop=None):
        self.stop = stop
        self.SEQ = seq
        self.DEPTH = depth
        self.LAT = seq // 2
        self.NTOK = self.LAT + CTXH
        self.NW = self.LAT // 512
        self.KT_R = self.NTOK // 128
        self.NKT = 2 * self.KT_R
        self.debug = debug
        self.stab = stab


def layer_dims(l):
    if l % 2 == 0:
        return True, 1024, 1024, 1024, 4096, 0, 1024, 2048, 3072
    return False, 1024, 256, 256, 2560, 0, 1024, 1280, 1536


def build_program(cfg):
    nc = bass.Bass("TRN2", target_bir_lowering=False)
    NTOK, LAT, NW, KT_R, NKT, DEPTH = cfg.NTOK, cfg.LAT, cfg.NW, cfg.KT_R, cfg.NKT, cfg.DEPTH
    dbg_kind = "ExternalOutput" if cfg.debug else "Internal"

    def din(name, shape, dt=F32):
        return nc.dram_tensor(name, list(shape), dt, kind="ExternalInput")

    xin = din("xin", [NTOK, D])
    cin = din("cin", [128, 2, 8])
    ada_w = din("ada_w", [DEPTH, D, 3 * D])
    ada_b = din("ada_b", [DEPTH, 128, 3 * D])
    pre_g = din("pre_g", [DEPTH, 128, D])
    post_g = din("post_g", [DEPTH, 128, D])
    w_out = din("w_out", [DEPTH, D, D])
    NA = (DEPTH + 1) // 2
    NB = max(DEPTH // 2, 1)
    a_w_in = din("a_w_in", [NA, D, 4096])
    a_lambda = din("a_lambda", [NA, 128, 256])
    a_subln = din("a_subln", [NA, 128, 1])
    b_w_in = din("b_w_in", [NB, D, 2560])
    b_qk_g = din("b_qk_g", [NB, 2, 128, 1])
    ropeC = [din("ropeA_C", [128, NTOK]), din("ropeB_C", [128, NTOK])]
    ropeS = [din("ropeA_S", [128, NTOK]), din("ropeB_S", [128, NTOK])]
    ident_in = din("ident", [128, 128], BF16)
    ind_in = din("indmat", [2, 128, 128], BF16)
    out = nc.dram_tensor("out", [LAT, D], F32, kind="ExternalOutput")

    xs = nc.dram_tensor("xs", [NTOK, D], F32, kind=dbg_kind)
    qT, zT, ogT, kTb, kTg, vb, vg = [], [], [], [], [], [], []
    for l in range(DEPTH):
        is_a, FQ, FK, FV, *_ = layer_dims(l)
        qT.append(nc.dram_tensor(f"qT{l}", [FQ, NTOK], BF16, kind=dbg_kind))
        zT.append(nc.dram_tensor(f"zT{l}", [D, NTOK], BF16, kind=dbg_kind))
        ogT.append(nc.dram_tensor(f"ogT{l}", [D, NTOK], BF16, kind=dbg_kind))
        kTb.append(nc.dram_tensor(f"kTb{l}", [FK, NTOK], BF16))
        kTg.append(nc.dram_tensor(f"kTg{l}", [2 * FK, NTOK], BF16))
        vb.append(nc.dram_tensor(f"vb{l}", [(FV // 128) * NTOK, 128], BF16))
        vg.append(nc.dram_tensor(f"vg{l}", [(FV // 128) * 2 * NTOK, 128], BF16))

    import contextlib
    es = contextlib.ExitStack()
    with es:
        def sem(name):
            return es.enter_context(nc.semaphore(name))

        sems = {e: sem(f"s_{e}") for e in Rec.ENGS}
        dma_sems = {e: [sem(f"d_{e}{i}") for i in range(Rec.NPOOL)] for e in ("sp", "pool", "act")}
        dma_sems["pe"] = dma_sems["dve"] = []
        cc_sem = sem("cc")
        R = Rec(nc, sems, dma_sems, cc_sem)

        l_tag = ["g"]

        def sb(st, name, shape, dt):
            return st.enter_context(nc.sbuf_tensor(f"sb{l_tag[0]}_{name}", list(shape), dt))

        def ps(st, name, shape, dt=F32):
            return st.enter_context(nc.psum_tensor(f"ps{l_tag[0]}_{name}", list(shape), dt))

        ident = sb(es, "ident", [128, 128], BF16)
        identf = sb(es, "identf", [128, 128], F32)
        ones_bf = sb(es, "ones_bf", [128, 128], BF16)
        onesf = sb(es, "onesf", [128, 128], F32)
        indm = sb(es, "indm", [128, 2, 128], BF16)
        epsb = sb(es, "epsb", [128, 1], F32)
        cint = sb(es, "cint", [128, 2, 8], F32)
        scs = sb(es, "scs", [128, 2, 8], F32)
        scb = sb(es, "scb", [128, 2, 8, 128], F32)
        Gb = [sb(es, f"Gb{r}", [128, D], F32) for r in range(2)]
        Acol = [sb(es, f"Acol{r}", [128, 8], F32) for r in range(2)]
        Scol = [sb(es, f"Scol{r}", [128, 8], F32) for r in range(2)]

        R.dma("sp", lambda e: e.dma_start(out=ident[:], in_=ident_in[:, :]), writes=["ident"])
        R.dma("sp", lambda e: e.dma_start(out=indm[:], in_=ind_in.ap().rearrange("s p m -> p s m")),
              writes=["indm"])
        R.dma("sp", lambda e: e.dma_start(out=cint[:], in_=cin[:, :, :]), writes=["cint"])
        R.dve(lambda e: e.tensor_copy(out=identf[:], in_=ident[:]), reads=["ident"], writes=["identf"])
        R.dve(lambda e: e.memset(ones_bf[:], 1.0), writes=["ones"])
        R.dve(lambda e: e.memset(onesf[:], 1.0), writes=["onesf"])
        R.dve(lambda e: e.memset(epsb[:], EPS), writes=["eps"])
        import os
        ZS = int(os.environ.get('KDEBUG_ZSTEP', '9'))
        if ZS >= 2:
            R.act(lambda e: e.activation(out=scs[:], in_=cint[:], func=AF.Silu), reads=["cint"], writes=["scs"])
        for r in range(2 if ZS >= 3 else 0):
            for k in range(8):
                R.dve(lambda e, r=r, k=k: e.tensor_scalar(
                    out=scb[:, r, k, :], in0=onesf[:], scalar1=scs[:, r, k:k + 1], scalar2=None,
                    op0=ALU.mult), reads=["onesf", "scs"], writes=[("scb", r, k)])
        R.flush()

        for l in range(DEPTH):
            if cfg.stop == 'Z':
                break
            is_a, FQ, FK, FV, NCOL, colq, colk, colv, colz = layer_dims(l)
            l_tag[0] = str(l)
            j = l // 2
            last = l == DEPTH - 1
            w_in = a_w_in if is_a else b_w_in
            rC, rS = (ropeC[0], ropeS[0]) if is_a else (ropeC[1], ropeS[1])
            src = xin if l == 0 else xs
            NQC = FQ // 128
            NKC = FK // 128
            NVH = FV // 128

            PH = ['M', 'P', 'X', 'A', 'O']
            run_ph = lambda t: cfg.stop is None or PH.index(t) <= PH.index(cfg.stop)
            with contextlib.ExitStack() as st:
                awt = [sb(st, f"awt{i}", [128, 8, 512], F32) for i in range(2)]
                modt = [sb(st, f"modt{r}", [128, 3 * D], F32) for r in range(2)]
                adab = sb(st, "adab", [128, 3 * D], F32)
                pgb = sb(st, "pgb", [128, D], F32)
                qgb = sb(st, "qgb", [128, D], F32)
                tmpA = sb(st, "tmpA", [128, D], F32)
                junk = sb(st, "junkM", [128, 128], F32)
                pm = [ps(st, f"pmM{i}", [128, 512]) for i in range(4)]
                R.dma("sp", lambda e: e.dma_start(out=adab[:], in_=ada_b[l, :, :]), writes=["adab"])
                R.dma("sp", lambda e: e.dma_start(out=pgb[:], in_=pre_g[l, :, :]), writes=["pgb"])
                R.dma("sp", lambda e: e.dma_start(out=qgb[:], in_=post_g[l, :, :]), writes=["qgb"])
                awv = ada_w[l].rearrange("(k p) n -> p k n", p=128)
                for c in range(6):
                    R.dma("sp", lambda e, c=c: e.dma_start(out=awt[c % 2][:], in_=awv[:, :, c * 512:(c + 1) * 512]),
                          writes=[("awt", c % 2)])
                    for r in range(2):
                        bank = pm[(2 * c + r) % 4]

                        def mm(e, c=c, r=r, bank=bank):
                            ins = None
                            for k in range(8):
                                ins = e.matmul(bank[:], scb[:, r, k, :], awt[c % 2][:, k, :],
                                               start=(k == 0), stop=(k == 7))
                            return ins
                        R.pe(mm, reads=[("awt", c % 2)], writes=[("pmM", (2 * c + r) % 4)])
                        R.dve(lambda e, c=c, r=r, bank=bank: e.tensor_tensor(
                            out=modt[r][:, c * 512:(c + 1) * 512], in0=bank[:],
                            in1=adab[:, c * 512:(c + 1) * 512], op=ALU.add),
                            reads=[("pmM", (2 * c + r) % 4), "adab"], writes=[("modt", r, c)])
                import os
                MS = int(os.environ.get('KDEBUG_MSTEP', '9'))
                for r in range(2 if MS >= 2 else 0):
                    allmod = [("modt", r, c) for c in range(6)]
                    R.dve(lambda e, r=r: e.scalar_tensor_tensor(
                        out=tmpA[:], in0=modt[r][:, D:2 * D], scalar=1.0, in1=pgb[:],
                        op0=ALU.add, op1=ALU.mult), reads=allmod + ["pgb"], writes=["tmpA"])
                    for k in range(8):
                        R.dve(lambda e, r=r, k=k: e.scalar_tensor_tensor(
                            out=junk[:], in0=tmpA[:, k * 128:(k + 1) * 128], scalar=1.0, in1=identf[:],
                            op0=ALU.mult, op1=ALU.mult, accum_out=Acol[r][:, k:k + 1]),
                            reads=["tmpA"], writes=["junkM", ("Acol", r)])
                    for k in range(8):
                        R.dve(lambda e, r=r, k=k: e.scalar_tensor_tensor(
                            out=junk[:], in0=modt[r][:, k * 128:(k + 1) * 128], scalar=1.0, in1=identf[:],
                            op0=ALU.mult, op1=ALU.mult, accum_out=Scol[r][:, k:k + 1]),
                            reads=allmod, writes=["junkM", ("Scol", r)])
                    R.dve(lambda e, r=r: e.tensor_tensor(out=Gb[r][:], in0=modt[r][:, 2 * D:3 * D], in1=qgb[:],
                                                         op=ALU.mult), reads=allmod + ["qgb"], writes=[("Gb", r)])
                R.flush()

            if not run_ph('P'):
                continue
            with contextlib.ExitStack() as st:
                wbf = sb(st, "wbf", [128, 8, NCOL], BF16)
                NXT = 8
                xt = [sb(st, f"xt{i}", [128, D], F32) for i in range(NXT)]
                xn = [sb(st, f"xn{i}", [128, D], BF16) for i in range(4)]
                sqj = sb(st, "sqj", [128, D], BF16)
                ssq = [sb(st, f"ssq{i}", [128, 4], F32) for i in range(2)]
                lnv = [sb(st, f"lnv{i}", [128, 4], F32) for i in range(2)]
                rst = [sb(st, f"rst{i}", [128, 4], F32) for i in range(2)]
                hT = [sb(st, f"hT{i}", [128, 8, 512], BF16) for i in range(2)]
                rCt = [sb(st, f"rCt{i}", [128, 512], F32) for i in range(2)]
                rSt = [sb(st, f"rSt{i}", [128, 512], F32) for i in range(2)]
                swt = [sb(st, f"swt{i}", [128, 512], F32) for i in range(2)]
                t1 = [sb(st, f"t1_{i}", [128, 512], F32) for i in range(2)]
                t2 = [sb(st, f"t2_{i}", [128, 512], F32) for i in range(2)]
                NOC = 4
                oc = [sb(st, f"oc{i}", [128, 512], BF16) for i in range(NOC)]
                vo = [sb(st, f"vo{i}", [128, FV], BF16) for i in range(2)]
                if not is_a:
                    sqb = [sb(st, f"sqb{i}", [128, 512], BF16) for i in range(2)]
                    lnt = [sb(st, f"lnt{i}", [128, 512], F32) for i in range(2)]
                    rsb = [sb(st, f"rsb{i}", [128, 512], F32) for i in range(2)]
                    qn = [sb(st, f"qn{i}", [128, 512], F32) for i in range(2)]
                    gqk = sb(st, "gqk", [128, 2], F32)
                tp = [ps(st, f"tp{i}", [128, 1024], BF16) for i in range(2)]
                pm = [ps(st, f"pmP{i}", [128, 512]) for i in range(4)]
                if not is_a:
                    pss = [ps(st, f"pss{i}", [128, 512]) for i in range(2)]

                wv = w_in[j].rearrange("(k p) n -> p k n", p=128)
                CW = 1024
                for k in range(8):
                    for c0 in range(0, NCOL, CW):
                        c1 = min(NCOL, c0 + CW)
                        R.dma("pool", lambda e, k=k, c0=c0, c1=c1: e.dma_start(out=wbf[:, k, c0:c1], in_=wv[:, k, c0:c1]),
                              writes=[("wbf", k, c0)])
                wkeys = [("wbf", k, c0) for k in range(8) for c0 in range(0, NCOL, CW)]
                if not is_a:
                    for i in range(2):
                        R.dma("sp", lambda e, i=i: e.dma_start(out=gqk[:, i:i + 1], in_=b_qk_g[j, i, :, :]),
                              writes=[("gqk", i)])

                cnt = {"xt": 0, "oc": 0, "pm": 0, "vo": 0, "tp": 0, "rp": 0, "b": 0}
                PS = int(os.environ.get('KDEBUG_PSTEP', '9'))
                xts_by_w = {}

                def issue_loads(w_):
                    T_ = 512 if w_ < NW else 128
                    tok0_ = w_ * 512 if w_ < NW else LAT
                    wi_ = w_ % 2
                    lst = []
                    for jj in range(T_ // 128):
                        xi = cnt["xt"] % NXT
                        cnt["xt"] += 1
                        lst.append(xi)
                        R.dma("sp", lambda e, xi=xi, jj=jj, tok0_=tok0_: e.dma_start(
                            out=xt[xi][:], in_=src[tok0_ + jj * 128: tok0_ + (jj + 1) * 128, :]),
                            reads=[("xs", tok0_ + jj * 128)], writes=[("xt", xi)])
                    xts_by_w[w_] = lst
                    R.dma("sp", lambda e, wi_=wi_, tok0_=tok0_, T_=T_: e.dma_start(out=rCt[wi_][:, 0:T_], in_=rC[:, tok0_:tok0_ + T_]),
                          writes=[("rCt", wi_)])
                    R.dma("sp", lambda e, wi_=wi_, tok0_=tok0_, T_=T_: e.dma_start(out=rSt[wi_][:, 0:T_], in_=rS[:, tok0_:tok0_ + T_]),
                          writes=[("rSt", wi_)])
                def tile(w, part, inject=None):
                    T = 512 if w < NW else 128
                    nsub = T // 128
                    tok0 = w * 512 if w < NW else LAT
                    r = 0 if w < NW else 1
                    wi = w % 2
                    hkeys = [("hT", wi, k) for k in range(8)]
                    if part == "front":
                        xts = xts_by_w[w]
                        for jj in range(nsub):
                            xi = xts[jj]
                            R.act(lambda e, xi=xi, jj=jj, wi=wi: e.activation(
                                out=sqj[:], in_=xt[xi][:], func=AF.Square, accum_out=ssq[wi][:, jj:jj + 1]),
                                reads=[("xt", xi)], writes=["sqj", ("ssq", wi, jj)])
                        sskeys = [("ssq", wi, jj) for jj in range(nsub)]
                        R.act(lambda e, wi=wi, nsub=nsub: e.activation(
                            out=lnv[wi][:, 0:nsub], in_=ssq[wi][:, 0:nsub], func=AF.Ln, bias=epsb[:], scale=1.0 / D),
                            reads=sskeys + ["eps"], writes=[("lnv", wi)])
                        R.act(lambda e, wi=wi, nsub=nsub: e.activation(
                            out=rst[wi][:, 0:nsub], in_=lnv[wi][:, 0:nsub], func=AF.Exp, scale=-0.5),
                            reads=[("lnv", wi)], writes=[("rst", wi)])
                        for jj in range(nsub):
                            R.act(lambda e, xi=xts[jj], jj=jj, wi=wi: e.activation(
                                out=xn[jj][:], in_=xt[xi][:], func=AF.Copy, scale=rst[wi][:, jj:jj + 1]),
                                reads=[("xt", xts[jj]), ("rst", wi)], writes=[("xn", jj)])
                        if PS < 3:
                            return
                        for kk in range(4):
                            ti = cnt["tp"] % 2
                            cnt["tp"] += 1

                            def trs(e, kk=kk, ti=ti, nsub=nsub):
                                ins = None
                                for half in range(2):
                                    k = 2 * kk + half
                                    for jj in range(nsub):
                                        ins = e.transpose(tp[ti][:, half * 512 + jj * 128: half * 512 + (jj + 1) * 128],
                                                          xn[jj][:, k * 128:(k + 1) * 128], ident[:])
                                return ins
                            R.pe(trs, reads=[("xn", jj) for jj in range(nsub)] + ["ident"], writes=[("tp", ti)])
                            for half in range(2):
                                k = 2 * kk + half
                                R.dve(lambda e, k=k, ti=ti, half=half, wi=wi, T=T, r=r: e.tensor_scalar(
                                    out=hT[wi][:, k, 0:T], in0=tp[ti][:, half * 512: half * 512 + T],
                                    scalar1=Acol[r][:, k:k + 1], scalar2=Scol[r][:, k:k + 1],
                                    op0=ALU.mult, op1=ALU.add),
                                    reads=[("tp", ti), ("Acol", r), ("Scol", r)], writes=[("hT", wi, k)])
                        return

                    if PS < 4:
                        return
                    chunks = []
                    qk = [("q", c, colq + c * 128) for c in range(NQC)] + [("k", c, colk + c * 128) for c in range(NKC)]
                    zc = [("z", c, colz + c * 128) for c in range(8)]
                    per = max(1, len(qk) // len(zc)) if is_a else len(qk)
                    while qk or zc:
                        for _ in range(per):
                            if qk:
                                chunks.append(qk.pop(0))
                        if zc:
                            chunks.append(zc.pop(0))

                    def stage1(ch):
                        kind, c, col0 = ch
                        bi = cnt["pm"] % 4
                        cnt["pm"] += 1

                        def mm(e, col0=col0, bi=bi, wi=wi, T=T):
                            ins = None
                            for k in range(8):
                                ins = e.matmul(pm[bi][:, 0:T], wbf[:, k, col0:col0 + 128], hT[wi][:, k, 0:T],
                                               start=(k == 0), stop=(k == 7))
                            return ins
                        R.pe(mm, reads=hkeys + wkeys, writes=[("pm", bi)])
                        return bi

                    def stage2(ch, bi):
                        with R.embedding():
                            stage2_(ch, bi)

                    def stage2_(ch, bi):
                        kind, c, col0 = ch
                        oi = cnt["oc"] % NOC
                        cnt["oc"] += 1
                        if kind == "z":
                            R.act(lambda e, bi=bi, oi=oi, T=T: e.activation(out=oc[oi][:, 0:T], in_=pm[bi][:, 0:T], func=AF.Silu),
                                  reads=[("pm", bi)], writes=[("oc", oi)])
                            dst = zT[l]
                            dkey = ("zT", c, w)
                        else:
                            ri = cnt["rp"] % 2
                            cnt["rp"] += 1
                            if is_a:
                                srcap = pm[bi]
                                skey = ("pm", bi)
                            else:
                                b = cnt["b"] % 2
                                cnt["b"] += 1
                                gi = 0 if kind == "q" else 1
                                R.act(lambda e, bi=bi, b=b, T=T: e.activation(out=sqb[b][:, 0:T], in_=pm[bi][:, 0:T], func=AF.Square),
                                      reads=[("pm", bi)], writes=[("sqb", b)])
                                R.pe(lambda e, b=b, T=T: e.matmul(pss[b][:, 0:T], ones_bf[:], sqb[b][:, 0:T], start=True, stop=True),
                                     reads=[("sqb", b), "ones"], writes=[("pss", b)])
                                R.act(lambda e, b=b, T=T: e.activation(out=lnt[b][:, 0:T], in_=pss[b][:, 0:T], func=AF.Ln,
                                                                      bias=epsb[:], scale=1.0 / 128),
                                      reads=[("pss", b), "eps"], writes=[("lnt", b)])
                                R.act(lambda e, b=b, T=T: e.activation(out=rsb[b][:, 0:T], in_=lnt[b][:, 0:T], func=AF.Exp, scale=-0.5),
                                      reads=[("lnt", b)], writes=[("rsb", b)])
                                R.dve(lambda e, b=b, bi=bi, gi=gi, T=T: e.scalar_tensor_tensor(
                                    out=qn[b][:, 0:T], in0=pm[bi][:, 0:T], scalar=gqk[:, gi:gi + 1], in1=rsb[b][:, 0:T],
                                    op0=ALU.mult, op1=ALU.mult),
                                    reads=[("pm", bi), ("rsb", b), ("gqk", gi)], writes=[("qn", b)])
                                srcap = qn[b]
                                skey = ("qn", b)
                            R.dve(lambda e, srcap=srcap, ri=ri, T=T: e.tensor_copy(out=swt[ri][0:64, 0:T], in_=srcap[64:128, 0:T]),
                                  reads=[skey], writes=[("swt", ri, 0)])
                            R.dve(lambda e, srcap=srcap, ri=ri, T=T: e.tensor_copy(out=swt[ri][64:128, 0:T], in_=srcap[0:64, 0:T]),
                                  reads=[skey], writes=[("swt", ri, 1)])
                            R.dve(lambda e, srcap=srcap, ri=ri, wi=wi, T=T: e.tensor_tensor(
                                out=t1[ri][:, 0:T], in0=srcap[:, 0:T], in1=rCt[wi][:, 0:T], op=ALU.mult),
                                reads=[skey, ("rCt", wi)], writes=[("t1", ri)])
                            R.pool(lambda e, ri=ri, wi=wi, T=T: e.tensor_tensor(
                                out=t2[ri][:, 0:T], in0=swt[ri][:, 0:T], in1=rSt[wi][:, 0:T], op=ALU.mult),
                                reads=[("swt", ri, 0), ("swt", ri, 1), ("rSt", wi)], writes=[("t2", ri)])
                            R.pool(lambda e, ri=ri, oi=oi, T=T: e.tensor_tensor(
                                out=oc[oi][:, 0:T], in0=t1[ri][:, 0:T], in1=t2[ri][:, 0:T], op=ALU.add),
                                reads=[("t1", ri), ("t2", ri)], writes=[("oc", oi)])
                            dst = qT[l] if kind == "q" else kTb[l]
                            dkey = ("qT" if kind == "q" else "kTb", c, w)
                        R.dma("sp", lambda e, dst=dst, c=c, oi=oi, tok0=tok0, T=T: e.dma_start(
                            out=dst[c * 128:(c + 1) * 128, tok0:tok0 + T], in_=oc[oi][:, 0:T]),
                            reads=[("oc", oi)], writes=[dkey])

                    prev = None
                    for ci, ch in enumerate(chunks):
                        bi = stage1(ch)
                        if prev is not None:
                            stage2(*prev)
                        prev = (ch, bi)
                        if inject is not None and ci == len(chunks) // 3:
                            inject()
                    vprev = None
                    for jj in range(nsub if PS >= 6 else 0):
                        vi = cnt["vo"] % 2
                        cnt["vo"] += 1
                        for c0 in range(0, FV, 512):
                            cw = min(512, FV - c0)
                            bi = cnt["pm"] % 4
                            cnt["pm"] += 1

                            def mmv(e, jj=jj, c0=c0, cw=cw, bi=bi, wi=wi):
                                ins = None
                                for k in range(8):
                                    ins = e.matmul(pm[bi][:, 0:cw], hT[wi][:, k, jj * 128:(jj + 1) * 128],
                                                   wbf[:, k, colv + c0: colv + c0 + cw], start=(k == 0), stop=(k == 7))
                                return ins
                            R.pe(mmv, reads=hkeys + wkeys, writes=[("pm", bi)])
                            if prev is not None:
                                stage2(*prev)
                                prev = None
                            if os.environ.get('KDEBUG_VCOPY', 'act') == 'dve':
                                R.dve(lambda e, vi=vi, c0=c0, cw=cw, bi=bi: e.tensor_copy(out=vo[vi][:, c0:c0 + cw], in_=pm[bi][:, 0:cw]),
                                      reads=[("pm", bi)], writes=[("vo", vi, c0)])
                            else:
                                R.act(lambda e, vi=vi, c0=c0, cw=cw, bi=bi: e.copy(out=vo[vi][:, c0:c0 + cw], in_=pm[bi][:, 0:cw]),
                                      reads=[("pm", bi)], writes=[("vo", vi, c0)])
                        R.dma("sp", lambda e, vi=vi, jj=jj, tok0=tok0: e.dma_start(
                            out=vb[l].rearrange("(h t) d -> t h d", h=NVH)[tok0 + jj * 128: tok0 + (jj + 1) * 128, :, :],
                            in_=vo[vi][:].rearrange("p (h d) -> p h d", h=NVH)),
                            reads=[("vo", vi, c0) for c0 in range(0, FV, 512)], writes=[("vb", w, jj)])
                    if prev is not None:
                        stage2(*prev)
                        prev = None
                if PS >= 2:
                    issue_loads(0)
                    tile(0, "front")
                    for w in range(NW + 1):
                        if w + 1 <= NW:
                            issue_loads(w + 1)
                        tile(w, "back", inject=(lambda w=w: tile(w + 1, "front")) if w + 1 <= NW else None)
                R.flush()

            if not run_ph('X'):
                continue
            RG = [[2 * p, 2 * p + 1] for p in range(NCORES // 2)]
            for c in range(max(NKC, NVH)):
                if c < NKC:
                    R.op("pool", lambda e, c=c: e.collective_compute(
                        "AllGather", ALU.bypass, replica_groups=RG,
                        ins=[kTb[l][c * 128:(c + 1) * 128, :].opt()], outs=[kTg[l][c * 256:(c + 1) * 256, :].opt()]),
                        writes=[("kTg", c)], kind="cc")
                if c < NVH:
                    R.op("pool", lambda e, c=c: e.collective_compute(
                        "AllGather", ALU.bypass, replica_groups=RG,
                        ins=[vb[l][c * NTOK:(c + 1) * NTOK, :].opt()], outs=[vg[l][c * 2 * NTOK:(c + 1) * 2 * NTOK, :].opt()]),
                        writes=[("vg", c)], kind="cc")
            if cfg.stop == 'X':
                R.flush()
                R.op("pool", lambda e: e.wait_ge(cc_sem, R.cc_n), kind="c")
                R.flush()

            if not run_ph('A'):
                continue
            with contextlib.ExitStack() as st:
                NU = 2 if is_a else 1
                kTs = [sb(st, f"kTs{i}", [128, 2 * NTOK], BF16) for i in range(2)]
                vs = [sb(st, f"vs{i}", [128, NKT, 128], BF16) for i in range(2)]
                qs = [[sb(st, f"qs{i}_{s}", [128, NTOK], BF16) for s in range(NU)] for i in range(2)]
                zs = [sb(st, f"zs{i}", [128, NTOK], BF16) for i in range(2)]
                NPT = 8
                pt = [sb(st, f"pt{i}", [128, 1024], BF16) for i in range(NPT)]
                accS = [sb(st, f"accS{i}", [128, 512], BF16) for i in range(2)]
                accP = [sb(st, f"accP{i}", [128, 1024], BF16) for i in range(2)]
                rr = [sb(st, f"rr{i}", [128, 512], F32) for i in range(2)]
                o_s = [sb(st, f"o_s{i}", [128, 512], F32) for i in range(2)]
                ocmb = sb(st, "ocmb", [128, 512], F32)
                sqe = sb(st, "sqe", [128, 512], BF16)
                sqo = [sb(st, f"sqo{i}", [128, 512], BF16) for i in range(2)]
                lno = sb(st, "lno", [128, 512], F32)
                rso = sb(st, "rso", [128, 512], F32)
                ono = sb(st, "ono", [128, 512], F32)
                ogs = [sb(st, f"ogs{i}", [128, 512], BF16) for i in range(2)]
                negM = [sb(st, f"negM{i}", [128, 1], F32) for i in range(4)]
                stt = sb(st, "stt", [128, 8], F32)
                stg = sb(st, "stg", [128, 64], F32)
                kmx = [sb(st, f"kmx{i}", [128, 2], F32) for i in range(2)]
                qmx = sb(st, "qmx", [128, 2], F32)
                if is_a:
                    lamt = sb(st, "lamt", [128, 256], F32)
                    lamj = sb(st, "lamj", [128, 64], F32)
                    lams = sb(st, "lams", [128, 4], F32)
                    neglam = sb(st, "neglam", [128, 1], F32)
                    gsub = sb(st, "gsub", [128, 1], F32)
                Sg = [ps(st, f"Sg{i}", [128, 1024]) for i in range(2)]
                Ob = [ps(st, f"Ob{i}", [128, 512]) for i in range(2)]
                Lb = ps(st, "Lb", [128, 512])
                accPS = ps(st, "accPS", [128, 512])
                pstat = [(Sg[0][:, 0:512], ("S", 0, 0)), (Sg[0][:, 512:1024], ("S", 0, 1)),
                         (Sg[1][:, 0:512], ("S", 1, 0)), (Sg[1][:, 512:1024], ("S", 1, 1))]

                scale = (64 ** -0.5) if is_a else (128 ** -0.5)
                if is_a:
                    lam_init = lambda_init_fn(l)
                    for i in range(2):
                        for s in range(2):
                            R.dve(lambda e, i=i, s=s: e.memset(qs[i][s][:], 0.0), writes=[("qs", i, s), ("qs2", i, s)])
                    R.dma("sp", lambda e: e.dma_start(out=lamt[:], in_=a_lambda[j, :, :]), writes=["lamt"])
                    R.dma("sp", lambda e: e.dma_start(out=gsub[:], in_=a_subln[j, :, :]), writes=["gsub0"])
                    for i in range(2):
                        R.dve(lambda e, i=i: e.scalar_tensor_tensor(
                            out=lamj[:], in0=lamt[:, (2 * i) * 64:(2 * i + 1) * 64], scalar=1.0,
                            in1=lamt[:, (2 * i + 1) * 64:(2 * i + 2) * 64], op0=ALU.mult, op1=ALU.mult,
                            accum_out=lams[:, i:i + 1]), reads=["lamt"], writes=["lamj", ("lams", i)])
                    R.act(lambda e: e.activation(out=lams[:, 2:4], in_=lams[:, 0:2], func=AF.Exp),
                          reads=[("lams", 0), ("lams", 1)], writes=["lame"])
                    R.dve(lambda e: e.scalar_tensor_tensor(out=neglam[:], in0=lams[:, 3:4], scalar=-lam_init,
                                                           in1=lams[:, 2:3], op0=ALU.add, op1=ALU.subtract),
                          reads=["lame"], writes=["neglam"])
                    R.dve(lambda e: e.tensor_scalar(out=gsub[:], in0=gsub[:], scalar1=(1.0 - lam_init), scalar2=None,
                                                    op0=ALU.mult), reads=["gsub0"], writes=["gsub"])

                FAST_RECIP = os.environ.get("KDEBUG_FASTRECIP", "0") == "1"
                pending = []
                state = {"unit": 0, "og": 0, "kvslot": -1, "kvhead": -1, "nm": 0, "sq": 0, "g": 0, "pb": 0}

                def drain(n=None, upto=None):
                    k = len(pending) if n is None else min(n, len(pending))
                    for _ in range(k):
                        if upto is not None and pending[0][0] > upto:
                            break
                        with R.embedding():
                            pending.pop(0)[1]()

                def defer(fn):
                    pending.append((state["unit"] - 1, fn))

                def flush_tail():
                    if state.get("tail") is not None:
                        t_ = state["tail"]
                        state["tail"] = None
                        t_()

                def maxsq(srct, ncols, rkeys, outs):
                    nch = 0
                    R.auto_embed = True
                    for c0 in range(0, ncols, 512):
                        cw = min(512, ncols - c0)
                        b = state["sq"] % 2
                        state["sq"] += 1
                        R.act(lambda e, b=b, c0=c0, cw=cw: e.activation(out=sqo[b][:, 0:cw], in_=srct[:, c0:c0 + cw], func=AF.Square),
                              reads=rkeys, writes=[("sqo", b)])
                        for oi_, (ind, dst, dkey) in enumerate(outs):
                            pb, pkey = pstat[state["pb"] % 4]
                            state["pb"] += 1
                            R.pe(lambda e, b=b, cw=cw, pb=pb, ind=ind: e.matmul(pb[:, 0:cw], ind, sqo[b][:, 0:cw], start=True, stop=True),
                                 reads=[("sqo", b)], writes=[pkey])
                            col = oi_ * 20 + nch
                            R.dve(lambda e, cw=cw, pb=pb, col=col: e.tensor_reduce(out=stg[:, col:col + 1], in_=pb[:, 0:cw], axis=AX.X, op=ALU.max),
                                  reads=[pkey], writes=[("stg", col)])
                        nch += 1
                    for oi_, (ind, dst, dkey) in enumerate(outs):
                        R.dve(lambda e, nch=nch, oi_=oi_, dst=dst: e.tensor_reduce(out=dst, in_=stg[:, oi_ * 20: oi_ * 20 + nch], axis=AX.X, op=ALU.max),
                              reads=[("stg", oi_ * 20 + i) for i in range(nch)], writes=[dkey])
                    R.auto_embed = False

                NH = 8
                for h in range(NH):
                    hs = h % 2
                    hk = h if is_a else h // 4
                    if hk != state["kvhead"]:
                        state["kvhead"] = hk
                        state["kvslot"] = (state["kvslot"] + 1) % 2
                        ks = state["kvslot"]
                        for rk in range(2):
                            R.dma("sp", lambda e, ks=ks, rk=rk, hk=hk: e.dma_start(
                                out=kTs[ks][:, rk * NTOK:(rk + 1) * NTOK],
                                in_=kTg[l][hk * 256 + rk * 128: hk * 256 + (rk + 1) * 128, :]),
                                reads=[("kTg", hk)], writes=[("kTs", ks, rk)])
                            vgv = vg[l][hk * 2 * NTOK:(hk + 1) * 2 * NTOK, :].rearrange("(kt p) f -> p kt f", p=128)
                            for part in range(4):
                                k0 = rk * KT_R + (KT_R * part) // 4
                                k1 = rk * KT_R + (KT_R * (part + 1)) // 4
                                if k1 > k0:
                                    R.dma("sp", lambda e, ks=ks, k0=k0, k1=k1, vgv=vgv: e.dma_start(
                                        out=vs[ks][:, k0:k1, :], in_=vgv[:, k0:k1, :]),
                                        reads=[("vg", hk)], writes=[("vs", ks, rk, part)])
                        if cfg.stab:
                            maxsq(kTs[ks], 2 * NTOK, [("kTs", ks, 0), ("kTs", ks, 1)],
                                  [((indm[:, s, :] if is_a else ones_bf[:]), kmx[ks][:, s:s + 1], ("kmx", ks, s)) for s in range(NU)])
                    ks = state["kvslot"]
                    if is_a:
                        for s in range(2):
                            for half in range(2):
                                p0 = 64 * half + 32 * s
                                R.dma("sp", lambda e, hs=hs, s=s, p0=p0, h=h: e.dma_start(
                                    out=qs[hs][s][p0:p0 + 32, :], in_=qT[l][h * 128 + p0: h * 128 + p0 + 32, :]),
                                    writes=[("qs", hs, s)] if half == 0 else [("qs2", hs, s)])
                    else:
                        R.dma("sp", lambda e, hs=hs, h=h: e.dma_start(out=qs[hs][0][:], in_=qT[l][h * 128:(h + 1) * 128, :]),
                              writes=[("qs", hs, 0)])
                    R.dma("sp", lambda e, hs=hs, h=h: e.dma_start(out=zs[hs][:], in_=zT[l][h * 128:(h + 1) * 128, :]),
                          writes=[("zs", hs)])

                    nm = []
                    for s in range(NU):
                        mi = state["nm"] % 4
                        state["nm"] += 1
                        nm.append(mi)
                        if cfg.stab:
                            maxsq(qs[hs][s], NTOK, [("qs", hs, s), ("qs2", hs, s)], [(ones_bf[:], qmx[:, s:s + 1], ("qmx", s))])
                            R.dve(lambda e, ks=ks, s=s: e.tensor_tensor(out=stt[:, 3:4], in0=kmx[ks][:, s:s + 1], in1=qmx[:, s:s + 1], op=ALU.mult),
                                  reads=[("kmx", ks, s), ("qmx", s)], writes=[("stt", 3)])
                            R.act(lambda e: e.activation(out=stt[:, 4:5], in_=stt[:, 3:4], func=AF.Ln, bias=epsb[:], scale=1.0),
                                  reads=[("stt", 3), "eps"], writes=[("stt", 4)])
                            R.act(lambda e: e.activation(out=stt[:, 5:6], in_=stt[:, 4:5], func=AF.Exp, scale=0.5),
                                  reads=[("stt", 4)], writes=[("stt", 5)])
                            R.dve(lambda e, mi=mi: e.tensor_scalar(out=negM[mi][:], in0=stt[:, 5:6], scalar1=-scale, scalar2=None, op0=ALU.mult),
                                  reads=[("stt", 5)], writes=[("negM", mi)])
                        else:
                            R.dve(lambda e, mi=mi: e.memset(negM[mi][:], 0.0), writes=[("negM", mi)])

                    def vpart(kt):
                        rk = kt // KT_R
                        for pp in range(4):
                            k0 = rk * KT_R + (KT_R * pp) // 4
                            k1 = rk * KT_R + (KT_R * (pp + 1)) // 4
                            if k0 <= kt < k1:
                                return rk, pp
                        return rk, 3

                    nqt = NW + (0 if last else 1)
                    for w in range(nqt):
                        T = 512 if w < NW else 128
                        tok0 = w * 512 if w < NW else LAT
                        ktl = list(range(NKT)) if w < NW else [KT_R - 1, 2 * KT_R - 1]
                        groups = [ktl[i:i + 2] for i in range(0, len(ktl), 2)]
                        if w >= NW:
                            flush_tail()
                            drain()
                        for s in range(NU):
                            ob = state["unit"] % 2
                            ab = ob
                            state["unit"] += 1
                            qsk = [("qs", hs, s), ("qs2", hs, s)] if is_a else [("qs", hs, s)]
                            mi = nm[s]

                            def QK(gi, grp, s=s, T=T, tok0=tok0, ks=ks, hs=hs, qsk=qsk):
                                for a, kt in enumerate(grp):
                                    R.pe(lambda e, gi=gi, a=a, kt=kt: e.matmul(
                                        Sg[gi][:, a * 512: a * 512 + T], kTs[ks][:, kt * 128:(kt + 1) * 128],
                                        qs[hs][s][:, tok0:tok0 + T], start=True, stop=True),
                                        reads=[("kTs", ks, kt // KT_R)] + qsk, writes=[("S", gi, a)]).embed = True
                            ng = len(groups)
                            drain(upto=state["unit"] - 3)
                            step = max(1, (ng - 1) // (len(pending) + 1))
                            used = {"d": False, "p": False}
                            QK(state["g"] % 2, groups[0])
                            flush_tail()
                            prevPV = None
                            for gidx, grp in enumerate(groups):
                                g = state["g"]
                                state["g"] += 1
                                gi = g % 2
                                pi = g % NPT
                                na = len(grp)
                                if gidx + 1 < ng:
                                    QK((g + 1) % 2, groups[gidx + 1])
                                S3 = Sg[gi][:].rearrange("p (a t) -> p a t", a=2)[:, 0:na, 0:T]
                                P3 = pt[pi][:].rearrange("p (a t) -> p a t", a=2)[:, 0:na, 0:T]
                                R.act(lambda e, S3=S3, P3=P3, mi=mi: e.activation(out=P3, in_=S3, func=AF.Exp, bias=negM[mi][:], scale=scale),
                                      reads=[("S", gi, a) for a in range(na)] + [("negM", mi)], writes=[("pt", pi)]).embed = True
                                if prevPV is not None:
                                    prevPV()

                                def PV(grp=grp, gidx=gidx, na=na, pi=pi, ob=ob, T=T, ks=ks, ng=ng):
                                    for a, kt in enumerate(grp):
                                        rk, part = vpart(kt)
                                        first = (gidx == 0 and a == 0)
                                        lastmm = (gidx == ng - 1 and a == na - 1)
                                        R.pe(lambda e, a=a, kt=kt, first=first, lastmm=lastmm: e.matmul(
                                            Ob[ob][:, 0:T], vs[ks][:, kt, :], pt[pi][:, a * 512: a * 512 + T], start=first, stop=lastmm),
                                            reads=[("vs", ks, rk, part), ("pt", pi)], writes=[("O", ob)]).embed = True
                                prevPV = PV
                                use_d = (gidx % 2 == 0)
                                if use_d:
                                    for a in range(na):
                                        src_ = pt[pi][:, a * 512: a * 512 + T]
                                        if not used["d"]:
                                            used["d"] = True
                                            R.dve(lambda e, src_=src_, T=T: e.tensor_copy(out=accPS[:, 0:T], in_=src_),
                                                  reads=[("pt", pi)], writes=["accPS"]).embed = True
                                        else:
                                            R.dve(lambda e, src_=src_, T=T: e.tensor_tensor(out=accPS[:, 0:T], in0=accPS[:, 0:T], in1=src_, op=ALU.add),
                                                  reads=[("pt", pi), "accPS"], writes=["accPS"]).embed = True
                                else:
                                    A3 = accP[ab][:].rearrange("p (a t) -> p a t", a=2)[:, 0:na, 0:T]
                                    akey = ("accP", ab)
                                    if not used["p"]:
                                        used["p"] = True
                                        R.pool(lambda e, A3=A3, P3=P3: e.tensor_copy(out=A3, in_=P3), reads=[("pt", pi)], writes=[akey]).embed = True
                                    else:
                                        R.pool(lambda e, A3=A3, P3=P3: e.tensor_tensor(out=A3, in0=A3, in1=P3, op=ALU.add),
                                               reads=[("pt", pi), akey], writes=[akey]).embed = True
                                if gidx >= 1 and gidx % step == 0:
                                    drain(1)
                            state["tail"] = prevPV
                            na0 = len(groups[0])
                            srcs = []
                            if used["d"]:
                                R.dve(lambda e, ab=ab, T=T: e.tensor_copy(out=accS[ab][:, 0:T], in_=accPS[:, 0:T]),
                                      reads=["accPS"], writes=[("accS", ab)])
                                srcs += [(accS[ab], ("accS", ab), 0)]
                            if used["p"]:
                                srcs += [(accP[ab], ("accP", ab), a) for a in range(na0)]

                            def Lstage(srcs=srcs, T=T):
                                def mmL(e):
                                    ins = None
                                    for i, (t_, k_, a) in enumerate(srcs):
                                        ins = e.matmul(Lb[:, 0:T], ones_bf[:], t_[:, a * 512: a * 512 + T],
                                                       start=(i == 0), stop=(i == len(srcs) - 1))
                                    return ins
                                R.pe(mmL, reads=list({k_ for (_, k_, _) in srcs}), writes=["Lb"])
                            defer(Lstage)
                            defer(lambda s=s, T=T: R.dve(
                                lambda e: (e.reciprocal_approx_fast(out=rr[s][:, 0:T], in_=Lb[:, 0:T]) if FAST_RECIP
                                           else e.reciprocal(out=rr[s][:, 0:T], in_=Lb[:, 0:T])),
                                reads=["Lb"], writes=[("rr", s)]))
                            defer(lambda ob=ob, s=s, T=T: R.dve(
                                lambda e: e.tensor_tensor(out=o_s[s][:, 0:T], in0=Ob[ob][:, 0:T], in1=rr[s][:, 0:T], op=ALU.mult),
                                reads=[("O", ob), ("rr", s)], writes=[("o_s", s)]))
                        oi = state["og"] % 2
                        state["og"] += 1
                        if is_a:
                            defer(lambda T=T: R.dve(
                                lambda e: e.scalar_tensor_tensor(out=ocmb[:, 0:T], in0=o_s[1][:, 0:T], scalar=neglam[:],
                                                                 in1=o_s[0][:, 0:T], op0=ALU.mult, op1=ALU.add),
                                reads=[("o_s", 0), ("o_s", 1), "neglam"], writes=["ocmb"]))
                            defer(lambda T=T: R.act(
                                lambda e: e.activation(out=sqe[:, 0:T], in_=ocmb[:, 0:T], func=AF.Square),
                                reads=["ocmb"], writes=["sqe"]))
                            defer(lambda T=T: R.pe(
                                lambda e: e.matmul(Lb[:, 0:T], ones_bf[:], sqe[:, 0:T], start=True, stop=True),
                                reads=["sqe"], writes=["Lb"]))
                            defer(lambda T=T: R.act(
                                lambda e: e.activation(out=lno[:, 0:T], in_=Lb[:, 0:T], func=AF.Ln, bias=epsb[:], scale=1.0 / 128),
                                reads=["Lb"], writes=["lno"]))
                            defer(lambda T=T: R.act(
                                lambda e: e.activation(out=rso[:, 0:T], in_=lno[:, 0:T], func=AF.Exp, scale=-0.5),
                                reads=["lno"], writes=["rso"]))
                            defer(lambda T=T: R.dve(
                                lambda e: e.scalar_tensor_tensor(out=ono[:, 0:T], in0=ocmb[:, 0:T], scalar=gsub[:],
                                                                 in1=rso[:, 0:T], op0=ALU.mult, op1=ALU.mult),
                                reads=["ocmb", "rso", "gsub"], writes=["ono"]))
                            fin_src, fin_key = ono, "ono"
                        else:
                            fin_src, fin_key = o_s[0], ("o_s", 0)
                        defer(lambda T=T, oi=oi, tok0=tok0, fin_src=fin_src, fin_key=fin_key, hs=hs: R.dve(
                            lambda e: e.tensor_tensor(out=ogs[oi][:, 0:T], in0=fin_src[:, 0:T], in1=zs[hs][:, tok0:tok0 + T], op=ALU.mult),
                            reads=[fin_key, ("zs", hs)], writes=[("ogs", oi)]))
                        defer(lambda T=T, oi=oi, tok0=tok0, h=h, w=w: R.dma(
                            "pool", lambda e: e.dma_start(out=ogT[l][h * 128:(h + 1) * 128, tok0:tok0 + T], in_=ogs[oi][:, 0:T]),
                            reads=[("ogs", oi)], writes=[("ogT", h, w)]))
                flush_tail()
                drain()
                R.flush()

            if not run_ph('O'):
                continue
            with contextlib.ExitStack() as st:
                wo = sb(st, "wo", [128, 8, D], BF16)
                og = [sb(st, f"og{i}", [128, 8, 512], BF16) for i in range(2)]
                NXO = 6
                xo = [sb(st, f"xo{i}", [128, D], F32) for i in range(NXO)]
                yt = [sb(st, f"yt{i}", [128, D], F32) for i in range(2)]
                xw = [sb(st, f"xw{i}", [128, D], F32) for i in range(3)]
                sqy = sb(st, "sqy", [128, 512], BF16)
                ssy = [sb(st, f"ssy{i}", [128, 4], F32) for i in range(2)]
                py = [ps(st, f"py{i}", [128, 512]) for i in range(4)]
                wov = w_out[l].rearrange("(k p) n -> p k n", p=128)
                for k in range(8):
                    R.dma("pool", lambda e, k=k: e.dma_start(out=wo[:, k, :], in_=wov[:, k, :]), writes=[("wo", k)])
                wokeys = [("wo", k) for k in range(8)]
                ogv = ogT[l].rearrange("(k p) t -> p k t", p=128)
                tcount = 0
                nwt = NW + (0 if last else 1)
                for w in range(nwt):
                    T = 512 if w < NW else 128
                    nsub = T // 128
                    tok0 = w * 512 if w < NW else LAT
                    r = 0 if w < NW else 1
                    wi = w % 2
                    R.dma("sp", lambda e, wi=wi, tok0=tok0, T=T: e.dma_start(out=og[wi][:, :, 0:T], in_=ogv[:, :, tok0:tok0 + T]),
                          writes=[("og", wi)])
                    for jj in range(nsub):
                        xi = tcount % NXO
                        yi = tcount % 2
                        xwi = tcount % 3
                        tcount += 1
                        t0 = tok0 + jj * 128
                        R.dma("sp", lambda e, xi=xi, t0=t0: e.dma_start(out=xo[xi][:], in_=src[t0:t0 + 128, :]), writes=[("xo", xi)])
                        for nh in range(2):
                            bi = (2 * yi + nh)

                            def mmo(e, nh=nh, jj=jj, wi=wi, bi=bi):
                                ins = None
                                for k in range(8):
                                    ins = e.matmul(py[bi][:], og[wi][:, k, jj * 128:(jj + 1) * 128], wo[:, k, nh * 512:(nh + 1) * 512],
                                                   start=(k == 0), stop=(k == 7))
                                return ins
                            R.pe(mmo, reads=[("og", wi)] + wokeys, writes=[("py", bi)])
                            R.act(lambda e, bi=bi, yi=yi, nh=nh: e.activation(out=sqy[:], in_=py[bi][:], func=AF.Square,
                                                                              accum_out=ssy[yi][:, nh:nh + 1]),
                                  reads=[("py", bi)], writes=["sqy", ("ssy", yi, nh)])
                        R.dve(lambda e, yi=yi: e.tensor_tensor(out=ssy[yi][:, 2:3], in0=ssy[yi][:, 0:1], in1=ssy[yi][:, 1:2], op=ALU.add),
                              reads=[("ssy", yi, 0), ("ssy", yi, 1)], writes=[("ssy", yi, 2)])
                        R.act(lambda e, yi=yi: e.activation(out=ssy[yi][:, 3:4], in_=ssy[yi][:, 2:3], func=AF.Ln, bias=epsb[:], scale=1.0 / D),
                              reads=[("ssy", yi, 2), "eps"], writes=[("ssy", yi, 3)])
                        R.act(lambda e, yi=yi: e.activation(out=ssy[yi][:, 2:3], in_=ssy[yi][:, 3:4], func=AF.Exp, scale=-0.5),
                              reads=[("ssy", yi, 3)], writes=[("ssy", yi, 4)])
                        for nh in range(2):
                            bi = (2 * yi + nh)
                            R.dve(lambda e, bi=bi, yi=yi, nh=nh, r=r: e.scalar_tensor_tensor(
                                out=yt[yi][:, nh * 512:(nh + 1) * 512], in0=py[bi][:], scalar=ssy[yi][:, 2:3],
                                in1=Gb[r][:, nh * 512:(nh + 1) * 512], op0=ALU.mult, op1=ALU.mult),
                                reads=[("py", bi), ("ssy", yi, 4), ("Gb", r)], writes=[("yt", yi, nh)])
                        R.dve(lambda e, yi=yi, xi=xi, xwi=xwi: e.tensor_tensor(out=xw[xwi][:], in0=yt[yi][:], in1=xo[xi][:], op=ALU.add),
                              reads=[("yt", yi, 0), ("yt", yi, 1), ("xo", xi)], writes=[("xw", xwi)])
                        if last:
                            R.dma("pool", lambda e, xwi=xwi, t0=t0: e.dma_start(out=out[t0:t0 + 128, :], in_=xw[xwi][:]),
                                  reads=[("xw", xwi)], writes=[("out", t0)])
                        else:
                            R.dma("pool", lambda e, xwi=xwi, t0=t0: e.dma_start(out=xs[t0:t0 + 128, :], in_=xw[xwi][:]),
                                  reads=[("xw", xwi)], writes=[("xs", t0)])
                R.flush()
    return nc


def _rope_tables(cfg, hf, head_dim, dup):
    LAT, NTOK = cfg.LAT, cfg.NTOK
    t = np.arange(hf * LAT, (hf + 1) * LAT)
    rows = (t // GRID_W).astype(np.float32)
    cols = (t % GRID_W).astype(np.float32)
    axis_dim = head_dim // 2
    freqs = (ROPE_THETA ** (-np.arange(0, axis_dim, 2, dtype=np.float32) / np.float32(axis_dim))).astype(np.float32)
    ang = np.concatenate([rows[:, None] * freqs, cols[:, None] * freqs], axis=-1).astype(np.float32)
    cos = np.cos(ang).astype(np.float32).T
    sin = np.sin(ang).astype(np.float32).T
    half = head_dim // 2
    C = np.ones((128, NTOK), np.float32)
    S = np.zeros((128, NTOK), np.float32)
    if dup:
        for blk in range(4):
            C[blk * 32:(blk + 1) * 32, :LAT] = cos
            S[blk * 32:(blk + 1) * 32, :LAT] = -sin if blk < 2 else sin
    else:
        C[0:64, :LAT] = cos
        C[64:128, :LAT] = cos
        S[0:64, :LAT] = -sin
        S[64:128, :LAT] = sin
    return C, S


def _perm_a_cols():
    perm = np.arange(4096)
    p128 = np.zeros(128, np.int64)
    for n in range(128):
        blk = n // 32
        s = blk % 2
        d = (n % 32) + (32 if blk >= 2 else 0)
        p128[n] = s * 64 + d
    for base in (0, 1024):
        for h in range(8):
            perm[base + h * 128: base + (h + 1) * 128] = base + h * 128 + p128
    return perm


def make_in_maps(cfg, x, c, ctx, c_ctx, ada_w, ada_b, pre_g, post_g, w_out, a_w_in, a_lambda, a_subln_g, b_w_in, b_qk_g):
    DEPTH = cfg.DEPTH
    f = lambda a: np.ascontiguousarray(np.asarray(a, dtype=np.float32))
    x, c, ctx, c_ctx = f(x), f(c), f(ctx), f(c_ctx)
    NA = (DEPTH + 1) // 2
    NB = max(DEPTH // 2, 1)
    shared = {
        "ada_w": f(ada_w)[:DEPTH],
        "ada_b": np.ascontiguousarray(np.broadcast_to(f(ada_b)[:DEPTH, None, :], (DEPTH, 128, 3 * D))),
        "pre_g": np.ascontiguousarray(np.broadcast_to(f(pre_g)[:DEPTH, None, :], (DEPTH, 128, D))),
        "post_g": np.ascontiguousarray(np.broadcast_to(f(post_g)[:DEPTH, None, :], (DEPTH, 128, D))),
        "w_out": f(w_out)[:DEPTH],
        "a_w_in": np.ascontiguousarray(f(a_w_in)[:NA][:, :, _perm_a_cols()]),
        "a_lambda": np.ascontiguousarray(np.broadcast_to(f(a_lambda)[:NA].reshape(NA, 1, 256), (NA, 128, 256))),
        "a_subln": np.ascontiguousarray(f(a_subln_g)[:NA].reshape(NA, 128, 1)),
        "b_w_in": f(b_w_in)[:NB],
        "b_qk_g": np.ascontiguousarray(f(b_qk_g)[:NB].reshape(NB, 2, 128, 1)),
        "ident": np.eye(128, dtype=np.float32).astype(ml_dtypes.bfloat16),
    }
    ind = np.zeros((2, 128, 128), np.float32)
    for s in range(2):
        for p in range(128):
            if (p // 32) % 2 == s:
                ind[s, p, :] = 1.0
    shared["indmat"] = ind.astype(ml_dtypes.bfloat16)
    maps = []
    for i in range(NCORES):
        b, hf = i // 2, i % 2
        LAT = cfg.LAT
        xin = np.concatenate([x[b, hf * LAT:(hf + 1) * LAT], ctx[b, hf * CTXH:(hf + 1) * CTXH]], axis=0)
        cin = np.stack([c[b].reshape(8, 128).T, c_ctx.reshape(8, 128).T], axis=1)
        ac, as_ = _rope_tables(cfg, hf, 64, True)
        bc, bs = _rope_tables(cfg, hf, 128, False)
        m = dict(shared)
        m.update({"xin": np.ascontiguousarray(xin), "cin": np.ascontiguousarray(cin),
                  "ropeA_C": ac, "ropeA_S": as_, "ropeB_C": bc, "ropeB_S": bs})
        maps.append(m)
    return maps


_CACHE = {}


def run(cfg, inputs, trace=False):
    key = (cfg.SEQ, cfg.DEPTH, cfg.debug, cfg.stab, cfg.stop)
    if key not in _CACHE:
        _CACHE[key] = build_program(cfg)
    nc = _CACHE[key]
    maps = make_in_maps(cfg, **inputs)
    res = run_bass_kernel_spmd(nc, maps, core_ids=list(range(NCORES)))
    return res


def kernel(x, c, ctx, c_ctx, ada_w, ada_b, pre_g, post_g, w_out, a_w_in, a_lambda, a_subln_g, b_w_in, b_qk_g):
    x = np.asarray(x)
    B, S, _ = x.shape
    cfg = Cfg(S, 4)
    res = run(cfg, dict(x=x, c=c, ctx=ctx, c_ctx=c_ctx, ada_w=ada_w, ada_b=ada_b, pre_g=pre_g, post_g=post_g,
                        w_out=w_out, a_w_in=a_w_in, a_lambda=a_lambda, a_subln_g=a_subln_g, b_w_in=b_w_in, b_qk_g=b_qk_g))
    outp = np.empty((B, S, D), np.float32)
    for i in range(NCORES):
        b, hf = i // 2, i % 2
        outp[b, hf * cfg.LAT:(hf + 1) * cfg.LAT] = np.asarray(res.results[i]["out"], dtype=np.float32)
    return outp
```
